# Optimizing a Trainium2 kernel written in Bass

```python
import jax, jax.numpy as jnp
from jax import lax
import numpy as np

D_MODEL = 2048
BATCH = 4
SEQ = 2048
DEPTH = 4

N_A = DEPTH // 2
N_B = DEPTH - N_A
D_FF = 4 * D_MODEL
CONV_WIDTH = 3
N_HEADS = 16
QK_NOPE = 128
QK_ROPE = 64
V_HEAD = 128
Q_LORA = 512
KV_LORA = 512
ROPE_THETA = 10000.0
Q_BLOCK = 128
ALPHA = (2 * DEPTH) ** 0.25
BETA = (8 * DEPTH) ** -0.25
LN_EPS = 1e-5
RMS_EPS = 1e-6

kernel_name = "yoco_shortconv_mla_deepnorm"


def layer_norm(x, g, b):
    xf = x.astype(jnp.float32)
    mu = jnp.mean(xf, axis=-1, keepdims=True)
    var = jnp.mean(jnp.square(xf - mu), axis=-1, keepdims=True)
    y = (xf - mu) * lax.rsqrt(var + LN_EPS) * g.astype(jnp.float32) + b.astype(jnp.float32)
    return y.astype(x.dtype)


def rms_norm(x, g):
    xf = x.astype(jnp.float32)
    y = xf * lax.rsqrt(jnp.mean(jnp.square(xf), axis=-1, keepdims=True) + RMS_EPS)
    return (y * g.astype(jnp.float32)).astype(x.dtype)


def rope_tables(seq, dim):
    inv = 1.0 / (ROPE_THETA ** (jnp.arange(0, dim, 2, dtype=jnp.float32) / dim))
    ang = jnp.arange(seq, dtype=jnp.float32)[:, None] * inv[None, :]
    return jnp.cos(ang), jnp.sin(ang)


def apply_rope(x, cos, sin):
    xf = x.astype(jnp.float32)
    half = xf.shape[-1] // 2
    x1, x2 = xf[..., :half], xf[..., half:]
    out = jnp.concatenate([x1 * cos - x2 * sin, x2 * cos + x1 * sin], axis=-1)
    return out.astype(x.dtype)


def short_conv_mixer(x, w_in, conv_w, w_out):
    bcu = x @ w_in
    gate_b, gate_c, u = jnp.split(bcu, 3, axis=-1)
    v = gate_c * u
    y = lax.conv_general_dilated(
        v, conv_w[:, None, :].astype(v.dtype),
        window_strides=(1,), padding=[(CONV_WIDTH - 1, 0)],
        dimension_numbers=("NWC", "WIO", "NWC"), feature_group_count=v.shape[-1])
    return (gate_b * y) @ w_out


def shared_latent_kv(h, w_dkv, kv_norm_g, w_ukv, cos, sin):
    bsz, seq, _ = h.shape
    ckv = h @ w_dkv
    c = rms_norm(ckv[..., :KV_LORA], kv_norm_g)
    k_pe = apply_rope(ckv[..., KV_LORA:], cos, sin)
    kv = (c @ w_ukv).reshape(bsz, seq, N_HEADS, QK_NOPE + V_HEAD)
    return kv[..., :QK_NOPE], k_pe, kv[..., QK_NOPE:]


def mla_mixer(x, w_dq, q_norm_g, w_uq, w_o, k_nope, k_pe, v, cos, sin):
    bsz, seq, _ = x.shape
    q = (rms_norm(x @ w_dq, q_norm_g) @ w_uq).reshape(bsz, seq, N_HEADS, QK_NOPE + QK_ROPE)
    q_nope = q[..., :QK_NOPE]
    q_pe = apply_rope(q[..., QK_NOPE:], cos[:, None, :], sin[:, None, :])
    nb = seq // Q_BLOCK
    scale = (QK_NOPE + QK_ROPE) ** -0.5
    k_pos = jnp.arange(seq)

    def to_blocks(t):
        return jnp.moveaxis(t.reshape(bsz, nb, Q_BLOCK, *t.shape[2:]), 1, 0)

    def attend_block(args):
        qn, qp, blk = args
        s = (jnp.einsum("bqhd,bkhd->bhqk", qn, k_nope)
             + jnp.einsum("bqhr,bkr->bhqk", qp, k_pe)).astype(jnp.float32) * scale
        q_pos = blk * Q_BLOCK + jnp.arange(Q_BLOCK)
        mask = k_pos[None, :] <= q_pos[:, None]
        s = jnp.where(mask[None, None], s, -jnp.inf)
        p = jax.nn.softmax(s, axis=-1).astype(v.dtype)
        return jnp.einsum("bhqk,bkhd->bqhd", p, v)

    o = lax.map(attend_block, (to_blocks(q_nope), to_blocks(q_pe), jnp.arange(nb)))
    o = jnp.moveaxis(o, 0, 1).reshape(bsz, seq, N_HEADS * V_HEAD)
    return o @ w_o


def sq_relu_mlp(x, w1, w2):
    return jnp.square(jax.nn.relu(x @ w1)) @ w2


def setup_inputs(seed: int = 0) -> dict:
    key = jax.random.key(seed)
    ks = jax.random.split(key, 16)
    f32 = jnp.float32

    def dense(k, shape, fan_in, scale=1.0):
        return jax.random.normal(k, shape, f32) * (scale * fan_in ** -0.5)

    return {
        "x": jax.random.normal(ks[0], (BATCH, SEQ, D_MODEL), f32),
        "ln_g": 1.0 + 0.02 * jax.random.normal(ks[1], (DEPTH, 2, D_MODEL), f32),
        "ln_b": 0.02 * jax.random.normal(ks[2], (DEPTH, 2, D_MODEL), f32),
        "conv_w_in": dense(ks[3], (N_A, D_MODEL, 3 * D_MODEL), D_MODEL),
        "conv_w": dense(ks[4], (N_A, CONV_WIDTH, D_MODEL), CONV_WIDTH),
        "conv_w_out": dense(ks[5], (N_A, D_MODEL, D_MODEL), D_MODEL, BETA),
        "kv_w_dkv": dense(ks[6], (D_MODEL, KV_LORA + QK_ROPE), D_MODEL),
        "kv_norm_g": 1.0 + 0.02 * jax.random.normal(ks[7], (KV_LORA,), f32),
        "kv_w_ukv": dense(ks[8], (KV_LORA, N_HEADS * (QK_NOPE + V_HEAD)), KV_LORA),
        "mla_w_dq": dense(ks[9], (N_B, D_MODEL, Q_LORA), D_MODEL),
        "mla_q_norm_g": 1.0 + 0.02 * jax.random.normal(ks[10], (N_B, Q_LORA), f32),
        "mla_w_uq": dense(ks[11], (N_B, Q_LORA, N_HEADS * (QK_NOPE + QK_ROPE)), Q_LORA),
        "mla_w_o": dense(ks[12], (N_B, N_HEADS * V_HEAD, D_MODEL), N_HEADS * V_HEAD, BETA),
        "mlp_w1": dense(ks[13], (DEPTH, D_MODEL, D_FF), D_MODEL),
        "mlp_w2": dense(ks[14], (DEPTH, D_FF, D_MODEL), D_FF, BETA),
    }


def reference(x, ln_g, ln_b, conv_w_in, conv_w, conv_w_out, kv_w_dkv, kv_norm_g,
              kv_w_ukv, mla_w_dq, mla_q_norm_g, mla_w_uq, mla_w_o, mlp_w1, mlp_w2):
    cos, sin = rope_tables(x.shape[1], QK_ROPE)
    h = x
    k_nope = k_pe = v = None
    for layer in range(DEPTH):
        if layer < N_A:
            mix = short_conv_mixer(h, conv_w_in[layer], conv_w[layer], conv_w_out[layer])
        else:
            if layer == N_A:
                k_nope, k_pe, v = shared_latent_kv(h, kv_w_dkv, kv_norm_g, kv_w_ukv, cos, sin)
            j = layer - N_A
            mix = mla_mixer(h, mla_w_dq[j], mla_q_norm_g[j], mla_w_uq[j], mla_w_o[j],
                            k_nope, k_pe, v, cos, sin)
        h = layer_norm(ALPHA * h + mix, ln_g[layer, 0], ln_b[layer, 0])
        h = layer_norm(ALPHA * h + sq_relu_mlp(h, mlp_w1[layer], mlp_w2[layer]),
                       ln_g[layer, 1], ln_b[layer, 1])
    return h
```

```python
import contextlib
import numpy as np
import ml_dtypes
import concourse.bass as bass
import concourse.mybir as mybir
from concourse.bass_utils import run_bass_kernel_spmd

F32 = mybir.dt.float32
BF16 = mybir.dt.bfloat16
ALU = mybir.AluOpType
AF = mybir.ActivationFunctionType
AX = mybir.AxisListType

D = 2048
T = 1024
NT = 8
KD = 16
NH = 16
DEPTH = 4
ALPHA = float((2 * DEPTH) ** 0.25)
LN_EPS = 1e-5
RMS_EPS = 1e-6
SCALE = float(192 ** -0.5)
NEG = -30000.0
NSLOT = 8
SLOT = 2048
SEM_LIMIT = 30000
PSUM_KEYS = ("PACC", "PTR", "PFM")


class Tok:
    __slots__ = ("sem", "name", "val", "eng")

    def __init__(self, sem, name, val, eng):
        self.sem, self.name, self.val, self.eng = sem, name, val, eng


class Tracker:
    def __init__(self, nc, es):
        self.nc, self.es = nc, es
        self.E = {"pe": nc.tensor, "act": nc.scalar, "dve": nc.vector, "pool": nc.gpsimd, "sp": nc.sync}
        self.sem, self.semname, self.cnt, self.gen = {}, {}, {}, {}
        for e in self.E:
            self.gen[e] = 0
            self._newsem(e)
        self.waited = {e: {} for e in self.E}
        self.last_w = {}
        self.readers = {}
        self.dsem = {}
        self.n_inst = {e: 0 for e in self.E}
        self.trace = {e: [] for e in self.E}

    def check_deadlock(self):
        pc = {e: 0 for e in self.E}
        sems = {}
        progress = True
        while progress:
            progress = False
            for e in self.E:
                tr = self.trace[e]
                while pc[e] < len(tr):
                    kind, name, val = tr[pc[e]]
                    if kind == "w":
                        if sems.get(name, 0) >= val:
                            pc[e] += 1; progress = True
                        else:
                            break
                    else:
                        sems[name] = sems.get(name, 0) + val
                        pc[e] += 1; progress = True
        stuck = {e: self.trace[e][pc[e]] for e in self.E if pc[e] < len(self.trace[e])}
        return stuck, {e: (pc[e], len(self.trace[e])) for e in self.E}

    def _newsem(self, e):
        nm = "s_%s_%d" % (e, self.gen[e])
        self.gen[e] += 1
        self.sem[e] = self.es.enter_context(self.nc.semaphore(nm))
        self.semname[e] = nm
        self.cnt[e] = 0

    def _wait(self, e, tok):
        w = self.waited[e]
        if w.get(tok.name, 0) >= tok.val:
            return
        self.E[e].wait_ge(tok.sem, tok.val)
        self.trace[e].append(("w", tok.name, tok.val))
        w[tok.name] = tok.val

    def _deps(self, e, reads, writes, extra):
        for k in reads:
            t = self.last_w.get(k)
            if t is not None:
                self._wait(e, t)
        for k in writes:
            t = self.last_w.get(k)
            if t is not None and not (t.eng == e and t.eng != "dma"):
                self._wait(e, t)
            for t in self.readers.get(k, {}).values():
                if not (t.eng == e and t.eng != "dma"):
                    self._wait(e, t)
        for t in extra:
            if t is not None:
                self._wait(e, t)

    def _record(self, tok, reads, writes):
        for k in reads:
            self.readers.setdefault(k, {})[tok.eng if tok.eng != "dma" else tok.name] = tok
        for k in writes:
            self.last_w[k] = tok
            self.readers[k] = {}

    def op(self, e, fn, reads=(), writes=(), extra=()):
        xr = [k for k in reads if k[0] in PSUM_KEYS]
        if xr:
            writes = list(writes) + xr
        self._deps(e, reads, writes, extra)
        ins = fn(self.E[e])
        if self.cnt[e] >= SEM_LIMIT:
            self._newsem(e)
        self.cnt[e] += 1
        ins.then_inc(self.sem[e], 1)
        self.trace[e].append(("i", self.semname[e], 1))
        tok = Tok(self.sem[e], self.semname[e], self.cnt[e], e)
        self._record(tok, reads, writes)
        return tok

    def dma(self, q, out, in_, semname, reads=(), writes=(), extra=()):
        self._deps(q, reads, writes, extra)
        if semname not in self.dsem:
            self.dsem[semname] = [self.es.enter_context(self.nc.semaphore("d_" + semname)), 0]
        ent = self.dsem[semname]
        ins = self.E[q].dma_start(out=out, in_=in_)
        ent[1] += 16
        ins.then_inc(ent[0], 16)
        self.trace[q].append(("i", "d_" + semname, 16))
        tok = Tok(ent[0], "d_" + semname, ent[1], "dma")
        self._record(tok, reads, writes)
        return tok

    def barrier_keys(self):
        return [Tok(self.sem[e], self.semname[e], self.cnt[e], e) for e in self.E if self.cnt[e] > 0]


_LAYER_ENDS = []


class Step:
    def __init__(self, fn, w=None, nslots=0):
        self.fn, self.w, self.nslots = fn, w, nslots


def build_program(stage, max_steps=None):
    layers = {"L0": [0], "L1": [1], "L23": [2, 3], "ALL": [0, 1, 2, 3]}[stage]
    FUSED = stage == "ALL"
    PAIRS = [[0, 1], [2, 3], [4, 5], [6, 7]]
    nc = bass.Bass("TRN2", target_bir_lowering=False)

    def din(name, shape, dt=F32):
        return nc.dram_tensor(name, list(shape), dt, kind="ExternalInput").ap()

    def dout(name, shape, dt=F32):
        return nc.dram_tensor(name, list(shape), dt, kind="ExternalOutput").ap()

    x_d = din("x", [T, D])
    ident_d = din("ident", [128, 128], BF16)
    lng_d = din("ln_g", [DEPTH, 2, D])
    lnb_d = din("ln_b", [DEPTH, 2, D])
    small_d = din("smallp", [128, 128])
    w1_d = din("w1", [len(layers), 64, 128, KD * 128])
    w2_d = din("w2", [len(layers), 4, 4, 128, KD * 512])
    out_d = dout("out", [T, D])
    conv_layers = [l for l in layers if l < 2]
    mla_layers = [l for l in layers if l >= 2]
    if conv_layers:
        win_d = din("w_in", [len(conv_layers), 16, 3, 128, KD * 128])
        wout_d = din("w_out", [len(conv_layers), 4, 128, KD * 512])
        halo_d = din("halo_t", [128, KD * 2])
    if stage in ("L1", "ALL"):
        wdkv_lat_d = din("wdkv_lat", [128, KD * 512])
        wdkv_pe_d = din("wdkv_pe", [128, KD * 128])
    if stage == "L1":
        rope_d = din("rope", [64, 2 * T])
        ct_own_d = dout("ct_own", [128, 4 * T], BF16)
        kpe_own_d = dout("kpe_own", [64, T], BF16)
    if mla_layers:
        wdq_d = din("wdq", [2, 128, KD * 512])
        wqkv_d = din("wqkv", [2, 8, 128, 2 * 4 * 512])
        wo_d = din("wo", [2, 4, 128, KD * 512])
        rope_d = din("rope", [64, 2 * T])
        mask_d = din("mask", [128, 128], BF16)
    if stage == "L23":
        ct_all_d = din("ct_all", [128, 4 * 2 * T], BF16)
        kpe_all_d = din("kpe_all", [65, 2 * T], BF16)
    if FUSED:
        maskrow_d = din("maskrow", [1, 2 * T], BF16)
        halo_src = nc.dram_tensor("halo_src", [128, 32], BF16, kind="Internal").ap()
        halo_all = nc.dram_tensor("halo_all", [256, 32], BF16, addr_space="Local", kind="Internal").ap()
        lat_src = nc.dram_tensor("lat_src", [128, 5 * T], BF16, kind="Internal").ap()
        lat_all = nc.dram_tensor("lat_all", [256, 5 * T], BF16, addr_space="Local", kind="Internal").ap()

    es = contextlib.ExitStack()
    with es:
        def sb(name, shape, dt):
            return es.enter_context(nc.sbuf_tensor(name, list(shape), dt))

        def ps(name, shape, dt):
            return es.enter_context(nc.psum_tensor(name, list(shape), dt))

        TR = Tracker(nc, es)
        H32 = sb("H32", [128, NT, D], F32)
        HT = sb("HT", [128, KD, T], BF16)
        GB = sb("GB", [128, 2, D], F32)
        WS = sb("WS", [128, NSLOT * SLOT], BF16)
        ARN = 57344 // 2
        AR = sb("AR", [128, ARN], BF16)
        IDB = sb("IDB", [128, 128], BF16)
        SMALL = sb("SMALL", [128, 128], F32)
        ST = sb("ST", [128, 8, 24], F32)
        MV = sb("MV", [128, 8, 8], F32)
        SC32 = sb("SC32", [128, 2, 512], F32)
        PACC = ps("PACC", [128, 4, 512], F32)
        PTR = ps("PTR", [128, KD, 128], BF16)
        PFM = ps("PFM", [128, 2, 512], F32)

        def arena(off_bytes, nbytes, dt):
            v = AR[:, off_bytes // 2:(off_bytes + nbytes) // 2]
            return v if dt == BF16 else v.bitcast(F32)

        steps = []

        def init_fn(_):
            TR.dma("sp", IDB[:], ident_d[:, :], "c0", writes=[("IDB",)])
            TR.dma("sp", SMALL[:], small_d[:, :], "c0", writes=[("SMALL",)])
            TR.dma("sp", H32[:], x_d.rearrange("(i p) d -> p i d", p=128), "c1",
                   writes=[("H32", i, b) for i in range(NT) for b in range(4)])
            HB = arena(49152, 8192, BF16).rearrange("p (a n) -> p a n", a=2)
            for i in range(NT):
                hb = HB[:, i % 2, :]
                TR.op("act", lambda e, i=i, hb=hb: e.copy(out=hb, in_=H32[:, i, :]),
                      reads=[("H32", i, b) for b in range(4)], writes=[("HB", i % 2)])
                to_feat(i, hb, ("HB", i % 2))
        steps.append(Step(init_fn))

        def to_feat(i, hb, hbkey):
            def f(e):
                for k in range(KD):
                    ins = e.transpose(PTR[:, k, :], hb[:, k * 128:(k + 1) * 128], IDB[:])
                return ins
            TR.op("pe", f, reads=[hbkey, ("IDB",)], writes=[("PTR",)])
            TR.op("act", lambda e: e.copy(out=HT[:, :, i * 128:(i + 1) * 128], in_=PTR[:]),
                  reads=[("PTR",)], writes=[("HT", i)])

        def ln_phase(l, s, final_store=False):
            def load_fn(_):
                TR.dma("sp", GB[:, 0, :], lng_d[l, s].partition_broadcast(128), "gb", writes=[("GB",)])
                TR.dma("sp", GB[:, 1, :], lnb_d[l, s].partition_broadcast(128), "gb", writes=[("GB",)])
            steps.append(Step(load_fn))

            def fn(_):
                HB = arena(49152, 8192, BF16).rearrange("p (a n) -> p a n", a=2)
                hk = lambda i: [("H32", i, b) for b in range(4)]
                R = range(NT)
                for i in R:
                    def stats(e, i=i):
                        for c in range(4):
                            ins = e.bn_stats(out=ST[:, i, c * 6:(c + 1) * 6], in_=H32[:, i, c * 512:(c + 1) * 512])
                        return ins
                    TR.op("dve", stats, reads=hk(i), writes=[("ST", i)])
                for i in R:
                    TR.op("dve", lambda e, p=i: e.bn_aggr(out=MV[:, p, 0:2], in_=ST[:, p, :]),
                          reads=[("ST", i)], writes=[("MV", i, 0)])
                for i in R:
                    TR.op("dve", lambda e, p=i: e.tensor_scalar(out=MV[:, p, 2:3], in0=MV[:, p, 1:2], scalar1=LN_EPS,
                                                               scalar2=None, op0=ALU.add),
                          reads=[("MV", i, 0)], writes=[("MV", i, 1)])
                for i in R:
                    TR.op("act", lambda e, p=i: e.activation(out=MV[:, p, 3:4], in_=MV[:, p, 2:3], func=AF.Sqrt),
                          reads=[("MV", i, 1)], writes=[("MV", i, 2)])
                for i in R:
                    TR.op("dve", lambda e, p=i: e.reciprocal(out=MV[:, p, 4:5], in_=MV[:, p, 3:4]),
                          reads=[("MV", i, 2)], writes=[("MV", i, 3)])
                for i in R:
                    TR.op("dve", lambda e, p=i: e.scalar_tensor_tensor(out=MV[:, p, 5:6], in0=MV[:, p, 0:1], scalar=-1.0,
                                                                      in1=MV[:, p, 4:5], op0=ALU.mult, op1=ALU.mult),
                          reads=[("MV", i, 0), ("MV", i, 3)], writes=[("MV", i, 4)])
                for i in R:
                    TR.op("act", lambda e, i=i: e.activation(out=H32[:, i, :], in_=H32[:, i, :], func=AF.Identity,
                                                            bias=MV[:, i, 5:6], scale=MV[:, i, 4:5]),
                          reads=hk(i) + [("MV", i, 3), ("MV", i, 4)], writes=hk(i))
                for i in R:
                    TR.op("dve", lambda e, i=i: e.tensor_tensor(out=H32[:, i, :], in0=H32[:, i, :], in1=GB[:, 0, :], op=ALU.mult),
                          reads=hk(i) + [("GB",)], writes=hk(i))
                    TR.op("pool", lambda e, i=i: e.tensor_tensor(out=H32[:, i, :], in0=H32[:, i, :], in1=GB[:, 1, :], op=ALU.add),
                          reads=hk(i) + [("GB",)], writes=hk(i))
                    if final_store:
                        TR.dma("sp", out_d[i * 128:(i + 1) * 128, :], H32[:, i, :], "out", reads=hk(i))
                for i in R:
                    if not final_store:
                        p = i % 2
                        hb = HB[:, p, :]
                        TR.op("act", lambda e, i=i, hb=hb: e.copy(out=hb, in_=H32[:, i, :]),
                              reads=hk(i), writes=[("HB", p)])
                        to_feat(i, hb, ("HB", p))
            steps.append(Step(fn))

        def proj_block(wv, b, lhs_fn, lhs_keys_fn, first, wkeys, nk=16):
            for i in range(NT):
                bank = (b * NT + i) % 4
                def f(e, i=i, bank=bank):
                    for k in range(nk):
                        ins = e.matmul(PACC[:, bank, :], lhsT=lhs_fn(k, i), rhs=wv[:, k, :], start=(k == 0), stop=(k == nk - 1))
                    return ins
                TR.op("pe", f, reads=lhs_keys_fn(i) + wkeys, writes=[("PACC", bank)])
                dst = H32[:, i, b * 512:(b + 1) * 512]
                if first:
                    TR.op("dve", lambda e, dst=dst, bank=bank: e.scalar_tensor_tensor(
                        out=dst, in0=dst, scalar=ALPHA, in1=PACC[:, bank, :], op0=ALU.mult, op1=ALU.add),
                        reads=[("H32", i, b), ("PACC", bank)], writes=[("H32", i, b)])
                else:
                    TR.op("dve", lambda e, dst=dst, bank=bank: e.tensor_tensor(out=dst, in0=dst, in1=PACC[:, bank, :], op=ALU.add),
                          reads=[("H32", i, b), ("PACC", bank)], writes=[("H32", i, b)])

        def mlp_phase(li, l):
            AT = arena(0, 32768, BF16).rearrange("p (c t) -> p c t", c=16)
            RL = arena(32768, 4096, F32).rearrange("p (a n) -> p a n", a=2)
            for q in range(4):
                for c in range(16):
                    fch = q * 16 + c
                    def fn(wt, c=c):
                        wv, wkeys = wt
                        wv = wv.rearrange("p (k n) -> p k n", k=KD)
                        for th in range(2):
                            def f(e, th=th):
                                for k in range(KD):
                                    ins = e.matmul(PFM[:, th, :], lhsT=wv[:, k, :], rhs=HT[:, k, th * 512:(th + 1) * 512],
                                                   start=(k == 0), stop=(k == KD - 1))
                                return ins
                            TR.op("pe", f, reads=[("HT", 4 * th + j) for j in range(4)] + wkeys, writes=[("PFM", th)])
                            TR.op("act", lambda e, th=th: e.activation(out=RL[:, th, :], in_=PFM[:, th, :], func=AF.Relu),
                                  reads=[("PFM", th)], writes=[("RL", th)])
                            TR.op("dve", lambda e, th=th, c=c: e.tensor_tensor(out=AT[:, c, th * 512:(th + 1) * 512], in0=RL[:, th, :],
                                                                              in1=RL[:, th, :], op=ALU.mult),
                                  reads=[("RL", th)], writes=[("AT", c, th)])
                    steps.append(Step(fn, w1_d[li, fch], 1))
                for b in range(4):
                    def fn2(wt, q=q, b=b):
                        wv, wkeys = wt
                        wv = wv.rearrange("p (k n) -> p k n", k=16)
                        proj_block(wv, b, lambda k, i: AT[:, k, i * 128:(i + 1) * 128],
                                   lambda i: [("AT", k, i // 4) for k in range(16)], first=(q == 0), wkeys=wkeys)
                    steps.append(Step(fn2, w2_d[li, q, b], 4))

        def conv_phase(l, ci):
            ZT = arena(0, 32768, BF16).rearrange("p (c t) -> p c t", c=16)
            TC = arena(32768, 4128, F32)
            V = arena(36896, 4128, F32)
            Y = arena(41024, 4096, F32)
            HALO = sb("HALO%d" % l, [128, KD, 2], BF16)
            PH = PTR[:].rearrange("p k n -> p (k n)").bitcast(F32)

            def load_halo(_):
                if FUSED and l == 1:
                    HL = sb("HL", [128, 32], BF16)
                    TR.op("pool", lambda e: e.collective_compute("AllGather", ALU.bypass, replica_groups=PAIRS, ins=[halo_src], outs=[halo_all]),
                          reads=[("HALOSRC",)], writes=[("HALOALL",)])
                    TR.dma("sp", HL[:], halo_all[0:128, :], "halo", reads=[("HALOALL",)], writes=[("HL",)])
                    TR.op("dve", lambda e: e.tensor_scalar(out=HALO[:].rearrange("p k t -> p (k t)"), in0=HL[:], scalar1=SMALL[:, 110:111],
                                                           scalar2=None, op0=ALU.mult),
                          reads=[("HL",), ("SMALL",)], writes=[("HALO",)])
                else:
                    TR.dma("pool", HALO[:].rearrange("p k t -> p (k t)"), halo_d[:, :], "halo", writes=[("HALO",)])
            steps.append(Step(load_halo))
            cw = lambda c, j: SMALL[:, l * 48 + c * 3 + j: l * 48 + c * 3 + j + 1]
            for c in range(16):
                def fn_part(wt, c=c, part=0):
                    wv, wkeys = wt
                    wv = wv.rearrange("p (k n) -> p k n", k=KD)
                    banks = {1: (PACC, 0), 2: (PACC, 2), 0: (PFM, 0)}[part]
                    pt, b0 = banks
                    pkey = "PACC" if pt is PACC else "PFM"
                    if part in (1, 2):
                        col = 0 if part == 1 else 2
                        def fh(e):
                            for k in range(KD):
                                ins = e.matmul(PH[:, col:col + 2], lhsT=wv[:, k, :], rhs=HALO[:, k, :], start=(k == 0), stop=(k == KD - 1))
                            return ins
                        TR.op("pe", fh, reads=[("HALO",)] + wkeys, writes=[("PTR",)])
                    for th in range(2):
                        def f(e, th=th):
                            for k in range(KD):
                                ins = e.matmul(pt[:, b0 + th, :], lhsT=wv[:, k, :], rhs=HT[:, k, th * 512:(th + 1) * 512],
                                               start=(k == 0), stop=(k == KD - 1))
                            return ins
                        TR.op("pe", f, reads=[("HT", 4 * th + j) for j in range(4)] + wkeys, writes=[(pkey, b0 + th)])
                    if part == 1:
                        for th in range(2):
                            TR.op("act", lambda e, th=th: e.copy(out=TC[:, th * 512:(th + 1) * 512], in_=PACC[:, th, :]),
                                  reads=[("PACC", th)], writes=[("TC", th)])
                        TR.op("act", lambda e: e.copy(out=TC[:, 1024:1026], in_=PH[:, 0:2]), reads=[("PTR",)], writes=[("TC", 2)])
                    elif part == 2:
                        for th in range(2):
                            TR.op("dve", lambda e, th=th: e.tensor_tensor(out=V[:, 2 + th * 512: 2 + (th + 1) * 512], in0=TC[:, th * 512:(th + 1) * 512],
                                                                         in1=PACC[:, 2 + th, :], op=ALU.mult),
                                  reads=[("TC", th), ("PACC", 2 + th)], writes=[("V", th)])
                        TR.op("dve", lambda e: e.tensor_tensor(out=V[:, 0:2], in0=TC[:, 1024:1026], in1=PH[:, 2:4], op=ALU.mult),
                              reads=[("TC", 2), ("PTR",)], writes=[("V", 2)])
                        vk = [("V", 0), ("V", 1), ("V", 2)]
                        TR.op("dve", lambda e, c=c: e.tensor_scalar(out=Y[:, 0:1024], in0=V[:, 2:1026], scalar1=cw(c, 2), scalar2=None, op0=ALU.mult),
                              reads=vk + [("SMALL",)], writes=[("Y",)])
                        TR.op("dve", lambda e, c=c: e.scalar_tensor_tensor(out=Y[:, 0:1024], in0=V[:, 1:1025], scalar=cw(c, 1), in1=Y[:, 0:1024],
                                                                          op0=ALU.mult, op1=ALU.add),
                              reads=vk + [("Y",), ("SMALL",)], writes=[("Y",)])
                        TR.op("dve", lambda e, c=c: e.scalar_tensor_tensor(out=Y[:, 0:1024], in0=V[:, 0:1024], scalar=cw(c, 0), in1=Y[:, 0:1024],
                                                                          op0=ALU.mult, op1=ALU.add),
                              reads=vk + [("Y",), ("SMALL",)], writes=[("Y",)])
                    else:
                        for th in range(2):
                            TR.op("dve", lambda e, th=th, c=c: e.tensor_tensor(out=ZT[:, c, th * 512:(th + 1) * 512], in0=Y[:, th * 512:(th + 1) * 512],
                                                                              in1=PFM[:, th, :], op=ALU.mult),
                                  reads=[("Y",), ("PFM", th)], writes=[("ZT", c, th)])
                for part in (1, 2, 0):
                    steps.append(Step(lambda wt, c=c, part=part, fn_part=fn_part: fn_part(wt, c, part), win_d[ci, c, part], 1))
            for b in range(4):
                def fn2(wt, b=b):
                    wv, wkeys = wt
                    wv = wv.rearrange("p (k n) -> p k n", k=16)
                    proj_block(wv, b, lambda k, i: ZT[:, k, i * 128:(i + 1) * 128],
                               lambda i: [("ZT", k, i // 4) for k in range(16)], first=True, wkeys=wkeys)
                steps.append(Step(fn2, wout_d[ci, b], 4))

        def rms_to_feat(i, bank, gcol0, dstT, dkey):
            p = i % 2
            SQ = SC32[:, p, :]
            NB = arena(53248 + p * 1024, 1024, BF16)
            TR.op("dve", lambda e: e.memset(MV[:, p, 6:7], 0.0), writes=[("MV", p, 6)])
            TR.op("act", lambda e: e.activation(out=SQ, in_=PACC[:, bank, :], func=AF.Square, accum_out=MV[:, p, 6:7]),
                  reads=[("PACC", bank), ("MV", p, 6)], writes=[("SQ", p), ("MV", p, 6)])
            TR.op("dve", lambda e: e.tensor_scalar(out=MV[:, p, 7:8], in0=MV[:, p, 6:7], scalar1=1.0 / 512, scalar2=RMS_EPS,
                                                   op0=ALU.mult, op1=ALU.add),
                  reads=[("MV", p, 6)], writes=[("MV", p, 7)])
            TR.op("act", lambda e: e.activation(out=MV[:, p, 6:7], in_=MV[:, p, 7:8], func=AF.Sqrt),
                  reads=[("MV", p, 7)], writes=[("MV", p, 6)])
            TR.op("dve", lambda e: e.reciprocal(out=MV[:, p, 7:8], in_=MV[:, p, 6:7]),
                  reads=[("MV", p, 6)], writes=[("MV", p, 7)])
            TR.op("dve", lambda e: e.tensor_scalar(out=NB, in0=PACC[:, bank, :], scalar1=MV[:, p, 7:8], scalar2=None, op0=ALU.mult),
                  reads=[("PACC", bank), ("MV", p, 7)], writes=[("NB", p)])
            def f(e):
                for kc in range(4):
                    ins = e.transpose(PTR[:, kc, :], NB[:, kc * 128:(kc + 1) * 128], IDB[:])
                return ins
            TR.op("pe", f, reads=[("NB", p), ("IDB",)], writes=[("PTR",)])
            for kc in range(4):
                TR.op("dve", lambda e, kc=kc: e.tensor_scalar(out=dstT[:, kc, i * 128:(i + 1) * 128], in0=PTR[:, kc, :],
                                                             scalar1=SMALL[:, gcol0 + kc: gcol0 + kc + 1], scalar2=None, op0=ALU.mult),
                      reads=[("PTR",), ("SMALL",)], writes=[(dkey, i)])

        def load_rope():
            def fn(_):
                TR.dma("sp", GB[0:64, 0, :], rope_d[:, :], "gb", writes=[("GB",)])
            steps.append(Step(fn))

        ROPE = GB[0:64, 0, :]
        RT = GB[0:64, 1, :]

        def rope_combine(pa, pb, keys_in, th, dst, dkeys):
            TR.op("dve", lambda e: e.tensor_tensor(out=RT[:, 0:512], in0=pa, in1=ROPE[:, th * 512:(th + 1) * 512], op=ALU.mult),
                  reads=keys_in + [("GB",)], writes=[("RT", 0)])
            TR.op("dve", lambda e: e.tensor_tensor(out=RT[:, 512:1024], in0=pb, in1=ROPE[:, T + th * 512:T + (th + 1) * 512], op=ALU.mult),
                  reads=keys_in + [("GB",)], writes=[("RT", 1)])
            TR.op("dve", lambda e: e.tensor_tensor(out=dst, in0=RT[:, 0:512], in1=RT[:, 512:1024], op=ALU.add),
                  reads=[("RT", 0), ("RT", 1), ("GB",)], writes=dkeys)

        def kv_latent_phase():
            CTO = arena(0, 8192, BF16).rearrange("p (k t) -> p k t", k=4)
            KPO = arena(8192, 2048, BF16)
            load_rope()

            def fn(wt):
                wv, wkeys = wt
                wv = wv.rearrange("p (k n) -> p k n", k=KD)
                for i in range(NT):
                    bank = i % 4
                    def f(e, i=i, bank=bank):
                        for k in range(KD):
                            ins = e.matmul(PACC[:, bank, :], lhsT=HT[:, k, i * 128:(i + 1) * 128], rhs=wv[:, k, :], start=(k == 0), stop=(k == KD - 1))
                        return ins
                    TR.op("pe", f, reads=[("HT", i)] + wkeys, writes=[("PACC", bank)])
                    rms_to_feat(i, bank, 96, CTO, "CTO")
            steps.append(Step(fn, wdkv_lat_d, 4))

            def fn2(wt):
                wv, wkeys = wt
                wv = wv.rearrange("p (k n) -> p k n", k=KD)
                for th in range(2):
                    for sw in range(2):
                        def f(e, th=th, sw=sw):
                            for k in range(KD):
                                ins = e.matmul(PFM[0:64, sw, :], lhsT=wv[:, k, sw * 64:(sw + 1) * 64], rhs=HT[:, k, th * 512:(th + 1) * 512],
                                               start=(k == 0), stop=(k == KD - 1))
                            return ins
                        TR.op("pe", f, reads=[("HT", 4 * th + j) for j in range(4)] + wkeys, writes=[("PFM", sw)])
                    rope_combine(PFM[0:64, 0, :], PFM[0:64, 1, :], [("PFM", 0), ("PFM", 1)], th,
                                 KPO[0:64, th * 512:(th + 1) * 512], [("KPO", th)])
                if FUSED:
                    TR.dma("sp", lat_src[:, 0:4 * T], CTO[:].rearrange("p k t -> p (k t)"), "lat", reads=[("CTO", i) for i in range(NT)],
                           writes=[("LATSRC", 0)])
                    TR.dma("sp", lat_src[0:64, 4 * T:5 * T], KPO[0:64, :], "lat", reads=[("KPO", 0), ("KPO", 1)], writes=[("LATSRC", 1)])
                    TR.op("pool", lambda e: e.collective_compute("AllGather", ALU.bypass, replica_groups=PAIRS, ins=[lat_src], outs=[lat_all]),
                          reads=[("LATSRC", 0), ("LATSRC", 1)], writes=[("LATALL",)])
                else:
                    TR.dma("sp", ct_own_d[:, :], CTO[:].rearrange("p k t -> p (k t)"), "out", reads=[("CTO", i) for i in range(NT)])
                    TR.dma("sp", kpe_own_d[:, :], KPO[0:64, :], "out", reads=[("KPO", 0), ("KPO", 1)])
            steps.append(Step(fn2, wdkv_pe_d, 1))

        def mla_phase(j, l):
            CT = arena(0, 16384, BF16).rearrange("p (k t) -> p k t", k=4)
            KPE = arena(16384, 4096, BF16)
            QLT = arena(20480, 8192, BF16).rearrange("p (k t) -> p k t", k=4)
            QN = arena(28672, 2048, BF16)
            QP = arena(30720, 2048, BF16)
            KN = arena(32768, 4096, BF16)
            V2 = arena(36864, 8192, BF16).rearrange("p (t n) -> p t n", t=16)
            PBS = [arena(45056, 4096, BF16), GB[:, 1, 1024:2048].bitcast(BF16)]
            PTS = [arena(49152, 4096, BF16).rearrange("p (t n) -> p t n", t=16),
                   SC32[:].rearrange("p a n -> p (a n)").bitcast(BF16).rearrange("p (t n) -> p t n", t=16)]
            OBS = [arena(55296, 256, BF16), arena(55808, 256, BF16)]
            MASK = arena(55552, 256, BF16)
            PO = PFM[:, 1, :].bitcast(BF16)
            load_rope()

            def load_kv(_):
                if FUSED:
                    TR.dma("sp", CT[:, :, 0:T], lat_all[0:128, 0:4 * T].rearrange("p (k t) -> p k t", k=4), "kv",
                           reads=[("LATALL",)], writes=[("CT",)], extra=TR.barrier_keys())
                    TR.dma("sp", CT[:, :, T:2 * T], lat_src[:, 0:4 * T].rearrange("p (k t) -> p k t", k=4), "kv",
                           reads=[("LATSRC", 0), ("LATALL",)], writes=[("CT",)])
                    TR.dma("sp", KPE[0:64, 0:T], lat_all[0:64, 4 * T:5 * T], "kv", reads=[("LATALL",)], writes=[("KPE",)])
                    TR.dma("sp", KPE[0:64, T:2 * T], lat_src[0:64, 4 * T:5 * T], "kv", reads=[("LATSRC", 1), ("LATALL",)], writes=[("KPE",)])
                    TR.dma("sp", KPE[64:65, :], maskrow_d[:, :], "kv", writes=[("KPE",)])
                else:
                    TR.dma("sp", CT[:].rearrange("p k t -> p (k t)"), ct_all_d[:, :], "kv", writes=[("CT",)], extra=TR.barrier_keys())
                    TR.dma("sp", KPE[0:65, :], kpe_all_d[:, :], "kv", writes=[("KPE",)])
                TR.dma("sp", MASK, mask_d[:, :], "kv", writes=[("MASK",)])
                TR.op("dve", lambda e: e.memset(QP[64:65, :], 1.0), writes=[("QP1",)])
            steps.append(Step(load_kv))

            def fn_dq(wt):
                wv, wkeys = wt
                wv = wv.rearrange("p (k n) -> p k n", k=KD)
                for i in range(NT):
                    bank = i % 4
                    def f(e, i=i, bank=bank):
                        for k in range(KD):
                            ins = e.matmul(PACC[:, bank, :], lhsT=HT[:, k, i * 128:(i + 1) * 128], rhs=wv[:, k, :], start=(k == 0), stop=(k == KD - 1))
                        return ins
                    TR.op("pe", f, reads=[("HT", i)] + wkeys, writes=[("PACC", bank)])
                    rms_to_feat(i, bank, 100 + 4 * j, QLT, "QLT")
            steps.append(Step(fn_dq, wdq_d[j], 4))

            for g in range(8):
                state = {}

                def fn_kv(wt, g=g, state=state):
                    wall, wkeys = wt
                    state["wukv"] = (wall[:, 0:2048].rearrange("p (k n) -> p k n", k=4), wkeys)
                    wv = state["wukv"][0]
                    for t in range(16):
                        def f(e, t=t):
                            for kc in range(4):
                                ins = e.matmul(PFM[:, 0, 0:256].rearrange("p (h c) -> p h c", h=2), lhsT=CT[:, kc, t * 128:(t + 1) * 128],
                                               rhs=wv[:, kc, :].rearrange("p (h c) -> p h c", h=2)[:, :, 128:256], start=(kc == 0), stop=(kc == 3))
                            return ins
                        TR.op("pe", f, reads=[("CT",)] + wkeys, writes=[("PFM", 0)])
                        TR.op("act", lambda e, t=t: e.copy(out=V2[:, t, :], in_=PFM[:, 0, 0:256]), reads=[("PFM", 0)], writes=[("V2", t)])

                def fn_q(wt, g=g, state=state):
                    wall, wqkeys = wt
                    wq = wall[:, 2048:4096].rearrange("p (k n) -> p k n", k=4)
                    wkv, wkvkeys = state["wukv"]
                    for hh in range(2):
                        h = 2 * g + hh
                        for kb in range(4):
                            def f(e, kb=kb):
                                for kc in range(4):
                                    ins = e.matmul(PACC[:, kb, :], lhsT=wkv[:, kc, hh * 256: hh * 256 + 128], rhs=CT[:, kc, kb * 512:(kb + 1) * 512],
                                                   start=(kc == 0), stop=(kc == 3))
                                return ins
                            TR.op("pe", f, reads=[("CT",)] + wkvkeys, writes=[("PACC", kb)])
                            TR.op("act", lambda e, kb=kb: e.copy(out=KN[:, kb * 512:(kb + 1) * 512], in_=PACC[:, kb, :]),
                                  reads=[("PACC", kb)], writes=[("KN", kb)])
                        for th in range(2):
                            def f(e, th=th):
                                for kc in range(4):
                                    ins = e.matmul(PACC[:, th, :], lhsT=wq[:, kc, hh * 256: hh * 256 + 128], rhs=QLT[:, kc, th * 512:(th + 1) * 512],
                                                   start=(kc == 0), stop=(kc == 3))
                                return ins
                            TR.op("pe", f, reads=[("QLT", 4 * th + jj) for jj in range(4)] + wqkeys, writes=[("PACC", th)])
                            TR.op("act", lambda e, th=th: e.copy(out=QN[:, th * 512:(th + 1) * 512], in_=PACC[:, th, :]),
                                  reads=[("PACC", th)], writes=[("QN", th)])
                            for sw in range(2):
                                def f2(e, th=th, sw=sw):
                                    for kc in range(4):
                                        ins = e.matmul(PFM[0:64, sw, :], lhsT=wq[:, kc, hh * 256 + 128 + sw * 64: hh * 256 + 192 + sw * 64],
                                                       rhs=QLT[:, kc, th * 512:(th + 1) * 512], start=(kc == 0), stop=(kc == 3))
                                    return ins
                                TR.op("pe", f2, reads=[("QLT", 4 * th + jj) for jj in range(4)] + wqkeys, writes=[("PFM", sw)])
                            rope_combine(PFM[0:64, 0, :], PFM[0:64, 1, :], [("PFM", 0), ("PFM", 1)], th,
                                         QP[0:64, th * 512:(th + 1) * 512], [("QP", th)])
                        SFL = PACC[:].rearrange("p b n -> p (b n)")

                        def nkof(qi):
                            nk = 8 + qi + 1
                            return nk, (nk * 128 + 511) // 512

                        def S_op(qi):
                            nk, nkb = nkof(qi)
                            def fs(e):
                                for kb in range(nkb):
                                    w = min(512, nk * 128 - kb * 512)
                                    e.matmul(PACC[:, kb, 0:w], lhsT=QN[:, qi * 128:(qi + 1) * 128], rhs=KN[:, kb * 512: kb * 512 + w],
                                             start=True, stop=False, skip_group_check=True)
                                for kb in range(nkb):
                                    w = min(512, nk * 128 - kb * 512)
                                    diag = (8 + qi) // 4 == kb
                                    ins = e.matmul(PACC[:, kb, 0:w], lhsT=QP[0:65, qi * 128:(qi + 1) * 128], rhs=KPE[0:65, kb * 512: kb * 512 + w],
                                                   start=False, stop=(not diag), skip_group_check=True)
                                kb = (8 + qi) // 4
                                o = ((8 + qi) % 4) * 128
                                ins = e.matmul(PACC[:, kb, o:o + 128], lhsT=IDB[:], rhs=MASK, start=False, stop=True, skip_group_check=True)
                                return ins
                            TR.op("pe", fs, reads=[("QN", qi // 4), ("QP", qi // 4), ("QP1",), ("KPE",), ("MASK",), ("IDB",)] + [("KN", kb) for kb in range(nkb)],
                                  writes=[("PACC", kb) for kb in range(nkb)])

                        def max_op(qi):
                            nk, nkb = nkof(qi)
                            p = qi % 2
                            pk = [("PACC", kb) for kb in range(nkb)]
                            TR.op("dve", lambda e: e.reduce_max(out=MV[:, p, 0:1], in_=SFL[:, 0:nk * 128], axis=AX.X),
                                  reads=pk, writes=[("AM", p, 0)])
                            TR.op("dve", lambda e: e.tensor_scalar(out=MV[:, p, 1:2], in0=MV[:, p, 0:1], scalar1=-SCALE, scalar2=None, op0=ALU.mult),
                                  reads=[("AM", p, 0)], writes=[("AM", p, 1)])
                            TR.op("dve", lambda e: e.memset(MV[:, p, 2:3], 0.0), writes=[("AM", p, 2)])

                        def exp_op(qi):
                            nk, nkb = nkof(qi)
                            p = qi % 2
                            pk = [("PACC", kb) for kb in range(nkb)]
                            TR.op("act", lambda e: e.activation(out=PBS[p][:, 0:nk * 128], in_=SFL[:, 0:nk * 128], func=AF.Exp,
                                                                bias=MV[:, p, 1:2], scale=SCALE, accum_out=MV[:, p, 2:3]),
                                  reads=pk + [("AM", p, 1), ("AM", p, 2)], writes=[("PB", p), ("AM", p, 2)])

                        def recip_op(qi):
                            p = qi % 2
                            TR.op("dve", lambda e: e.reciprocal(out=MV[:, p, 3:4], in_=MV[:, p, 2:3]), reads=[("AM", p, 2)], writes=[("AM", p, 3)])

                        def T_op(qi):
                            nk, nkb = nkof(qi)
                            p = qi % 2
                            def ft(e):
                                for t in range(nk):
                                    ins = e.transpose(PTR[:, t, :], PBS[p][:, t * 128:(t + 1) * 128], IDB[:])
                                return ins
                            TR.op("pe", ft, reads=[("PB", p), ("IDB",), ("GB",)], writes=[("PTR",)])

                        def evacPT_op(qi):
                            nk, nkb = nkof(qi)
                            p = qi % 2
                            TR.op("act", lambda e: e.copy(out=PTS[p][:, 0:nk, :], in_=PTR[:, 0:nk, :]), reads=[("PTR",)], writes=[("PT", p)])

                        def PV_op(qi):
                            nk, nkb = nkof(qi)
                            p = qi % 2
                            def fo(e):
                                for t in range(nk):
                                    ins = e.matmul(PFM[:, 0, 0:128], lhsT=PTS[p][:, t, :], rhs=V2[:, t, hh * 128:(hh + 1) * 128], start=(t == 0), stop=(t == nk - 1))
                                return ins
                            TR.op("pe", fo, reads=[("PT", p)] + [("V2", t) for t in range(nk)], writes=[("PFM", 0)])

                        def Oscale_op(qi):
                            p = qi % 2
                            TR.op("dve", lambda e: e.tensor_scalar(out=OBS[p], in0=PFM[:, 0, 0:128], scalar1=MV[:, p, 3:4], scalar2=None, op0=ALU.mult),
                                  reads=[("PFM", 0), ("AM", p, 3)], writes=[("OB", p)])

                        def OT_op(qi):
                            p = qi % 2
                            TR.op("pe", lambda e: e.transpose(PO[:, 0:128], OBS[p], IDB[:]), reads=[("OB", p), ("IDB",)], writes=[("PFM", 1)])

                        def evacOT_op(qi):
                            TR.op("act", lambda e: e.copy(out=HT[:, h, qi * 128:(qi + 1) * 128], in_=PO[:, 0:128]),
                                  reads=[("PFM", 1)], writes=[("HT", qi)])

                        S_op(0); max_op(0); exp_op(0); recip_op(0)
                        for qi in range(NT):
                            if qi + 1 < NT:
                                S_op(qi + 1)
                            T_op(qi)
                            if qi + 1 < NT:
                                max_op(qi + 1)
                                exp_op(qi + 1)
                                recip_op(qi + 1)
                            evacPT_op(qi)
                            PV_op(qi)
                            Oscale_op(qi)
                            OT_op(qi)
                            evacOT_op(qi)
                steps.append(Step(lambda wt, fn_kv=fn_kv, fn_q=fn_q: (fn_kv(wt), fn_q(wt)), wqkv_d[j, g], 2))

            for b in range(4):
                def fn_o(wt, b=b):
                    wv, wkeys = wt
                    wv = wv.rearrange("p (k n) -> p k n", k=16)
                    proj_block(wv, b, lambda k, i: HT[:, k, i * 128:(i + 1) * 128], lambda i: [("HT", i)], first=True, wkeys=wkeys)
                steps.append(Step(fn_o, wo_d[j, b], 4))

        for li, l in enumerate(layers):
            last = (li == len(layers) - 1)
            if l < 2:
                conv_phase(l, conv_layers.index(l))
            else:
                mla_phase(l - 2, l)
            ln_phase(l, 0)
            mlp_phase(li, l)
            ln_phase(l, 1, final_store=(last and stage != "L1"))
            if FUSED and l == 0:
                def halo_export(_):
                    HS = sb("HS", [128, 16, 2], BF16)
                    TR.op("act", lambda e: e.copy(out=HS[:], in_=HT[:, :, T - 2:T]), reads=[("HT", NT - 1)], writes=[("HS",)])
                    TR.dma("sp", halo_src[:, :], HS[:].rearrange("p k t -> p (k t)"), "halo", reads=[("HS",)], writes=[("HALOSRC",)])
                steps.append(Step(halo_export))
            if FUSED and l == 1:
                kv_latent_phase()
            if stage == "L1":
                kv_latent_phase()

                def store_h(_):
                    for i in range(NT):
                        TR.dma("sp", out_d[i * 128:(i + 1) * 128, :], H32[:, i, :], "out", reads=[("H32", i, b) for b in range(4)])
                steps.append(Step(store_h))

            _LAYER_ENDS.append(len(steps))
        wsteps = [si for si, s in enumerate(steps) if s.w is not None]
        slots_of = {}
        pos = 0
        for si in wsteps:
            n = steps[si].nslots
            if pos + n > NSLOT:
                pos = 0
            slots_of[si] = list(range(pos, pos + n))
            pos += n
        owner = [None] * NSLOT
        nxt = 0
        wtile = {}

        def issue(si):
            s = steps[si]
            sl = slots_of[si]
            width = s.w.shape[-1]
            view = WS[:, sl[0] * SLOT: sl[0] * SLOT + width]
            keys = [("WS", k) for k in sl]
            TR.dma("pool", view, s.w, "ws%d" % sl[0], writes=keys)
            wtile[si] = (view, keys)
            for k in sl:
                owner[k] = si

        if max_steps is not None:
            steps = steps[:max_steps]
            wsteps = [si for si in wsteps if si < max_steps]

            def dbg_store(_):
                for i in range(NT):
                    TR.dma("sp", out_d[i * 128:(i + 1) * 128, :], H32[:, i, :], "out", reads=[("H32", i, b) for b in range(4)],
                           extra=TR.barrier_keys())
            steps.append(Step(dbg_store))
        for si, s in enumerate(steps):
            while nxt < len(wsteps):
                cand = wsteps[nxt]
                if all(owner[k] is None or owner[k] < si for k in slots_of[cand]) and cand - si < 12:
                    issue(cand)
                    nxt += 1
                else:
                    break
            if s.w is not None:
                assert si in wtile, "weight not issued before use"
                s.fn(wtile[si])
            else:
                s.fn(None)
        if "out" in TR.dsem:
            ent = TR.dsem["out"]
            nc.sync.wait_ge(ent[0], ent[1])
        stuck, prog = TR.check_deadlock()
        if stuck:
            raise RuntimeError("sync deadlock: %r %r" % (stuck, prog))
        nc._tr_stats = prog if False else None
    return nc


_BF = ml_dtypes.bfloat16


def _rope_tables(pos0):
    inv = (1.0 / (np.float32(10000.0) ** (np.arange(0, 64, 2, dtype=np.float32) / np.float32(64)))).astype(np.float32)
    ang = (np.arange(pos0, pos0 + T, dtype=np.float32)[:, None] * inv[None, :]).astype(np.float32)
    cos, sin = np.cos(ang).astype(np.float32), np.sin(ang).astype(np.float32)
    cos2 = np.concatenate([cos, cos], axis=1).T
    sin2 = np.concatenate([-sin, sin], axis=1).T
    return np.ascontiguousarray(np.concatenate([cos2, sin2], axis=1).astype(np.float32))


def _chunk_kn(w, n):
    K, N = w.shape
    return np.ascontiguousarray(w.reshape(K // 128, 128, N // n, n).transpose(2, 1, 0, 3)).reshape(N // n, 128, (K // 128) * n)


_PROGS = {}


def _prog(stage):
    if stage not in _PROGS:
        _PROGS[stage] = build_program(stage)
    return _PROGS[stage]


def kernel(x, ln_g, ln_b, conv_w_in, conv_w, conv_w_out, kv_w_dkv, kv_norm_g, kv_w_ukv,
           mla_w_dq, mla_q_norm_g, mla_w_uq, mla_w_o, mlp_w1, mlp_w2):
    f = lambda a: np.asarray(a, dtype=np.float32)
    x, ln_g, ln_b = f(x), f(ln_g), f(ln_b)
    conv_w_in, conv_w, conv_w_out = f(conv_w_in), f(conv_w), f(conv_w_out)
    kv_w_dkv, kv_norm_g, kv_w_ukv = f(kv_w_dkv), f(kv_norm_g), f(kv_w_ukv)
    mla_w_dq, mla_q_norm_g, mla_w_uq, mla_w_o = f(mla_w_dq), f(mla_q_norm_g), f(mla_w_uq), f(mla_w_o)
    mlp_w1, mlp_w2 = f(mlp_w1), f(mlp_w2)
    n = 8
    cores = list(range(n))
    ident = np.eye(128, dtype=np.float32).astype(_BF)
    qq, kk = np.meshgrid(np.arange(128), np.arange(128), indexing="ij")
    mask = np.where(kk <= qq, 0.0, NEG).astype(np.float32).astype(_BF)
    small = np.zeros((128, 128), np.float32)
    for l in range(2):
        small[:, l * 48:(l + 1) * 48] = conv_w[l].reshape(3, 16, 128).transpose(2, 1, 0).reshape(128, 48)
    small[:, 96:100] = kv_norm_g.reshape(4, 128).T
    for j in range(2):
        small[:, 100 + 4 * j:104 + 4 * j] = mla_q_norm_g[j].reshape(4, 128).T
    w1r = np.stack([_chunk_kn(mlp_w1[l], 128) for l in range(4)])
    w2r = np.stack([np.ascontiguousarray(mlp_w2[l].reshape(4, 16, 128, 4, 512).transpose(0, 3, 2, 1, 4)).reshape(4, 4, 128, 16 * 512)
                    for l in range(4)])
    win = np.stack([np.ascontiguousarray(conv_w_in[l].reshape(16, 128, 3, 16, 128).transpose(3, 2, 1, 0, 4)).reshape(16, 3, 128, 16 * 128)
                    for l in range(2)])
    wout = np.stack([_chunk_kn(conv_w_out[l], 512) for l in range(2)])
    wdkv_lat = _chunk_kn(kv_w_dkv[:, :512], 512)[0]
    pe = kv_w_dkv[:, 512:576]
    wdkv_pe = _chunk_kn(np.concatenate([pe, pe[:, 32:], pe[:, :32]], axis=1), 128)[0]
    wdq = np.stack([_chunk_kn(mla_w_dq[j], 512)[0] for j in range(2)])
    wuq = []
    for j in range(2):
        w = mla_w_uq[j].reshape(512, 16, 192)
        wa = np.concatenate([w, w[:, :, 160:192], w[:, :, 128:160]], axis=2)
        wuq.append(_chunk_kn(wa.reshape(512, 4096), 512))
    wuq = np.stack(wuq)
    wukv = _chunk_kn(kv_w_ukv, 512)
    wqkv = np.ascontiguousarray(np.concatenate([np.broadcast_to(wukv[None], wuq.shape), wuq], axis=-1))
    wo = np.stack([_chunk_kn(mla_w_o[j], 512) for j in range(2)])
    ropes = [_rope_tables(0), _rope_tables(T)]

    def halo_t(rows):
        return np.ascontiguousarray(rows.reshape(2, 16, 128).transpose(2, 1, 0)).reshape(128, 32)

    maps = []
    for c in cores:
        b, hf = c // 2, c % 2
        sm = small.copy()
        sm[:, 110] = float(hf)
        halo = np.zeros((2, D), np.float32) if hf == 0 else x[b, T - 2:T]
        mrow = np.zeros((1, 2 * T), np.float32)
        if hf == 0:
            mrow[0, :T] = NEG
        maps.append({"x": np.ascontiguousarray(x[b, hf * T:(hf + 1) * T]), "ident": ident, "ln_g": ln_g, "ln_b": ln_b,
                     "smallp": sm, "w1": w1r, "w2": w2r, "w_in": win, "w_out": wout, "halo_t": halo_t(halo),
                     "wdkv_lat": wdkv_lat, "wdkv_pe": wdkv_pe, "wdq": wdq, "wqkv": wqkv, "wo": wo,
                     "rope": ropes[hf], "mask": mask, "maskrow": mrow.astype(_BF)})
    res = run_bass_kernel_spmd(_prog("ALL"), maps, core_ids=cores).results
    out = np.empty((4, 2048, D), np.float32)
    for c in cores:
        out[c // 2, (c % 2) * T:(c % 2 + 1) * T] = np.asarray(res[c]["out"], dtype=np.float32)
    return out
```

```python
import contextlib
import numpy as np
import ml_dtypes
import concourse.bass as bass
import concourse.mybir as mybir
from concourse.bass_utils import run_bass_kernel_spmd

F32 = mybir.dt.float32
BF16 = mybir.dt.bfloat16
ALU = mybir.AluOpType
AF = mybir.ActivationFunctionType
AX = mybir.AxisListType

D = 2048
T = 1024
NT = 8
KD = 16
NH = 16
DEPTH = 4
ALPHA = float((2 * DEPTH) ** 0.25)
LN_EPS = 1e-5
RMS_EPS = 1e-6
SCALE = float(192 ** -0.5)
NEG = -30000.0
NSLOT = 8
SLOT = 2048
SEM_LIMIT = 30000
PSUM_KEYS = ("PACC", "PTR", "PFM")


class Tok:
    __slots__ = ("sem", "name", "val", "eng")

    def __init__(self, sem, name, val, eng):
        self.sem, self.name, self.val, self.eng = sem, name, val, eng


class Tracker:
    def __init__(self, nc, es):
        self.nc, self.es = nc, es
        self.E = {"pe": nc.tensor, "act": nc.scalar, "dve": nc.vector, "pool": nc.gpsimd, "sp": nc.sync}
        self.sem, self.semname, self.cnt, self.gen = {}, {}, {}, {}
        for e in self.E:
            self.gen[e] = 0
            self._newsem(e)
        self.waited = {e: {} for e in self.E}
        self.last_w = {}
        self.readers = {}
        self.dsem = {}
        self.n_inst = {e: 0 for e in self.E}
        self.trace = {e: [] for e in self.E}

    def check_deadlock(self):
        pc = {e: 0 for e in self.E}
        sems = {}
        progress = True
        while progress:
            progress = False
            for e in self.E:
                tr = self.trace[e]
                while pc[e] < len(tr):
                    kind, name, val = tr[pc[e]]
                    if kind == "w":
                        if sems.get(name, 0) >= val:
                            pc[e] += 1; progress = True
                        else:
                            break
                    else:
                        sems[name] = sems.get(name, 0) + val
                        pc[e] += 1; progress = True
        stuck = {e: self.trace[e][pc[e]] for e in self.E if pc[e] < len(self.trace[e])}
        return stuck, {e: (pc[e], len(self.trace[e])) for e in self.E}

    def _newsem(self, e):
        nm = "s_%s_%d" % (e, self.gen[e])
        self.gen[e] += 1
        self.sem[e] = self.es.enter_context(self.nc.semaphore(nm))
        self.semname[e] = nm
        self.cnt[e] = 0

    def _wait(self, e, tok):
        w = self.waited[e]
        if w.get(tok.name, 0) >= tok.val:
            return
        self.E[e].wait_ge(tok.sem, tok.val)
        self.trace[e].append(("w", tok.name, tok.val))
        w[tok.name] = tok.val

    def _deps(self, e, reads, writes, extra):
        for k in reads:
            t = self.last_w.get(k)
            if t is not None:
                self._wait(e, t)
        for k in writes:
            t = self.last_w.get(k)
            if t is not None and not (t.eng == e and t.eng != "dma"):
                self._wait(e, t)
            for t in self.readers.get(k, {}).values():
                if not (t.eng == e and t.eng != "dma"):
                    self._wait(e, t)
        for t in extra:
            if t is not None:
                self._wait(e, t)

    def _record(self, tok, reads, writes):
        for k in reads:
            self.readers.setdefault(k, {})[tok.eng if tok.eng != "dma" else tok.name] = tok
        for k in writes:
            self.last_w[k] = tok
            self.readers[k] = {}

    def op(self, e, fn, reads=(), writes=(), extra=()):
        xr = [k for k in reads if k[0] in PSUM_KEYS]
        if xr:
            writes = list(writes) + xr
        self._deps(e, reads, writes, extra)
        ins = fn(self.E[e])
        if self.cnt[e] >= SEM_LIMIT:
            self._newsem(e)
        self.cnt[e] += 1
        ins.then_inc(self.sem[e], 1)
        self.trace[e].append(("i", self.semname[e], 1))
        tok = Tok(self.sem[e], self.semname[e], self.cnt[e], e)
        self._record(tok, reads, writes)
        return tok

    def dma(self, q, out, in_, semname, reads=(), writes=(), extra=()):
        self._deps(q, reads, writes, extra)
        if semname not in self.dsem:
            self.dsem[semname] = [self.es.enter_context(self.nc.semaphore("d_" + semname)), 0]
        ent = self.dsem[semname]
        ins = self.E[q].dma_start(out=out, in_=in_)
        ent[1] += 16
        ins.then_inc(ent[0], 16)
        self.trace[q].append(("i", "d_" + semname, 16))
        tok = Tok(ent[0], "d_" + semname, ent[1], "dma")
        self._record(tok, reads, writes)
        return tok

    def barrier_keys(self):
        return [Tok(self.sem[e], self.semname[e], self.cnt[e], e) for e in self.E if self.cnt[e] > 0]


_LAYER_ENDS = []


class Step:
    def __init__(self, fn, w=None, nslots=0):
        self.fn, self.w, self.nslots = fn, w, nslots


def build_program(stage, max_steps=None):
    layers = {"L0": [0], "L1": [1], "L23": [2, 3], "ALL": [0, 1, 2, 3]}[stage]
    FUSED = stage == "ALL"
    PAIRS = [[0, 1], [2, 3], [4, 5], [6, 7]]
    nc = bass.Bass("TRN2", target_bir_lowering=False)

    def din(name, shape, dt=F32):
        return nc.dram_tensor(name, list(shape), dt, kind="ExternalInput").ap()

    def dout(name, shape, dt=F32):
        return nc.dram_tensor(name, list(shape), dt, kind="ExternalOutput").ap()

    x_d = din("x", [T, D])
    ident_d = din("ident", [128, 128], BF16)
    lng_d = din("ln_g", [DEPTH, 2, D])
    lnb_d = din("ln_b", [DEPTH, 2, D])
    small_d = din("smallp", [128, 128])
    w1_d = din("w1", [len(layers), 64, 128, KD * 128])
    w2_d = din("w2", [len(layers), 4, 4, 128, KD * 512])
    out_d = dout("out", [T, D])
    conv_layers = [l for l in layers if l < 2]
    mla_layers = [l for l in layers if l >= 2]
    if conv_layers:
        win_d = din("w_in", [len(conv_layers), 16, 3, 128, KD * 128])
        wout_d = din("w_out", [len(conv_layers), 4, 128, KD * 512])
        halo_d = din("halo_t", [128, KD * 2])
    if stage in ("L1", "ALL"):
        wdkv_lat_d = din("wdkv_lat", [128, KD * 512])
        wdkv_pe_d = din("wdkv_pe", [128, KD * 128])
    if stage == "L1":
        rope_d = din("rope", [64, 2 * T])
        ct_own_d = dout("ct_own", [128, 4 * T], BF16)
        kpe_own_d = dout("kpe_own", [64, T], BF16)
    if mla_layers:
        wdq_d = din("wdq", [2, 128, KD * 512])
        wqkv_d = din("wqkv", [2, 8, 128, 2 * 4 * 512])
        wo_d = din("wo", [2, 4, 128, KD * 512])
        rope_d = din("rope", [64, 2 * T])
        mask_d = din("mask", [128, 128], BF16)
    if stage == "L23":
        ct_all_d = din("ct_all", [128, 4 * 2 * T], BF16)
        kpe_all_d = din("kpe_all", [65, 2 * T], BF16)
    if FUSED:
        maskrow_d = din("maskrow", [1, 2 * T], BF16)
        halo_src = nc.dram_tensor("halo_src", [128, 32], BF16, kind="Internal").ap()
        halo_all = nc.dram_tensor("halo_all", [256, 32], BF16, addr_space="Local", kind="Internal").ap()
        lat_src = nc.dram_tensor("lat_src", [128, 5 * T], BF16, kind="Internal").ap()
        lat_all = nc.dram_tensor("lat_all", [256, 5 * T], BF16, addr_space="Local", kind="Internal").ap()

    es = contextlib.ExitStack()
    with es:
        def sb(name, shape, dt):
            return es.enter_context(nc.sbuf_tensor(name, list(shape), dt))

        def ps(name, shape, dt):
            return es.enter_context(nc.psum_tensor(name, list(shape), dt))

        TR = Tracker(nc, es)
        H32 = sb("H32", [128, NT, D], F32)
        HT = sb("HT", [128, KD, T], BF16)
        GB = sb("GB", [128, 2, D], F32)
        WS = sb("WS", [128, NSLOT * SLOT], BF16)
        ARN = 57344 // 2
        AR = sb("AR", [128, ARN], BF16)
        IDB = sb("IDB", [128, 128], BF16)
        SMALL = sb("SMALL", [128, 128], F32)
        ST = sb("ST", [128, 8, 24], F32)
        MV = sb("MV", [128, 8, 8], F32)
        SC32 = sb("SC32", [128, 2, 512], F32)
        AMX = sb("AMX", [128, 2, 8], F32)
        PACC = ps("PACC", [128, 4, 512], F32)
        PTR = ps("PTR", [128, KD, 128], BF16)
        PFM = ps("PFM", [128, 2, 512], F32)

        def arena(off_bytes, nbytes, dt):
            v = AR[:, off_bytes // 2:(off_bytes + nbytes) // 2]
            return v if dt == BF16 else v.bitcast(F32)

        steps = []

        def init_fn(_):
            TR.dma("sp", IDB[:], ident_d[:, :], "c0", writes=[("IDB",)])
            TR.dma("sp", SMALL[:], small_d[:, :], "c0", writes=[("SMALL",)])
            TR.dma("sp", H32[:], x_d.rearrange("(i p) d -> p i d", p=128), "c1",
                   writes=[("H32", i, b) for i in range(NT) for b in range(4)])
            HB = arena(49152, 8192, BF16).rearrange("p (a n) -> p a n", a=2)
            for i in range(NT):
                hb = HB[:, i % 2, :]
                TR.op("act", lambda e, i=i, hb=hb: e.copy(out=hb, in_=H32[:, i, :]),
                      reads=[("H32", i, b) for b in range(4)], writes=[("HB", i % 2)])
                to_feat(i, hb, ("HB", i % 2))
        steps.append(Step(init_fn))

        def to_feat(i, hb, hbkey):
            def f(e):
                for k in range(KD):
                    ins = e.transpose(PTR[:, k, :], hb[:, k * 128:(k + 1) * 128], IDB[:])
                return ins
            TR.op("pe", f, reads=[hbkey, ("IDB",)], writes=[("PTR",)])
            TR.op("act", lambda e: e.copy(out=HT[:, :, i * 128:(i + 1) * 128], in_=PTR[:]),
                  reads=[("PTR",)], writes=[("HT", i)])

        def ln_phase(l, s, final_store=False):
            def load_fn(_):
                TR.dma("sp", GB[:, 0, :], lng_d[l, s].partition_broadcast(128), "gb", writes=[("GB",)])
                TR.dma("sp", GB[:, 1, :], lnb_d[l, s].partition_broadcast(128), "gb", writes=[("GB",)])
            steps.append(Step(load_fn))

            def fn(_):
                HB = arena(49152, 8192, BF16).rearrange("p (a n) -> p a n", a=2)
                hk = lambda i: [("H32", i, b) for b in range(4)]
                R = range(NT)
                for i in R:
                    def stats(e, i=i):
                        for c in range(4):
                            ins = e.bn_stats(out=ST[:, i, c * 6:(c + 1) * 6], in_=H32[:, i, c * 512:(c + 1) * 512])
                        return ins
                    TR.op("dve", stats, reads=hk(i), writes=[("ST", i)])
                for i in R:
                    TR.op("dve", lambda e, p=i: e.bn_aggr(out=MV[:, p, 0:2], in_=ST[:, p, :]),
                          reads=[("ST", i)], writes=[("MV", i, 0)])
                for i in R:
                    TR.op("dve", lambda e, p=i: e.tensor_scalar(out=MV[:, p, 2:3], in0=MV[:, p, 1:2], scalar1=LN_EPS,
                                                               scalar2=None, op0=ALU.add),
                          reads=[("MV", i, 0)], writes=[("MV", i, 1)])
                for i in R:
                    TR.op("act", lambda e, p=i: e.activation(out=MV[:, p, 3:4], in_=MV[:, p, 2:3], func=AF.Sqrt),
                          reads=[("MV", i, 1)], writes=[("MV", i, 2)])
                for i in R:
                    TR.op("dve", lambda e, p=i: e.reciprocal(out=MV[:, p, 4:5], in_=MV[:, p, 3:4]),
                          reads=[("MV", i, 2)], writes=[("MV", i, 3)])
                for i in R:
                    TR.op("dve", lambda e, p=i: e.scalar_tensor_tensor(out=MV[:, p, 5:6], in0=MV[:, p, 0:1], scalar=-1.0,
                                                                      in1=MV[:, p, 4:5], op0=ALU.mult, op1=ALU.mult),
                          reads=[("MV", i, 0), ("MV", i, 3)], writes=[("MV", i, 4)])
                for i in R:
                    TR.op("act", lambda e, i=i: e.activation(out=H32[:, i, :], in_=H32[:, i, :], func=AF.Identity,
                                                            bias=MV[:, i, 5:6], scale=MV[:, i, 4:5]),
                          reads=hk(i) + [("MV", i, 3), ("MV", i, 4)], writes=hk(i))
                for i in R:
                    TR.op("dve", lambda e, i=i: e.tensor_tensor(out=H32[:, i, :], in0=H32[:, i, :], in1=GB[:, 0, :], op=ALU.mult),
                          reads=hk(i) + [("GB",)], writes=hk(i))
                    TR.op("pool", lambda e, i=i: e.tensor_tensor(out=H32[:, i, :], in0=H32[:, i, :], in1=GB[:, 1, :], op=ALU.add),
                          reads=hk(i) + [("GB",)], writes=hk(i))
                    if final_store:
                        TR.dma("sp", out_d[i * 128:(i + 1) * 128, :], H32[:, i, :], "out", reads=hk(i))
                for i in R:
                    if not final_store:
                        p = i % 2
                        hb = HB[:, p, :]
                        TR.op("act", lambda e, i=i, hb=hb: e.copy(out=hb, in_=H32[:, i, :]),
                              reads=hk(i), writes=[("HB", p)])
                        to_feat(i, hb, ("HB", p))
            steps.append(Step(fn))

        def proj_block(wv, b, lhs_fn, lhs_keys_fn, first, wkeys, nk=16):
            for i in range(NT):
                bank = (b * NT + i) % 4
                def f(e, i=i, bank=bank):
                    for k in range(nk):
                        ins = e.matmul(PACC[:, bank, :], lhsT=lhs_fn(k, i), rhs=wv[:, k, :], start=(k == 0), stop=(k == nk - 1))
                    return ins
                TR.op("pe", f, reads=lhs_keys_fn(i) + wkeys, writes=[("PACC", bank)])
                dst = H32[:, i, b * 512:(b + 1) * 512]
                if first:
                    TR.op("dve", lambda e, dst=dst, bank=bank: e.scalar_tensor_tensor(
                        out=dst, in0=dst, scalar=ALPHA, in1=PACC[:, bank, :], op0=ALU.mult, op1=ALU.add),
                        reads=[("H32", i, b), ("PACC", bank)], writes=[("H32", i, b)])
                else:
                    TR.op("dve", lambda e, dst=dst, bank=bank: e.tensor_tensor(out=dst, in0=dst, in1=PACC[:, bank, :], op=ALU.add),
                          reads=[("H32", i, b), ("PACC", bank)], writes=[("H32", i, b)])

        def mlp_phase(li, l):
            AT = arena(0, 32768, BF16).rearrange("p (c t) -> p c t", c=16)
            RL = arena(32768, 4096, F32).rearrange("p (a n) -> p a n", a=2)
            for q in range(4):
                for c in range(16):
                    fch = q * 16 + c
                    def fn(wt, c=c):
                        wv, wkeys = wt
                        wv = wv.rearrange("p (k n) -> p k n", k=KD)
                        for th in range(2):
                            def f(e, th=th):
                                for k in range(KD):
                                    ins = e.matmul(PFM[:, th, :], lhsT=wv[:, k, :], rhs=HT[:, k, th * 512:(th + 1) * 512],
                                                   start=(k == 0), stop=(k == KD - 1))
                                return ins
                            TR.op("pe", f, reads=[("HT", 4 * th + j) for j in range(4)] + wkeys, writes=[("PFM", th)])
                            TR.op("act", lambda e, th=th: e.activation(out=RL[:, th, :], in_=PFM[:, th, :], func=AF.Relu),
                                  reads=[("PFM", th)], writes=[("RL", th)])
                            TR.op("dve", lambda e, th=th, c=c: e.tensor_tensor(out=AT[:, c, th * 512:(th + 1) * 512], in0=RL[:, th, :],
                                                                              in1=RL[:, th, :], op=ALU.mult),
                                  reads=[("RL", th)], writes=[("AT", c, th)])
                    steps.append(Step(fn, w1_d[li, fch], 1))
                for b in range(4):
                    def fn2(wt, q=q, b=b):
                        wv, wkeys = wt
                        wv = wv.rearrange("p (k n) -> p k n", k=16)
                        proj_block(wv, b, lambda k, i: AT[:, k, i * 128:(i + 1) * 128],
                                   lambda i: [("AT", k, i // 4) for k in range(16)], first=(q == 0), wkeys=wkeys)
                    steps.append(Step(fn2, w2_d[li, q, b], 4))

        def conv_phase(l, ci):
            ZT = arena(0, 32768, BF16).rearrange("p (c t) -> p c t", c=16)
            TC = arena(32768, 4128, F32)
            V = arena(36896, 4128, F32)
            Y = arena(41024, 4096, F32)
            HALO = sb("HALO%d" % l, [128, KD, 2], BF16)
            PH = PTR[:].rearrange("p k n -> p (k n)").bitcast(F32)

            def load_halo(_):
                if FUSED and l == 1:
                    HL = sb("HL", [128, 32], BF16)
                    TR.op("pool", lambda e: e.collective_compute("AllGather", ALU.bypass, replica_groups=PAIRS, ins=[halo_src], outs=[halo_all]),
                          reads=[("HALOSRC",)], writes=[("HALOALL",)])
                    TR.dma("sp", HL[:], halo_all[0:128, :], "halo", reads=[("HALOALL",)], writes=[("HL",)])
                    TR.op("dve", lambda e: e.tensor_scalar(out=HALO[:].rearrange("p k t -> p (k t)"), in0=HL[:], scalar1=SMALL[:, 110:111],
                                                           scalar2=None, op0=ALU.mult),
                          reads=[("HL",), ("SMALL",)], writes=[("HALO",)])
                else:
                    TR.dma("pool", HALO[:].rearrange("p k t -> p (k t)"), halo_d[:, :], "halo", writes=[("HALO",)])
            steps.append(Step(load_halo))
            cw = lambda c, j: SMALL[:, l * 48 + c * 3 + j: l * 48 + c * 3 + j + 1]
            for c in range(16):
                def fn_part(wt, c=c, part=0):
                    wv, wkeys = wt
                    wv = wv.rearrange("p (k n) -> p k n", k=KD)
                    banks = {1: (PACC, 0), 2: (PACC, 2), 0: (PFM, 0)}[part]
                    pt, b0 = banks
                    pkey = "PACC" if pt is PACC else "PFM"
                    if part in (1, 2):
                        col = 0 if part == 1 else 2
                        def fh(e):
                            for k in range(KD):
                                ins = e.matmul(PH[:, col:col + 2], lhsT=wv[:, k, :], rhs=HALO[:, k, :], start=(k == 0), stop=(k == KD - 1))
                            return ins
                        TR.op("pe", fh, reads=[("HALO",)] + wkeys, writes=[("PTR",)])
                    for th in range(2):
                        def f(e, th=th):
                            for k in range(KD):
                                ins = e.matmul(pt[:, b0 + th, :], lhsT=wv[:, k, :], rhs=HT[:, k, th * 512:(th + 1) * 512],
                                               start=(k == 0), stop=(k == KD - 1))
                            return ins
                        TR.op("pe", f, reads=[("HT", 4 * th + j) for j in range(4)] + wkeys, writes=[(pkey, b0 + th)])
                    if part == 1:
                        for th in range(2):
                            TR.op("act", lambda e, th=th: e.copy(out=TC[:, th * 512:(th + 1) * 512], in_=PACC[:, th, :]),
                                  reads=[("PACC", th)], writes=[("TC", th)])
                        TR.op("act", lambda e: e.copy(out=TC[:, 1024:1026], in_=PH[:, 0:2]), reads=[("PTR",)], writes=[("TC", 2)])
                    elif part == 2:
                        for th in range(2):
                            TR.op("dve", lambda e, th=th: e.tensor_tensor(out=V[:, 2 + th * 512: 2 + (th + 1) * 512], in0=TC[:, th * 512:(th + 1) * 512],
                                                                         in1=PACC[:, 2 + th, :], op=ALU.mult),
                                  reads=[("TC", th), ("PACC", 2 + th)], writes=[("V", th)])
                        TR.op("dve", lambda e: e.tensor_tensor(out=V[:, 0:2], in0=TC[:, 1024:1026], in1=PH[:, 2:4], op=ALU.mult),
                              reads=[("TC", 2), ("PTR",)], writes=[("V", 2)])
                        vk = [("V", 0), ("V", 1), ("V", 2)]
                        TR.op("dve", lambda e, c=c: e.tensor_scalar(out=Y[:, 0:1024], in0=V[:, 2:1026], scalar1=cw(c, 2), scalar2=None, op0=ALU.mult),
                              reads=vk + [("SMALL",)], writes=[("Y",)])
                        TR.op("dve", lambda e, c=c: e.scalar_tensor_tensor(out=Y[:, 0:1024], in0=V[:, 1:1025], scalar=cw(c, 1), in1=Y[:, 0:1024],
                                                                          op0=ALU.mult, op1=ALU.add),
                              reads=vk + [("Y",), ("SMALL",)], writes=[("Y",)])
                        TR.op("dve", lambda e, c=c: e.scalar_tensor_tensor(out=Y[:, 0:1024], in0=V[:, 0:1024], scalar=cw(c, 0), in1=Y[:, 0:1024],
                                                                          op0=ALU.mult, op1=ALU.add),
                              reads=vk + [("Y",), ("SMALL",)], writes=[("Y",)])
                    else:
                        for th in range(2):
                            TR.op("dve", lambda e, th=th, c=c: e.tensor_tensor(out=ZT[:, c, th * 512:(th + 1) * 512], in0=Y[:, th * 512:(th + 1) * 512],
                                                                              in1=PFM[:, th, :], op=ALU.mult),
                                  reads=[("Y",), ("PFM", th)], writes=[("ZT", c, th)])
                for part in (1, 2, 0):
                    steps.append(Step(lambda wt, c=c, part=part, fn_part=fn_part: fn_part(wt, c, part), win_d[ci, c, part], 1))
            for b in range(4):
                def fn2(wt, b=b):
                    wv, wkeys = wt
                    wv = wv.rearrange("p (k n) -> p k n", k=16)
                    proj_block(wv, b, lambda k, i: ZT[:, k, i * 128:(i + 1) * 128],
                               lambda i: [("ZT", k, i // 4) for k in range(16)], first=True, wkeys=wkeys)
                steps.append(Step(fn2, wout_d[ci, b], 4))

        def rms_to_feat(i, bank, gcol0, dstT, dkey):
            p = i % 2
            SQ = SC32[:, p, :]
            NB = arena(53248 + p * 1024, 1024, BF16)
            TR.op("dve", lambda e: e.memset(MV[:, p, 6:7], 0.0), writes=[("MV", p, 6)])
            TR.op("act", lambda e: e.activation(out=SQ, in_=PACC[:, bank, :], func=AF.Square, accum_out=MV[:, p, 6:7]),
                  reads=[("PACC", bank), ("MV", p, 6)], writes=[("SQ", p), ("MV", p, 6)])
            TR.op("dve", lambda e: e.tensor_scalar(out=MV[:, p, 7:8], in0=MV[:, p, 6:7], scalar1=1.0 / 512, scalar2=RMS_EPS,
                                                   op0=ALU.mult, op1=ALU.add),
                  reads=[("MV", p, 6)], writes=[("MV", p, 7)])
            TR.op("act", lambda e: e.activation(out=MV[:, p, 6:7], in_=MV[:, p, 7:8], func=AF.Sqrt),
                  reads=[("MV", p, 7)], writes=[("MV", p, 6)])
            TR.op("dve", lambda e: e.reciprocal(out=MV[:, p, 7:8], in_=MV[:, p, 6:7]),
                  reads=[("MV", p, 6)], writes=[("MV", p, 7)])
            TR.op("dve", lambda e: e.tensor_scalar(out=NB, in0=PACC[:, bank, :], scalar1=MV[:, p, 7:8], scalar2=None, op0=ALU.mult),
                  reads=[("PACC", bank), ("MV", p, 7)], writes=[("NB", p)])
            def f(e):
                for kc in range(4):
                    ins = e.transpose(PTR[:, kc, :], NB[:, kc * 128:(kc + 1) * 128], IDB[:])
                return ins
            TR.op("pe", f, reads=[("NB", p), ("IDB",)], writes=[("PTR",)])
            for kc in range(4):
                TR.op("dve", lambda e, kc=kc: e.tensor_scalar(out=dstT[:, kc, i * 128:(i + 1) * 128], in0=PTR[:, kc, :],
                                                             scalar1=SMALL[:, gcol0 + kc: gcol0 + kc + 1], scalar2=None, op0=ALU.mult),
                      reads=[("PTR",), ("SMALL",)], writes=[(dkey, i)])

        def load_rope():
            def fn(_):
                TR.dma("sp", GB[0:64, 0, :], rope_d[:, :], "gb", writes=[("GB",)])
            steps.append(Step(fn))

        ROPE = GB[0:64, 0, :]
        RT = GB[0:64, 1, :]

        def rope_combine(pa, pb, keys_in, th, dst, dkeys):
            TR.op("dve", lambda e: e.tensor_tensor(out=RT[:, 0:512], in0=pa, in1=ROPE[:, th * 512:(th + 1) * 512], op=ALU.mult),
                  reads=keys_in + [("GB",)], writes=[("RT", 0)])
            TR.op("dve", lambda e: e.tensor_tensor(out=RT[:, 512:1024], in0=pb, in1=ROPE[:, T + th * 512:T + (th + 1) * 512], op=ALU.mult),
                  reads=keys_in + [("GB",)], writes=[("RT", 1)])
            TR.op("dve", lambda e: e.tensor_tensor(out=dst, in0=RT[:, 0:512], in1=RT[:, 512:1024], op=ALU.add),
                  reads=[("RT", 0), ("RT", 1), ("GB",)], writes=dkeys)

        def kv_latent_phase():
            CTO = arena(0, 8192, BF16).rearrange("p (k t) -> p k t", k=4)
            KPO = arena(8192, 2048, BF16)
            load_rope()

            def fn(wt):
                wv, wkeys = wt
                wv = wv.rearrange("p (k n) -> p k n", k=KD)
                for i in range(NT):
                    bank = i % 4
                    def f(e, i=i, bank=bank):
                        for k in range(KD):
                            ins = e.matmul(PACC[:, bank, :], lhsT=HT[:, k, i * 128:(i + 1) * 128], rhs=wv[:, k, :], start=(k == 0), stop=(k == KD - 1))
                        return ins
                    TR.op("pe", f, reads=[("HT", i)] + wkeys, writes=[("PACC", bank)])
                    rms_to_feat(i, bank, 96, CTO, "CTO")
            steps.append(Step(fn, wdkv_lat_d, 4))

            def fn2(wt):
                wv, wkeys = wt
                wv = wv.rearrange("p (k n) -> p k n", k=KD)
                for th in range(2):
                    for sw in range(2):
                        def f(e, th=th, sw=sw):
                            for k in range(KD):
                                ins = e.matmul(PFM[0:64, sw, :], lhsT=wv[:, k, sw * 64:(sw + 1) * 64], rhs=HT[:, k, th * 512:(th + 1) * 512],
                                               start=(k == 0), stop=(k == KD - 1))
                            return ins
                        TR.op("pe", f, reads=[("HT", 4 * th + j) for j in range(4)] + wkeys, writes=[("PFM", sw)])
                    rope_combine(PFM[0:64, 0, :], PFM[0:64, 1, :], [("PFM", 0), ("PFM", 1)], th,
                                 KPO[0:64, th * 512:(th + 1) * 512], [("KPO", th)])
                if FUSED:
                    TR.dma("sp", lat_src[:, 0:4 * T], CTO[:].rearrange("p k t -> p (k t)"), "lat", reads=[("CTO", i) for i in range(NT)],
                           writes=[("LATSRC", 0)])
                    TR.dma("sp", lat_src[0:64, 4 * T:5 * T], KPO[0:64, :], "lat", reads=[("KPO", 0), ("KPO", 1)], writes=[("LATSRC", 1)])
                    TR.op("pool", lambda e: e.collective_compute("AllGather", ALU.bypass, replica_groups=PAIRS, ins=[lat_src], outs=[lat_all]),
                          reads=[("LATSRC", 0), ("LATSRC", 1)], writes=[("LATALL",)])
                else:
                    TR.dma("sp", ct_own_d[:, :], CTO[:].rearrange("p k t -> p (k t)"), "out", reads=[("CTO", i) for i in range(NT)])
                    TR.dma("sp", kpe_own_d[:, :], KPO[0:64, :], "out", reads=[("KPO", 0), ("KPO", 1)])
            steps.append(Step(fn2, wdkv_pe_d, 1))

        def mla_phase(j, l):
            CT = arena(0, 16384, BF16).rearrange("p (k t) -> p k t", k=4)
            KPE = arena(16384, 4096, BF16)
            QLT = arena(20480, 8192, BF16).rearrange("p (k t) -> p k t", k=4)
            QN = arena(28672, 2048, BF16)
            QP = arena(30720, 2048, BF16)
            KN = arena(32768, 4096, BF16)
            V2 = arena(36864, 8192, BF16).rearrange("p (t n) -> p t n", t=16)
            PBS = [arena(45056, 4096, BF16), GB[:, 1, 1024:2048].bitcast(BF16)]
            PTS = [arena(49152, 4096, BF16).rearrange("p (t n) -> p t n", t=16),
                   SC32[:].rearrange("p a n -> p (a n)").bitcast(BF16).rearrange("p (t n) -> p t n", t=16)]
            OBS = [arena(55296, 256, BF16), arena(55808, 256, BF16)]
            MASK = arena(55552, 256, BF16)
            PO = PFM[:, 1, :].bitcast(BF16)
            load_rope()

            def load_kv(_):
                if FUSED:
                    TR.dma("sp", CT[:, :, 0:T], lat_all[0:128, 0:4 * T].rearrange("p (k t) -> p k t", k=4), "kv",
                           reads=[("LATALL",)], writes=[("CT",)], extra=TR.barrier_keys())
                    TR.dma("sp", CT[:, :, T:2 * T], lat_src[:, 0:4 * T].rearrange("p (k t) -> p k t", k=4), "kv",
                           reads=[("LATSRC", 0), ("LATALL",)], writes=[("CT",)])
                    TR.dma("sp", KPE[0:64, 0:T], lat_all[0:64, 4 * T:5 * T], "kv", reads=[("LATALL",)], writes=[("KPE",)])
                    TR.dma("sp", KPE[0:64, T:2 * T], lat_src[0:64, 4 * T:5 * T], "kv", reads=[("LATSRC", 1), ("LATALL",)], writes=[("KPE",)])
                    TR.dma("sp", KPE[64:65, :], maskrow_d[:, :], "kv", writes=[("KPE",)])
                else:
                    TR.dma("sp", CT[:].rearrange("p k t -> p (k t)"), ct_all_d[:, :], "kv", writes=[("CT",)], extra=TR.barrier_keys())
                    TR.dma("sp", KPE[0:65, :], kpe_all_d[:, :], "kv", writes=[("KPE",)])
                TR.dma("sp", MASK, mask_d[:, :], "kv", writes=[("MASK",)])
                TR.op("dve", lambda e: e.memset(QP[64:65, :], 1.0), writes=[("QP1",)])
            steps.append(Step(load_kv))

            def fn_dq(wt):
                wv, wkeys = wt
                wv = wv.rearrange("p (k n) -> p k n", k=KD)
                for i in range(NT):
                    bank = i % 4
                    def f(e, i=i, bank=bank):
                        for k in range(KD):
                            ins = e.matmul(PACC[:, bank, :], lhsT=HT[:, k, i * 128:(i + 1) * 128], rhs=wv[:, k, :], start=(k == 0), stop=(k == KD - 1))
                        return ins
                    TR.op("pe", f, reads=[("HT", i)] + wkeys, writes=[("PACC", bank)])
                    rms_to_feat(i, bank, 100 + 4 * j, QLT, "QLT")
            steps.append(Step(fn_dq, wdq_d[j], 4))

            for g in range(8):
                state = {}

                def fn_kv(wt, g=g, state=state):
                    wall, wkeys = wt
                    state["wukv"] = (wall[:, 0:2048].rearrange("p (k n) -> p k n", k=4), wkeys)
                    wv = state["wukv"][0]
                    for t in range(16):
                        def f(e, t=t):
                            for kc in range(4):
                                ins = e.matmul(PFM[:, 0, 0:256].rearrange("p (h c) -> p h c", h=2), lhsT=CT[:, kc, t * 128:(t + 1) * 128],
                                               rhs=wv[:, kc, :].rearrange("p (h c) -> p h c", h=2)[:, :, 128:256], start=(kc == 0), stop=(kc == 3))
                            return ins
                        TR.op("pe", f, reads=[("CT",)] + wkeys, writes=[("PFM", 0)])
                        TR.op("act", lambda e, t=t: e.copy(out=V2[:, t, :], in_=PFM[:, 0, 0:256]), reads=[("PFM", 0)], writes=[("V2", t)])

                def fn_q(wt, g=g, state=state):
                    wall, wqkeys = wt
                    wq = wall[:, 2048:4096].rearrange("p (k n) -> p k n", k=4)
                    wkv, wkvkeys = state["wukv"]
                    for hh in range(2):
                        h = 2 * g + hh
                        for kb in range(4):
                            def f(e, kb=kb):
                                for kc in range(4):
                                    ins = e.matmul(PACC[:, kb, :], lhsT=wkv[:, kc, hh * 256: hh * 256 + 128], rhs=CT[:, kc, kb * 512:(kb + 1) * 512],
                                                   start=(kc == 0), stop=(kc == 3))
                                return ins
                            TR.op("pe", f, reads=[("CT",)] + wkvkeys, writes=[("PACC", kb)])
                            TR.op("act", lambda e, kb=kb: e.copy(out=KN[:, kb * 512:(kb + 1) * 512], in_=PACC[:, kb, :]),
                                  reads=[("PACC", kb)], writes=[("KN", kb)])
                        for th in range(2):
                            def f(e, th=th):
                                for kc in range(4):
                                    ins = e.matmul(PACC[:, th, :], lhsT=wq[:, kc, hh * 256: hh * 256 + 128], rhs=QLT[:, kc, th * 512:(th + 1) * 512],
                                                   start=(kc == 0), stop=(kc == 3))
                                return ins
                            TR.op("pe", f, reads=[("QLT", 4 * th + jj) for jj in range(4)] + wqkeys, writes=[("PACC", th)])
                            TR.op("act", lambda e, th=th: e.copy(out=QN[:, th * 512:(th + 1) * 512], in_=PACC[:, th, :]),
                                  reads=[("PACC", th)], writes=[("QN", th)])
                            for sw in range(2):
                                def f2(e, th=th, sw=sw):
                                    for kc in range(4):
                                        ins = e.matmul(PFM[0:64, sw, :], lhsT=wq[:, kc, hh * 256 + 128 + sw * 64: hh * 256 + 192 + sw * 64],
                                                       rhs=QLT[:, kc, th * 512:(th + 1) * 512], start=(kc == 0), stop=(kc == 3))
                                    return ins
                                TR.op("pe", f2, reads=[("QLT", 4 * th + jj) for jj in range(4)] + wqkeys, writes=[("PFM", sw)])
                            rope_combine(PFM[0:64, 0, :], PFM[0:64, 1, :], [("PFM", 0), ("PFM", 1)], th,
                                         QP[0:64, th * 512:(th + 1) * 512], [("QP", th)])
                        SFL = PACC[:].rearrange("p b n -> p (b n)")

                        def nkof(qi):
                            nk = 8 + qi + 1
                            return nk, (nk * 128 + 511) // 512

                        def S_op(qi):
                            nk, nkb = nkof(qi)
                            for kb in range(nkb):
                                w = min(512, nk * 128 - kb * 512)
                                diag = kb == nkb - 1
                                def fs(e, kb=kb, w=w, diag=diag):
                                    e.matmul(PACC[:, kb, 0:w], lhsT=QN[:, qi * 128:(qi + 1) * 128], rhs=KN[:, kb * 512: kb * 512 + w],
                                             start=True, stop=False, skip_group_check=True)
                                    ins = e.matmul(PACC[:, kb, 0:w], lhsT=QP[0:65, qi * 128:(qi + 1) * 128], rhs=KPE[0:65, kb * 512: kb * 512 + w],
                                                   start=False, stop=(not diag), skip_group_check=True)
                                    if diag:
                                        o = ((8 + qi) % 4) * 128
                                        ins = e.matmul(PACC[:, kb, o:o + 128], lhsT=IDB[:], rhs=MASK, start=False, stop=True, skip_group_check=True)
                                    return ins
                                TR.op("pe", fs, reads=[("QN", qi // 4), ("QP", qi // 4), ("QP1",), ("KPE",), ("MASK",), ("IDB",), ("KN", kb)],
                                      writes=[("PACC", kb)])

                        def max_op(qi):
                            nk, nkb = nkof(qi)
                            p = qi % 2
                            for kb in range(nkb):
                                w = min(512, nk * 128 - kb * 512)
                                TR.op("dve", lambda e, kb=kb, w=w: e.reduce_max(out=AMX[:, p, kb:kb + 1], in_=PACC[:, kb, 0:w], axis=AX.X),
                                      reads=[("PACC", kb)], writes=[("AMX", p, kb)])
                            TR.op("dve", lambda e: e.reduce_max(out=MV[:, p, 0:1], in_=AMX[:, p, 0:nkb], axis=AX.X),
                                  reads=[("AMX", p, kb) for kb in range(nkb)], writes=[("AM", p, 0)])
                            TR.op("dve", lambda e: e.tensor_scalar(out=MV[:, p, 1:2], in0=MV[:, p, 0:1], scalar1=-SCALE, scalar2=None, op0=ALU.mult),
                                  reads=[("AM", p, 0)], writes=[("AM", p, 1)])
                            TR.op("dve", lambda e: e.memset(AMX[:, p, 4:8], 0.0), writes=[("AMS", p, kb) for kb in range(4)])

                        def exp_op(qi):
                            nk, nkb = nkof(qi)
                            p = qi % 2
                            for kb in range(nkb):
                                w = min(512, nk * 128 - kb * 512)
                                TR.op("act", lambda e, kb=kb, w=w: e.activation(out=PBS[p][:, kb * 512: kb * 512 + w], in_=PACC[:, kb, 0:w], func=AF.Exp,
                                                                             bias=MV[:, p, 1:2], scale=SCALE, accum_out=AMX[:, p, 4 + kb:5 + kb]),
                                      reads=[("PACC", kb), ("AM", p, 1), ("AMS", p, kb)], writes=[("PB", p, kb), ("AMS", p, kb)])

                        def recip_op(qi):
                            nk, nkb = nkof(qi)
                            p = qi % 2
                            TR.op("dve", lambda e: e.reduce_sum(out=MV[:, p, 2:3], in_=AMX[:, p, 4:4 + nkb], axis=AX.X),
                                  reads=[("AMS", p, kb) for kb in range(nkb)], writes=[("AM", p, 2)])
                            TR.op("dve", lambda e: e.reciprocal(out=MV[:, p, 3:4], in_=MV[:, p, 2:3]), reads=[("AM", p, 2)], writes=[("AM", p, 3)])

                        def T_op(qi):
                            nk, nkb = nkof(qi)
                            p = qi % 2
                            def ft(e):
                                for t in range(nk):
                                    ins = e.transpose(PTR[:, t, :], PBS[p][:, t * 128:(t + 1) * 128], IDB[:])
                                return ins
                            TR.op("pe", ft, reads=[("PB", p, kb) for kb in range(4)] + [("IDB",), ("GB",)], writes=[("PTR",)])

                        def evacPT_op(qi):
                            nk, nkb = nkof(qi)
                            p = qi % 2
                            TR.op("act", lambda e: e.copy(out=PTS[p][:, 0:nk, :], in_=PTR[:, 0:nk, :]), reads=[("PTR",)], writes=[("PT", p)])

                        def PV_op(qi):
                            nk, nkb = nkof(qi)
                            p = qi % 2
                            def fo(e):
                                for t in range(nk):
                                    ins = e.matmul(PFM[:, 0, 0:128], lhsT=PTS[p][:, t, :], rhs=V2[:, t, hh * 128:(hh + 1) * 128], start=(t == 0), stop=(t == nk - 1))
                                return ins
                            TR.op("pe", fo, reads=[("PT", p)] + [("V2", t) for t in range(nk)], writes=[("PFM", 0)])

                        def Oscale_op(qi):
                            p = qi % 2
                            TR.op("dve", lambda e: e.tensor_scalar(out=OBS[p], in0=PFM[:, 0, 0:128], scalar1=MV[:, p, 3:4], scalar2=None, op0=ALU.mult),
                                  reads=[("PFM", 0), ("AM", p, 3)], writes=[("OB", p)])

                        def OT_op(qi):
                            p = qi % 2
                            TR.op("pe", lambda e: e.transpose(PO[:, 0:128], OBS[p], IDB[:]), reads=[("OB", p), ("IDB",)], writes=[("PFM", 1)])

                        def evacOT_op(qi):
                            TR.op("act", lambda e: e.copy(out=HT[:, h, qi * 128:(qi + 1) * 128], in_=PO[:, 0:128]),
                                  reads=[("PFM", 1)], writes=[("HT", qi)])

                        S_op(0); max_op(0); exp_op(0); recip_op(0)
                        for qi in range(NT):
                            if qi + 1 < NT:
                                S_op(qi + 1)
                            T_op(qi)
                            if qi + 1 < NT:
                                max_op(qi + 1)
                                exp_op(qi + 1)
                                recip_op(qi + 1)
                            evacPT_op(qi)
                            PV_op(qi)
                            Oscale_op(qi)
                            OT_op(qi)
                            evacOT_op(qi)
                steps.append(Step(lambda wt, fn_kv=fn_kv, fn_q=fn_q: (fn_kv(wt), fn_q(wt)), wqkv_d[j, g], 2))

            for b in range(4):
                def fn_o(wt, b=b):
                    wv, wkeys = wt
                    wv = wv.rearrange("p (k n) -> p k n", k=16)
                    proj_block(wv, b, lambda k, i: HT[:, k, i * 128:(i + 1) * 128], lambda i: [("HT", i)], first=True, wkeys=wkeys)
                steps.append(Step(fn_o, wo_d[j, b], 4))

        for li, l in enumerate(layers):
            last = (li == len(layers) - 1)
            if l < 2:
                conv_phase(l, conv_layers.index(l))
            else:
                mla_phase(l - 2, l)
            ln_phase(l, 0)
            mlp_phase(li, l)
            ln_phase(l, 1, final_store=(last and stage != "L1"))
            if FUSED and l == 0:
                def halo_export(_):
                    HS = sb("HS", [128, 16, 2], BF16)
                    TR.op("act", lambda e: e.copy(out=HS[:], in_=HT[:, :, T - 2:T]), reads=[("HT", NT - 1)], writes=[("HS",)])
                    TR.dma("sp", halo_src[:, :], HS[:].rearrange("p k t -> p (k t)"), "halo", reads=[("HS",)], writes=[("HALOSRC",)])
                steps.append(Step(halo_export))
            if FUSED and l == 1:
                kv_latent_phase()
            if stage == "L1":
                kv_latent_phase()

                def store_h(_):
                    for i in range(NT):
                        TR.dma("sp", out_d[i * 128:(i + 1) * 128, :], H32[:, i, :], "out", reads=[("H32", i, b) for b in range(4)])
                steps.append(Step(store_h))

            _LAYER_ENDS.append(len(steps))
        wsteps = [si for si, s in enumerate(steps) if s.w is not None]
        slots_of = {}
        pos = 0
        for si in wsteps:
            n = steps[si].nslots
            if pos + n > NSLOT:
                pos = 0
            slots_of[si] = list(range(pos, pos + n))
            pos += n
        owner = [None] * NSLOT
        nxt = 0
        wtile = {}

        def issue(si):
            s = steps[si]
            sl = slots_of[si]
            width = s.w.shape[-1]
            view = WS[:, sl[0] * SLOT: sl[0] * SLOT + width]
            keys = [("WS", k) for k in sl]
            TR.dma("pool", view, s.w, "ws%d" % sl[0], writes=keys)
            wtile[si] = (view, keys)
            for k in sl:
                owner[k] = si

        if max_steps is not None:
            steps = steps[:max_steps]
            wsteps = [si for si in wsteps if si < max_steps]

            def dbg_store(_):
                for i in range(NT):
                    TR.dma("sp", out_d[i * 128:(i + 1) * 128, :], H32[:, i, :], "out", reads=[("H32", i, b) for b in range(4)],
                           extra=TR.barrier_keys())
            steps.append(Step(dbg_store))
        for si, s in enumerate(steps):
            while nxt < len(wsteps):
                cand = wsteps[nxt]
                if all(owner[k] is None or owner[k] < si for k in slots_of[cand]) and cand - si < 12:
                    issue(cand)
                    nxt += 1
                else:
                    break
            if s.w is not None:
                assert si in wtile, "weight not issued before use"
                s.fn(wtile[si])
            else:
                s.fn(None)
        if "out" in TR.dsem:
            ent = TR.dsem["out"]
            nc.sync.wait_ge(ent[0], ent[1])
        stuck, prog = TR.check_deadlock()
        if stuck:
            raise RuntimeError("sync deadlock: %r %r" % (stuck, prog))
        nc._tr_stats = prog if False else None
    return nc


_BF = ml_dtypes.bfloat16


def _rope_tables(pos0):
    inv = (1.0 / (np.float32(10000.0) ** (np.arange(0, 64, 2, dtype=np.float32) / np.float32(64)))).astype(np.float32)
    ang = (np.arange(pos0, pos0 + T, dtype=np.float32)[:, None] * inv[None, :]).astype(np.float32)
    cos, sin = np.cos(ang).astype(np.float32), np.sin(ang).astype(np.float32)
    cos2 = np.concatenate([cos, cos], axis=1).T
    sin2 = np.concatenate([-sin, sin], axis=1).T
    return np.ascontiguousarray(np.concatenate([cos2, sin2], axis=1).astype(np.float32))


def _chunk_kn(w, n):
    K, N = w.shape
    return np.ascontiguousarray(w.reshape(K // 128, 128, N // n, n).transpose(2, 1, 0, 3)).reshape(N // n, 128, (K // 128) * n)


_PROGS = {}


def _prog(stage):
    if stage not in _PROGS:
        _PROGS[stage] = build_program(stage)
    return _PROGS[stage]


def kernel(x, ln_g, ln_b, conv_w_in, conv_w, conv_w_out, kv_w_dkv, kv_norm_g, kv_w_ukv,
           mla_w_dq, mla_q_norm_g, mla_w_uq, mla_w_o, mlp_w1, mlp_w2):
    f = lambda a: np.asarray(a, dtype=np.float32)
    x, ln_g, ln_b = f(x), f(ln_g), f(ln_b)
    conv_w_in, conv_w, conv_w_out = f(conv_w_in), f(conv_w), f(conv_w_out)
    kv_w_dkv, kv_norm_g, kv_w_ukv = f(kv_w_dkv), f(kv_norm_g), f(kv_w_ukv)
    mla_w_dq, mla_q_norm_g, mla_w_uq, mla_w_o = f(mla_w_dq), f(mla_q_norm_g), f(mla_w_uq), f(mla_w_o)
    mlp_w1, mlp_w2 = f(mlp_w1), f(mlp_w2)
    n = 8
    cores = list(range(n))
    ident = np.eye(128, dtype=np.float32).astype(_BF)
    qq, kk = np.meshgrid(np.arange(128), np.arange(128), indexing="ij")
    mask = np.where(kk <= qq, 0.0, NEG).astype(np.float32).astype(_BF)
    small = np.zeros((128, 128), np.float32)
    for l in range(2):
        small[:, l * 48:(l + 1) * 48] = conv_w[l].reshape(3, 16, 128).transpose(2, 1, 0).reshape(128, 48)
    small[:, 96:100] = kv_norm_g.reshape(4, 128).T
    for j in range(2):
        small[:, 100 + 4 * j:104 + 4 * j] = mla_q_norm_g[j].reshape(4, 128).T
    w1r = np.stack([_chunk_kn(mlp_w1[l], 128) for l in range(4)])
    w2r = np.stack([np.ascontiguousarray(mlp_w2[l].reshape(4, 16, 128, 4, 512).transpose(0, 3, 2, 1, 4)).reshape(4, 4, 128, 16 * 512)
                    for l in range(4)])
    win = np.stack([np.ascontiguousarray(conv_w_in[l].reshape(16, 128, 3, 16, 128).transpose(3, 2, 1, 0, 4)).reshape(16, 3, 128, 16 * 128)
                    for l in range(2)])
    wout = np.stack([_chunk_kn(conv_w_out[l], 512) for l in range(2)])
    wdkv_lat = _chunk_kn(kv_w_dkv[:, :512], 512)[0]
    pe = kv_w_dkv[:, 512:576]
    wdkv_pe = _chunk_kn(np.concatenate([pe, pe[:, 32:], pe[:, :32]], axis=1), 128)[0]
    wdq = np.stack([_chunk_kn(mla_w_dq[j], 512)[0] for j in range(2)])
    wuq = []
    for j in range(2):
        w = mla_w_uq[j].reshape(512, 16, 192)
        wa = np.concatenate([w, w[:, :, 160:192], w[:, :, 128:160]], axis=2)
        wuq.append(_chunk_kn(wa.reshape(512, 4096), 512))
    wuq = np.stack(wuq)
    wukv = _chunk_kn(kv_w_ukv, 512)
    wqkv = np.ascontiguousarray(np.concatenate([np.broadcast_to(wukv[None], wuq.shape), wuq], axis=-1))
    wo = np.stack([_chunk_kn(mla_w_o[j], 512) for j in range(2)])
    ropes = [_rope_tables(0), _rope_tables(T)]

    def halo_t(rows):
        return np.ascontiguousarray(rows.reshape(2, 16, 128).transpose(2, 1, 0)).reshape(128, 32)

    maps = []
    for c in cores:
        b, hf = c // 2, c % 2
        sm = small.copy()
        sm[:, 110] = float(hf)
        halo = np.zeros((2, D), np.float32) if hf == 0 else x[b, T - 2:T]
        mrow = np.zeros((1, 2 * T), np.float32)
        if hf == 0:
            mrow[0, :T] = NEG
        maps.append({"x": np.ascontiguousarray(x[b, hf * T:(hf + 1) * T]), "ident": ident, "ln_g": ln_g, "ln_b": ln_b,
                     "smallp": sm, "w1": w1r, "w2": w2r, "w_in": win, "w_out": wout, "halo_t": halo_t(halo),
                     "wdkv_lat": wdkv_lat, "wdkv_pe": wdkv_pe, "wdq": wdq, "wqkv": wqkv, "wo": wo,
                     "rope": ropes[hf], "mask": mask, "maskrow": mrow.astype(_BF)})
    res = run_bass_kernel_spmd(_prog("ALL"), maps, core_ids=cores).results
    out = np.empty((4, 2048, D), np.float32)
    for c in cores:
        out[c // 2, (c % 2) * T:(c % 2 + 1) * T] = np.asarray(res[c]["out"], dtype=np.float32)
    return out
```

```python
import contextlib
import numpy as np
import ml_dtypes
import concourse.bass as bass
import concourse.mybir as mybir
from concourse.bass_utils import run_bass_kernel_spmd

F32 = mybir.dt.float32
BF16 = mybir.dt.bfloat16
ALU = mybir.AluOpType
AF = mybir.ActivationFunctionType
AX = mybir.AxisListType

D = 2048
T = 1024
NT = 8
KD = 16
NH = 16
DEPTH = 4
ALPHA = float((2 * DEPTH) ** 0.25)
LN_EPS = 1e-5
RMS_EPS = 1e-6
SCALE = float(192 ** -0.5)
NEG = -30000.0
NSLOT = 8
SLOT = 2048
SEM_LIMIT = 30000
FILL_A = 5
FILL_B = 2
FILL_C = 3
PSUM_KEYS = ("PACC", "PTR", "PFM")


class Tok:
    __slots__ = ("sem", "name", "val", "eng")

    def __init__(self, sem, name, val, eng):
        self.sem, self.name, self.val, self.eng = sem, name, val, eng


class Tracker:
    def __init__(self, nc, es):
        self.nc, self.es = nc, es
        self.E = {"pe": nc.tensor, "act": nc.scalar, "dve": nc.vector, "pool": nc.gpsimd, "sp": nc.sync}
        self.sem, self.semname, self.cnt, self.gen = {}, {}, {}, {}
        for e in self.E:
            self.gen[e] = 0
            self._newsem(e)
        self.waited = {e: {} for e in self.E}
        self.last_w = {}
        self.readers = {}
        self.dsem = {}
        self.n_inst = {e: 0 for e in self.E}
        self.trace = {e: [] for e in self.E}

    def check_deadlock(self):
        pc = {e: 0 for e in self.E}
        sems = {}
        progress = True
        while progress:
            progress = False
            for e in self.E:
                tr = self.trace[e]
                while pc[e] < len(tr):
                    kind, name, val = tr[pc[e]]
                    if kind == "w":
                        if sems.get(name, 0) >= val:
                            pc[e] += 1; progress = True
                        else:
                            break
                    else:
                        sems[name] = sems.get(name, 0) + val
                        pc[e] += 1; progress = True
        stuck = {e: self.trace[e][pc[e]] for e in self.E if pc[e] < len(self.trace[e])}
        return stuck, {e: (pc[e], len(self.trace[e])) for e in self.E}

    def _newsem(self, e):
        nm = "s_%s_%d" % (e, self.gen[e])
        self.gen[e] += 1
        self.sem[e] = self.es.enter_context(self.nc.semaphore(nm))
        self.semname[e] = nm
        self.cnt[e] = 0

    def _wait(self, e, tok):
        w = self.waited[e]
        if w.get(tok.name, 0) >= tok.val:
            return
        self.E[e].wait_ge(tok.sem, tok.val)
        self.trace[e].append(("w", tok.name, tok.val))
        w[tok.name] = tok.val

    def _deps(self, e, reads, writes, extra):
        for k in reads:
            t = self.last_w.get(k)
            if t is not None:
                self._wait(e, t)
        for k in writes:
            t = self.last_w.get(k)
            if t is not None and not (t.eng == e and t.eng != "dma"):
                self._wait(e, t)
            for t in self.readers.get(k, {}).values():
                if not (t.eng == e and t.eng != "dma"):
                    self._wait(e, t)
        for t in extra:
            if t is not None:
                self._wait(e, t)

    def _record(self, tok, reads, writes):
        for k in reads:
            self.readers.setdefault(k, {})[tok.eng if tok.eng != "dma" else tok.name] = tok
        for k in writes:
            self.last_w[k] = tok
            self.readers[k] = {}

    def op(self, e, fn, reads=(), writes=(), extra=()):
        xr = [k for k in reads if k[0] in PSUM_KEYS]
        if xr:
            writes = list(writes) + xr
        self._deps(e, reads, writes, extra)
        ins = fn(self.E[e])
        if self.cnt[e] >= SEM_LIMIT:
            self._newsem(e)
        self.cnt[e] += 1
        ins.then_inc(self.sem[e], 1)
        self.trace[e].append(("i", self.semname[e], 1))
        tok = Tok(self.sem[e], self.semname[e], self.cnt[e], e)
        self._record(tok, reads, writes)
        return tok

    def dma(self, q, out, in_, semname, reads=(), writes=(), extra=()):
        self._deps(q, reads, writes, extra)
        if semname not in self.dsem:
            self.dsem[semname] = [self.es.enter_context(self.nc.semaphore("d_" + semname)), 0]
        ent = self.dsem[semname]
        ins = self.E[q].dma_start(out=out, in_=in_)
        ent[1] += 16
        ins.then_inc(ent[0], 16)
        self.trace[q].append(("i", "d_" + semname, 16))
        tok = Tok(ent[0], "d_" + semname, ent[1], "dma")
        self._record(tok, reads, writes)
        return tok

    def barrier_keys(self):
        return [Tok(self.sem[e], self.semname[e], self.cnt[e], e) for e in self.E if self.cnt[e] > 0]


_LAYER_ENDS = []


class Step:
    def __init__(self, fn, w=None, nslots=0):
        self.fn, self.w, self.nslots = fn, w, nslots


def build_program(stage, max_steps=None):
    layers = {"L0": [0], "L1": [1], "L23": [2, 3], "ALL": [0, 1, 2, 3]}[stage]
    FUSED = stage == "ALL"
    PAIRS = [[0, 1], [2, 3], [4, 5], [6, 7]]
    nc = bass.Bass("TRN2", target_bir_lowering=False)

    def din(name, shape, dt=F32):
        return nc.dram_tensor(name, list(shape), dt, kind="ExternalInput").ap()

    def dout(name, shape, dt=F32):
        return nc.dram_tensor(name, list(shape), dt, kind="ExternalOutput").ap()

    x_d = din("x", [T, D])
    ident_d = din("ident", [128, 128], BF16)
    lng_d = din("ln_g", [DEPTH, 2, D])
    lnb_d = din("ln_b", [DEPTH, 2, D])
    small_d = din("smallp", [128, 128])
    w1_d = din("w1", [len(layers), 64, 128, KD * 128])
    w2_d = din("w2", [len(layers), 4, 4, 128, KD * 512])
    out_d = dout("out", [T, D])
    conv_layers = [l for l in layers if l < 2]
    mla_layers = [l for l in layers if l >= 2]
    if conv_layers:
        win_d = din("w_in", [len(conv_layers), 16, 3, 128, KD * 128])
        wout_d = din("w_out", [len(conv_layers), 4, 128, KD * 512])
        halo_d = din("halo_t", [128, KD * 2])
    if stage in ("L1", "ALL"):
        wdkv_lat_d = din("wdkv_lat", [128, KD * 512])
        wdkv_pe_d = din("wdkv_pe", [128, KD * 128])
    if stage == "L1":
        rope_d = din("rope", [64, 2 * T])
        ct_own_d = dout("ct_own", [128, 4 * T], BF16)
        kpe_own_d = dout("kpe_own", [64, T], BF16)
    if mla_layers:
        wdq_d = din("wdq", [2, 128, KD * 512])
        wqkv_d = din("wqkv", [2, 8, 128, 2 * 4 * 512])
        wo_d = din("wo", [2, 4, 128, KD * 512])
        rope_d = din("rope", [64, 2 * T])
        mask_d = din("mask", [128, 128], BF16)
    if stage == "L23":
        ct_all_d = din("ct_all", [128, 4 * 2 * T], BF16)
        kpe_all_d = din("kpe_all", [65, 2 * T], BF16)
    if FUSED:
        maskrow_d = din("maskrow", [1, 2 * T], BF16)
        halo_src = nc.dram_tensor("halo_src", [128, 32], BF16, kind="Internal").ap()
        halo_all = nc.dram_tensor("halo_all", [256, 32], BF16, addr_space="Local", kind="Internal").ap()
        lat_src = nc.dram_tensor("lat_src", [128, 5 * T], BF16, kind="Internal").ap()
        lat_all = nc.dram_tensor("lat_all", [256, 5 * T], BF16, addr_space="Local", kind="Internal").ap()

    es = contextlib.ExitStack()
    with es:
        def sb(name, shape, dt):
            return es.enter_context(nc.sbuf_tensor(name, list(shape), dt))

        def ps(name, shape, dt):
            return es.enter_context(nc.psum_tensor(name, list(shape), dt))

        TR = Tracker(nc, es)
        H32 = sb("H32", [128, NT, D], F32)
        HT = sb("HT", [128, KD, T], BF16)
        GB = sb("GB", [128, 2, D], F32)
        WS = sb("WS", [128, NSLOT * SLOT], BF16)
        ARN = 57344 // 2
        AR = sb("AR", [128, ARN], BF16)
        IDB = sb("IDB", [128, 128], BF16)
        SMALL = sb("SMALL", [128, 128], F32)
        ST = sb("ST", [128, 8, 24], F32)
        MV = sb("MV", [128, 8, 8], F32)
        SC32 = sb("SC32", [128, 2, 512], F32)
        PACC = ps("PACC", [128, 4, 512], F32)
        PTR = ps("PTR", [128, KD, 128], BF16)
        PFM = ps("PFM", [128, 2, 512], F32)

        def arena(off_bytes, nbytes, dt):
            v = AR[:, off_bytes // 2:(off_bytes + nbytes) // 2]
            return v if dt == BF16 else v.bitcast(F32)

        steps = []

        def init_fn(_):
            TR.dma("sp", IDB[:], ident_d[:, :], "c0", writes=[("IDB",)])
            TR.dma("sp", SMALL[:], small_d[:, :], "c0", writes=[("SMALL",)])
            TR.dma("sp", H32[:], x_d.rearrange("(i p) d -> p i d", p=128), "c1",
                   writes=[("H32", i, b) for i in range(NT) for b in range(4)])
            HB = arena(49152, 8192, BF16).rearrange("p (a n) -> p a n", a=2)
            for i in range(NT):
                hb = HB[:, i % 2, :]
                TR.op("act", lambda e, i=i, hb=hb: e.copy(out=hb, in_=H32[:, i, :]),
                      reads=[("H32", i, b) for b in range(4)], writes=[("HB", i % 2)])
                to_feat(i, hb, ("HB", i % 2))
        steps.append(Step(init_fn))

        def to_feat(i, hb, hbkey):
            def f(e):
                for k in range(KD):
                    ins = e.transpose(PTR[:, k, :], hb[:, k * 128:(k + 1) * 128], IDB[:])
                return ins
            TR.op("pe", f, reads=[hbkey, ("IDB",)], writes=[("PTR",)])
            TR.op("act", lambda e: e.copy(out=HT[:, :, i * 128:(i + 1) * 128], in_=PTR[:]),
                  reads=[("PTR",)], writes=[("HT", i)])

        def ln_phase(l, s, final_store=False):
            def load_fn(_):
                TR.dma("sp", GB[:, 0, :], lng_d[l, s].partition_broadcast(128), "gb", writes=[("GB",)])
                TR.dma("sp", GB[:, 1, :], lnb_d[l, s].partition_broadcast(128), "gb", writes=[("GB",)])
            steps.append(Step(load_fn))

            def fn(_):
                HB = arena(49152, 8192, BF16).rearrange("p (a n) -> p a n", a=2)
                hk = lambda i: [("H32", i, b) for b in range(4)]
                R = range(NT)
                for i in R:
                    def stats(e, i=i):
                        for c in range(4):
                            ins = e.bn_stats(out=ST[:, i, c * 6:(c + 1) * 6], in_=H32[:, i, c * 512:(c + 1) * 512])
                        return ins
                    TR.op("dve", stats, reads=hk(i), writes=[("ST", i)])
                for i in R:
                    TR.op("dve", lambda e, p=i: e.bn_aggr(out=MV[:, p, 0:2], in_=ST[:, p, :]),
                          reads=[("ST", i)], writes=[("MV", i, 0)])
                for i in R:
                    TR.op("dve", lambda e, p=i: e.tensor_scalar(out=MV[:, p, 2:3], in0=MV[:, p, 1:2], scalar1=LN_EPS,
                                                               scalar2=None, op0=ALU.add),
                          reads=[("MV", i, 0)], writes=[("MV", i, 1)])
                for i in R:
                    TR.op("act", lambda e, p=i: e.activation(out=MV[:, p, 3:4], in_=MV[:, p, 2:3], func=AF.Sqrt),
                          reads=[("MV", i, 1)], writes=[("MV", i, 2)])
                for i in R:
                    TR.op("dve", lambda e, p=i: e.reciprocal(out=MV[:, p, 4:5], in_=MV[:, p, 3:4]),
                          reads=[("MV", i, 2)], writes=[("MV", i, 3)])
                for i in R:
                    TR.op("dve", lambda e, p=i: e.scalar_tensor_tensor(out=MV[:, p, 5:6], in0=MV[:, p, 0:1], scalar=-1.0,
                                                                      in1=MV[:, p, 4:5], op0=ALU.mult, op1=ALU.mult),
                          reads=[("MV", i, 0), ("MV", i, 3)], writes=[("MV", i, 4)])
                for i in R:
                    TR.op("act", lambda e, i=i: e.activation(out=H32[:, i, :], in_=H32[:, i, :], func=AF.Identity,
                                                            bias=MV[:, i, 5:6], scale=MV[:, i, 4:5]),
                          reads=hk(i) + [("MV", i, 3), ("MV", i, 4)], writes=hk(i))
                for i in R:
                    TR.op("dve", lambda e, i=i: e.tensor_tensor(out=H32[:, i, :], in0=H32[:, i, :], in1=GB[:, 0, :], op=ALU.mult),
                          reads=hk(i) + [("GB",)], writes=hk(i))
                    TR.op("pool", lambda e, i=i: e.tensor_tensor(out=H32[:, i, :], in0=H32[:, i, :], in1=GB[:, 1, :], op=ALU.add),
                          reads=hk(i) + [("GB",)], writes=hk(i))
                    if final_store:
                        TR.dma("sp", out_d[i * 128:(i + 1) * 128, :], H32[:, i, :], "out", reads=hk(i))
                for i in R:
                    if not final_store:
                        p = i % 2
                        hb = HB[:, p, :]
                        TR.op("act", lambda e, i=i, hb=hb: e.copy(out=hb, in_=H32[:, i, :]),
                              reads=hk(i), writes=[("HB", p)])
                        to_feat(i, hb, ("HB", p))
            steps.append(Step(fn))

        def proj_block(wv, b, lhs_fn, lhs_keys_fn, first, wkeys, nk=16):
            for i in range(NT):
                bank = (b * NT + i) % 4
                def f(e, i=i, bank=bank):
                    for k in range(nk):
                        ins = e.matmul(PACC[:, bank, :], lhsT=lhs_fn(k, i), rhs=wv[:, k, :], start=(k == 0), stop=(k == nk - 1))
                    return ins
                TR.op("pe", f, reads=lhs_keys_fn(i) + wkeys, writes=[("PACC", bank)])
                dst = H32[:, i, b * 512:(b + 1) * 512]
                if first:
                    TR.op("dve", lambda e, dst=dst, bank=bank: e.scalar_tensor_tensor(
                        out=dst, in0=dst, scalar=ALPHA, in1=PACC[:, bank, :], op0=ALU.mult, op1=ALU.add),
                        reads=[("H32", i, b), ("PACC", bank)], writes=[("H32", i, b)])
                else:
                    TR.op("dve", lambda e, dst=dst, bank=bank: e.tensor_tensor(out=dst, in0=dst, in1=PACC[:, bank, :], op=ALU.add),
                          reads=[("H32", i, b), ("PACC", bank)], writes=[("H32", i, b)])

        def mlp_phase(li, l):
            AT = arena(0, 32768, BF16).rearrange("p (c t) -> p c t", c=16)
            RL = arena(32768, 4096, F32).rearrange("p (a n) -> p a n", a=2)
            for q in range(4):
                for c in range(16):
                    fch = q * 16 + c
                    def fn(wt, c=c):
                        wv, wkeys = wt
                        wv = wv.rearrange("p (k n) -> p k n", k=KD)
                        for th in range(2):
                            def f(e, th=th):
                                for k in range(KD):
                                    ins = e.matmul(PFM[:, th, :], lhsT=wv[:, k, :], rhs=HT[:, k, th * 512:(th + 1) * 512],
                                                   start=(k == 0), stop=(k == KD - 1))
                                return ins
                            TR.op("pe", f, reads=[("HT", 4 * th + j) for j in range(4)] + wkeys, writes=[("PFM", th)])
                            TR.op("act", lambda e, th=th: e.activation(out=RL[:, th, :], in_=PFM[:, th, :], func=AF.Relu),
                                  reads=[("PFM", th)], writes=[("RL", th)])
                            TR.op("dve", lambda e, th=th, c=c: e.tensor_tensor(out=AT[:, c, th * 512:(th + 1) * 512], in0=RL[:, th, :],
                                                                              in1=RL[:, th, :], op=ALU.mult),
                                  reads=[("RL", th)], writes=[("AT", c, th)])
                    steps.append(Step(fn, w1_d[li, fch], 1))
                for b in range(4):
                    def fn2(wt, q=q, b=b):
                        wv, wkeys = wt
                        wv = wv.rearrange("p (k n) -> p k n", k=16)
                        proj_block(wv, b, lambda k, i: AT[:, k, i * 128:(i + 1) * 128],
                                   lambda i: [("AT", k, i // 4) for k in range(16)], first=(q == 0), wkeys=wkeys)
                    steps.append(Step(fn2, w2_d[li, q, b], 4))

        def conv_phase(l, ci):
            ZT = arena(0, 32768, BF16).rearrange("p (c t) -> p c t", c=16)
            TC = arena(32768, 4128, F32)
            V = arena(36896, 4128, F32)
            Y = arena(41024, 4096, F32)
            HALO = sb("HALO%d" % l, [128, KD, 2], BF16)
            PH = PTR[:].rearrange("p k n -> p (k n)").bitcast(F32)

            def load_halo(_):
                if FUSED and l == 1:
                    HL = sb("HL", [128, 32], BF16)
                    TR.op("pool", lambda e: e.collective_compute("AllGather", ALU.bypass, replica_groups=PAIRS, ins=[halo_src], outs=[halo_all]),
                          reads=[("HALOSRC",)], writes=[("HALOALL",)])
                    TR.dma("sp", HL[:], halo_all[0:128, :], "halo", reads=[("HALOALL",)], writes=[("HL",)])
                    TR.op("dve", lambda e: e.tensor_scalar(out=HALO[:].rearrange("p k t -> p (k t)"), in0=HL[:], scalar1=SMALL[:, 110:111],
                                                           scalar2=None, op0=ALU.mult),
                          reads=[("HL",), ("SMALL",)], writes=[("HALO",)])
                else:
                    TR.dma("pool", HALO[:].rearrange("p k t -> p (k t)"), halo_d[:, :], "halo", writes=[("HALO",)])
            steps.append(Step(load_halo))
            cw = lambda c, j: SMALL[:, l * 48 + c * 3 + j: l * 48 + c * 3 + j + 1]
            for c in range(16):
                def fn_part(wt, c=c, part=0):
                    wv, wkeys = wt
                    wv = wv.rearrange("p (k n) -> p k n", k=KD)
                    banks = {1: (PACC, 0), 2: (PACC, 2), 0: (PFM, 0)}[part]
                    pt, b0 = banks
                    pkey = "PACC" if pt is PACC else "PFM"
                    if part in (1, 2):
                        col = 0 if part == 1 else 2
                        def fh(e):
                            for k in range(KD):
                                ins = e.matmul(PH[:, col:col + 2], lhsT=wv[:, k, :], rhs=HALO[:, k, :], start=(k == 0), stop=(k == KD - 1))
                            return ins
                        TR.op("pe", fh, reads=[("HALO",)] + wkeys, writes=[("PTR",)])
                    for th in range(2):
                        def f(e, th=th):
                            for k in range(KD):
                                ins = e.matmul(pt[:, b0 + th, :], lhsT=wv[:, k, :], rhs=HT[:, k, th * 512:(th + 1) * 512],
                                               start=(k == 0), stop=(k == KD - 1))
                            return ins
                        TR.op("pe", f, reads=[("HT", 4 * th + j) for j in range(4)] + wkeys, writes=[(pkey, b0 + th)])
                    if part == 1:
                        for th in range(2):
                            TR.op("act", lambda e, th=th: e.copy(out=TC[:, th * 512:(th + 1) * 512], in_=PACC[:, th, :]),
                                  reads=[("PACC", th)], writes=[("TC", th)])
                        TR.op("act", lambda e: e.copy(out=TC[:, 1024:1026], in_=PH[:, 0:2]), reads=[("PTR",)], writes=[("TC", 2)])
                    elif part == 2:
                        for th in range(2):
                            TR.op("dve", lambda e, th=th: e.tensor_tensor(out=V[:, 2 + th * 512: 2 + (th + 1) * 512], in0=TC[:, th * 512:(th + 1) * 512],
                                                                         in1=PACC[:, 2 + th, :], op=ALU.mult),
                                  reads=[("TC", th), ("PACC", 2 + th)], writes=[("V", th)])
                        TR.op("dve", lambda e: e.tensor_tensor(out=V[:, 0:2], in0=TC[:, 1024:1026], in1=PH[:, 2:4], op=ALU.mult),
                              reads=[("TC", 2), ("PTR",)], writes=[("V", 2)])
                        vk = [("V", 0), ("V", 1), ("V", 2)]
                        TR.op("dve", lambda e, c=c: e.tensor_scalar(out=Y[:, 0:1024], in0=V[:, 2:1026], scalar1=cw(c, 2), scalar2=None, op0=ALU.mult),
                              reads=vk + [("SMALL",)], writes=[("Y",)])
                        TR.op("dve", lambda e, c=c: e.scalar_tensor_tensor(out=Y[:, 0:1024], in0=V[:, 1:1025], scalar=cw(c, 1), in1=Y[:, 0:1024],
                                                                          op0=ALU.mult, op1=ALU.add),
                              reads=vk + [("Y",), ("SMALL",)], writes=[("Y",)])
                        TR.op("dve", lambda e, c=c: e.scalar_tensor_tensor(out=Y[:, 0:1024], in0=V[:, 0:1024], scalar=cw(c, 0), in1=Y[:, 0:1024],
                                                                          op0=ALU.mult, op1=ALU.add),
                              reads=vk + [("Y",), ("SMALL",)], writes=[("Y",)])
                    else:
                        for th in range(2):
                            TR.op("dve", lambda e, th=th, c=c: e.tensor_tensor(out=ZT[:, c, th * 512:(th + 1) * 512], in0=Y[:, th * 512:(th + 1) * 512],
                                                                              in1=PFM[:, th, :], op=ALU.mult),
                                  reads=[("Y",), ("PFM", th)], writes=[("ZT", c, th)])
                for part in (1, 2, 0):
                    steps.append(Step(lambda wt, c=c, part=part, fn_part=fn_part: fn_part(wt, c, part), win_d[ci, c, part], 1))
            for b in range(4):
                def fn2(wt, b=b):
                    wv, wkeys = wt
                    wv = wv.rearrange("p (k n) -> p k n", k=16)
                    proj_block(wv, b, lambda k, i: ZT[:, k, i * 128:(i + 1) * 128],
                               lambda i: [("ZT", k, i // 4) for k in range(16)], first=True, wkeys=wkeys)
                steps.append(Step(fn2, wout_d[ci, b], 4))

        def rms_to_feat(i, bank, gcol0, dstT, dkey):
            p = i % 2
            SQ = SC32[:, p, :]
            NB = arena(53248 + p * 1024, 1024, BF16)
            TR.op("dve", lambda e: e.memset(MV[:, p, 6:7], 0.0), writes=[("MV", p, 6)])
            TR.op("act", lambda e: e.activation(out=SQ, in_=PACC[:, bank, :], func=AF.Square, accum_out=MV[:, p, 6:7]),
                  reads=[("PACC", bank), ("MV", p, 6)], writes=[("SQ", p), ("MV", p, 6)])
            TR.op("dve", lambda e: e.tensor_scalar(out=MV[:, p, 7:8], in0=MV[:, p, 6:7], scalar1=1.0 / 512, scalar2=RMS_EPS,
                                                   op0=ALU.mult, op1=ALU.add),
                  reads=[("MV", p, 6)], writes=[("MV", p, 7)])
            TR.op("act", lambda e: e.activation(out=MV[:, p, 6:7], in_=MV[:, p, 7:8], func=AF.Sqrt),
                  reads=[("MV", p, 7)], writes=[("MV", p, 6)])
            TR.op("dve", lambda e: e.reciprocal(out=MV[:, p, 7:8], in_=MV[:, p, 6:7]),
                  reads=[("MV", p, 6)], writes=[("MV", p, 7)])
            TR.op("dve", lambda e: e.tensor_scalar(out=NB, in0=PACC[:, bank, :], scalar1=MV[:, p, 7:8], scalar2=None, op0=ALU.mult),
                  reads=[("PACC", bank), ("MV", p, 7)], writes=[("NB", p)])
            def f(e):
                for kc in range(4):
                    ins = e.transpose(PTR[:, kc, :], NB[:, kc * 128:(kc + 1) * 128], IDB[:])
                return ins
            TR.op("pe", f, reads=[("NB", p), ("IDB",)], writes=[("PTR",)])
            for kc in range(4):
                TR.op("dve", lambda e, kc=kc: e.tensor_scalar(out=dstT[:, kc, i * 128:(i + 1) * 128], in0=PTR[:, kc, :],
                                                             scalar1=SMALL[:, gcol0 + kc: gcol0 + kc + 1], scalar2=None, op0=ALU.mult),
                      reads=[("PTR",), ("SMALL",)], writes=[(dkey, i)])

        def load_rope():
            def fn(_):
                TR.dma("sp", GB[0:64, 0, :], rope_d[:, :], "gb", writes=[("GB",)])
            steps.append(Step(fn))

        ROPE = GB[0:64, 0, :]
        RT = GB[0:64, 1, :]

        def rope_combine(pa, pb, keys_in, th, dst, dkeys):
            TR.op("dve", lambda e: e.tensor_tensor(out=RT[:, 0:512], in0=pa, in1=ROPE[:, th * 512:(th + 1) * 512], op=ALU.mult),
                  reads=keys_in + [("GB",)], writes=[("RT", 0)])
            TR.op("dve", lambda e: e.tensor_tensor(out=RT[:, 512:1024], in0=pb, in1=ROPE[:, T + th * 512:T + (th + 1) * 512], op=ALU.mult),
                  reads=keys_in + [("GB",)], writes=[("RT", 1)])
            TR.op("dve", lambda e: e.tensor_tensor(out=dst, in0=RT[:, 0:512], in1=RT[:, 512:1024], op=ALU.add),
                  reads=[("RT", 0), ("RT", 1), ("GB",)], writes=dkeys)

        def kv_latent_phase():
            CTO = arena(0, 8192, BF16).rearrange("p (k t) -> p k t", k=4)
            KPO = arena(8192, 2048, BF16)
            load_rope()

            def fn(wt):
                wv, wkeys = wt
                wv = wv.rearrange("p (k n) -> p k n", k=KD)
                for i in range(NT):
                    bank = i % 4
                    def f(e, i=i, bank=bank):
                        for k in range(KD):
                            ins = e.matmul(PACC[:, bank, :], lhsT=HT[:, k, i * 128:(i + 1) * 128], rhs=wv[:, k, :], start=(k == 0), stop=(k == KD - 1))
                        return ins
                    TR.op("pe", f, reads=[("HT", i)] + wkeys, writes=[("PACC", bank)])
                    rms_to_feat(i, bank, 96, CTO, "CTO")
            steps.append(Step(fn, wdkv_lat_d, 4))

            def fn2(wt):
                wv, wkeys = wt
                wv = wv.rearrange("p (k n) -> p k n", k=KD)
                for th in range(2):
                    for sw in range(2):
                        def f(e, th=th, sw=sw):
                            for k in range(KD):
                                ins = e.matmul(PFM[0:64, sw, :], lhsT=wv[:, k, sw * 64:(sw + 1) * 64], rhs=HT[:, k, th * 512:(th + 1) * 512],
                                               start=(k == 0), stop=(k == KD - 1))
                            return ins
                        TR.op("pe", f, reads=[("HT", 4 * th + j) for j in range(4)] + wkeys, writes=[("PFM", sw)])
                    rope_combine(PFM[0:64, 0, :], PFM[0:64, 1, :], [("PFM", 0), ("PFM", 1)], th,
                                 KPO[0:64, th * 512:(th + 1) * 512], [("KPO", th)])
                if FUSED:
                    TR.dma("sp", lat_src[:, 0:4 * T], CTO[:].rearrange("p k t -> p (k t)"), "lat", reads=[("CTO", i) for i in range(NT)],
                           writes=[("LATSRC", 0)])
                    TR.dma("sp", lat_src[0:64, 4 * T:5 * T], KPO[0:64, :], "lat", reads=[("KPO", 0), ("KPO", 1)], writes=[("LATSRC", 1)])
                    TR.op("pool", lambda e: e.collective_compute("AllGather", ALU.bypass, replica_groups=PAIRS, ins=[lat_src], outs=[lat_all]),
                          reads=[("LATSRC", 0), ("LATSRC", 1)], writes=[("LATALL",)])
                else:
                    TR.dma("sp", ct_own_d[:, :], CTO[:].rearrange("p k t -> p (k t)"), "out", reads=[("CTO", i) for i in range(NT)])
                    TR.dma("sp", kpe_own_d[:, :], KPO[0:64, :], "out", reads=[("KPO", 0), ("KPO", 1)])
            steps.append(Step(fn2, wdkv_pe_d, 1))

        def mla_phase(j, l):
            CT = arena(0, 16384, BF16).rearrange("p (k t) -> p k t", k=4)
            KPE = arena(16384, 4096, BF16)
            QLT = arena(20480, 8192, BF16).rearrange("p (k t) -> p k t", k=4)
            QN = arena(28672, 2048, BF16)
            QP = arena(30720, 2048, BF16)
            KN = arena(32768, 4096, BF16)
            V2 = arena(36864, 8192, BF16).rearrange("p (t n) -> p t n", t=16)
            PBS = [arena(45056, 4096, BF16), GB[:, 1, 1024:2048].bitcast(BF16)]
            PTS = [arena(49152, 4096, BF16).rearrange("p (t n) -> p t n", t=16),
                   SC32[:].rearrange("p a n -> p (a n)").bitcast(BF16).rearrange("p (t n) -> p t n", t=16)]
            OBS = [arena(55296, 256, BF16), arena(55808, 256, BF16)]
            MASK = arena(55552, 256, BF16)
            PO = PFM[:, 0, 256:512].bitcast(BF16)
            load_rope()

            def load_kv(_):
                if FUSED:
                    TR.dma("sp", CT[:, :, 0:T], lat_all[0:128, 0:4 * T].rearrange("p (k t) -> p k t", k=4), "kv",
                           reads=[("LATALL",)], writes=[("CT",)], extra=TR.barrier_keys())
                    TR.dma("sp", CT[:, :, T:2 * T], lat_src[:, 0:4 * T].rearrange("p (k t) -> p k t", k=4), "kv",
                           reads=[("LATSRC", 0), ("LATALL",)], writes=[("CT",)])
                    TR.dma("sp", KPE[0:64, 0:T], lat_all[0:64, 4 * T:5 * T], "kv", reads=[("LATALL",)], writes=[("KPE",)])
                    TR.dma("sp", KPE[0:64, T:2 * T], lat_src[0:64, 4 * T:5 * T], "kv", reads=[("LATSRC", 1), ("LATALL",)], writes=[("KPE",)])
                    TR.dma("sp", KPE[64:65, :], maskrow_d[:, :], "kv", writes=[("KPE",)])
                else:
                    TR.dma("sp", CT[:].rearrange("p k t -> p (k t)"), ct_all_d[:, :], "kv", writes=[("CT",)], extra=TR.barrier_keys())
                    TR.dma("sp", KPE[0:65, :], kpe_all_d[:, :], "kv", writes=[("KPE",)])
                TR.dma("sp", MASK, mask_d[:, :], "kv", writes=[("MASK",)])
                TR.op("dve", lambda e: e.memset(QP[64:65, :], 1.0), writes=[("QP1",)])
            steps.append(Step(load_kv))

            def fn_dq(wt):
                wv, wkeys = wt
                wv = wv.rearrange("p (k n) -> p k n", k=KD)
                for i in range(NT):
                    bank = i % 4
                    def f(e, i=i, bank=bank):
                        for k in range(KD):
                            ins = e.matmul(PACC[:, bank, :], lhsT=HT[:, k, i * 128:(i + 1) * 128], rhs=wv[:, k, :], start=(k == 0), stop=(k == KD - 1))
                        return ins
                    TR.op("pe", f, reads=[("HT", i)] + wkeys, writes=[("PACC", bank)])
                    rms_to_feat(i, bank, 100 + 4 * j, QLT, "QLT")
            steps.append(Step(fn_dq, wdq_d[j], 4))

            for g in range(8):
                state = {}

                def fn_kv(wt, g=g, state=state):
                    wall, wkeys = wt
                    state["wukv"] = (wall[:, 0:2048].rearrange("p (k n) -> p k n", k=4), wkeys)
                    wv = state["wukv"][0]
                    for t in range(16):
                        def f(e, t=t):
                            for kc in range(4):
                                ins = e.matmul(PFM[:, t % 2, 0:256].rearrange("p (h c) -> p h c", h=2), lhsT=CT[:, kc, t * 128:(t + 1) * 128],
                                               rhs=wv[:, kc, :].rearrange("p (h c) -> p h c", h=2)[:, :, 128:256], start=(kc == 0), stop=(kc == 3))
                            return ins
                        TR.op("pe", f, reads=[("CT",)] + wkeys, writes=[("PFM", t % 2)])
                        if t % 2 == 0:
                            TR.op("act", lambda e, t=t: e.copy(out=V2[:, t, :], in_=PFM[:, 0, 0:256]), reads=[("PFM", 0)], writes=[("V2", t)])
                        else:
                            TR.op("dve", lambda e, t=t: e.tensor_copy(out=V2[:, t, :], in_=PFM[:, 1, 0:256]), reads=[("PFM", 1)], writes=[("V2", t)])

                def fn_q(wt, g=g, state=state):
                    wall, wqkeys = wt
                    wq = wall[:, 2048:4096].rearrange("p (k n) -> p k n", k=4)
                    wkv, wkvkeys = state["wukv"]
                    for hh in range(2):
                        h = 2 * g + hh
                        for kb in range(4):
                            def f(e, kb=kb):
                                for kc in range(4):
                                    ins = e.matmul(PACC[:, kb, :], lhsT=wkv[:, kc, hh * 256: hh * 256 + 128], rhs=CT[:, kc, kb * 512:(kb + 1) * 512],
                                                   start=(kc == 0), stop=(kc == 3))
                                return ins
                            TR.op("pe", f, reads=[("CT",)] + wkvkeys, writes=[("PACC", kb)])
                            TR.op("act", lambda e, kb=kb: e.copy(out=KN[:, kb * 512:(kb + 1) * 512], in_=PACC[:, kb, :]),
                                  reads=[("PACC", kb)], writes=[("KN", kb)])
                        for th in range(2):
                            def f(e, th=th):
                                for kc in range(4):
                                    ins = e.matmul(PACC[:, th, :], lhsT=wq[:, kc, hh * 256: hh * 256 + 128], rhs=QLT[:, kc, th * 512:(th + 1) * 512],
                                                   start=(kc == 0), stop=(kc == 3))
                                return ins
                            TR.op("pe", f, reads=[("QLT", 4 * th + jj) for jj in range(4)] + wqkeys, writes=[("PACC", th)])
                            TR.op("act", lambda e, th=th: e.copy(out=QN[:, th * 512:(th + 1) * 512], in_=PACC[:, th, :]),
                                  reads=[("PACC", th)], writes=[("QN", th)])
                            for sw in range(2):
                                def f2(e, th=th, sw=sw):
                                    for kc in range(4):
                                        ins = e.matmul(PFM[0:64, sw, :], lhsT=wq[:, kc, hh * 256 + 128 + sw * 64: hh * 256 + 192 + sw * 64],
                                                       rhs=QLT[:, kc, th * 512:(th + 1) * 512], start=(kc == 0), stop=(kc == 3))
                                    return ins
                                TR.op("pe", f2, reads=[("QLT", 4 * th + jj) for jj in range(4)] + wqkeys, writes=[("PFM", sw)])
                            rope_combine(PFM[0:64, 0, :], PFM[0:64, 1, :], [("PFM", 0), ("PFM", 1)], th,
                                         QP[0:64, th * 512:(th + 1) * 512], [("QP", th)])
                        SFL = PACC[:].rearrange("p b n -> p (b n)")

                        def nkof(qi):
                            nk = 8 + qi + 1
                            return nk, (nk * 128 + 511) // 512

                        def S_op(qi):
                            nk, nkb = nkof(qi)
                            def fs(e):
                                for kb in range(nkb):
                                    w = min(512, nk * 128 - kb * 512)
                                    e.matmul(PACC[:, kb, 0:w], lhsT=QN[:, qi * 128:(qi + 1) * 128], rhs=KN[:, kb * 512: kb * 512 + w],
                                             start=True, stop=False, skip_group_check=True)
                                for kb in range(nkb):
                                    w = min(512, nk * 128 - kb * 512)
                                    diag = (8 + qi) // 4 == kb
                                    ins = e.matmul(PACC[:, kb, 0:w], lhsT=QP[0:65, qi * 128:(qi + 1) * 128], rhs=KPE[0:65, kb * 512: kb * 512 + w],
                                                   start=False, stop=(not diag), skip_group_check=True)
                                kb = (8 + qi) // 4
                                o = ((8 + qi) % 4) * 128
                                ins = e.matmul(PACC[:, kb, o:o + 128], lhsT=IDB[:], rhs=MASK, start=False, stop=True, skip_group_check=True)
                                return ins
                            TR.op("pe", fs, reads=[("QN", qi // 4), ("QP", qi // 4), ("QP1",), ("KPE",), ("MASK",), ("IDB",)] + [("KN", kb) for kb in range(nkb)],
                                  writes=[("PACC", kb) for kb in range(nkb)])

                        def max_op(qi):
                            nk, nkb = nkof(qi)
                            p = qi % 2
                            pk = [("PACC", kb) for kb in range(nkb)]
                            TR.op("dve", lambda e: e.reduce_max(out=MV[:, p, 0:1], in_=SFL[:, 0:nk * 128], axis=AX.X),
                                  reads=pk, writes=[("AM", p, 0)])
                            TR.op("dve", lambda e: e.tensor_scalar(out=MV[:, p, 1:2], in0=MV[:, p, 0:1], scalar1=-SCALE, scalar2=None, op0=ALU.mult),
                                  reads=[("AM", p, 0)], writes=[("AM", p, 1)])
                            TR.op("dve", lambda e: e.memset(MV[:, p, 2:3], 0.0), writes=[("AM", p, 2)])

                        def exp_op(qi):
                            nk, nkb = nkof(qi)
                            p = qi % 2
                            pk = [("PACC", kb) for kb in range(nkb)]
                            TR.op("act", lambda e: e.activation(out=PBS[p][:, 0:nk * 128], in_=SFL[:, 0:nk * 128], func=AF.Exp,
                                                                bias=MV[:, p, 1:2], scale=SCALE, accum_out=MV[:, p, 2:3]),
                                  reads=pk + [("AM", p, 1), ("AM", p, 2)], writes=[("PB", p), ("AM", p, 2)])

                        def recip_op(qi):
                            p = qi % 2
                            TR.op("dve", lambda e: e.reciprocal(out=MV[:, p, 3:4], in_=MV[:, p, 2:3]), reads=[("AM", p, 2)], writes=[("AM", p, 3)])

                        def T_op(qi):
                            nk, nkb = nkof(qi)
                            p = qi % 2
                            def ft(e):
                                for t in range(nk):
                                    ins = e.transpose(PTR[:, t, :], PBS[p][:, t * 128:(t + 1) * 128], IDB[:])
                                return ins
                            TR.op("pe", ft, reads=[("PB", p), ("IDB",), ("GB",)], writes=[("PTR",)])

                        def evacPT_op(qi):
                            nk, nkb = nkof(qi)
                            p = qi % 2
                            TR.op("act", lambda e: e.copy(out=PTS[p][:, 0:nk, :], in_=PTR[:, 0:nk, :]), reads=[("PTR",)], writes=[("PT", p)])

                        def PV_op(qi):
                            nk, nkb = nkof(qi)
                            p = qi % 2
                            def fo(e):
                                for t in range(nk):
                                    ins = e.matmul(PFM[:, 0, 0:128], lhsT=PTS[p][:, t, :], rhs=V2[:, t, hh * 128:(hh + 1) * 128], start=(t == 0), stop=(t == nk - 1))
                                return ins
                            TR.op("pe", fo, reads=[("PT", p)] + [("V2", t) for t in range(nk)], writes=[("PFM", 0)])

                        def Oscale_op(qi):
                            p = qi % 2
                            TR.op("dve", lambda e: e.tensor_scalar(out=OBS[p], in0=PFM[:, 0, 0:128], scalar1=MV[:, p, 3:4], scalar2=None, op0=ALU.mult),
                                  reads=[("PFM", 0), ("AM", p, 3)], writes=[("OB", p)])

                        def OT_op(qi):
                            p = qi % 2
                            TR.op("pe", lambda e: e.transpose(PO[:, 0:128], OBS[p], IDB[:]), reads=[("OB", p), ("IDB",)], writes=[("PFM", 0)])

                        def evacOT_op(qi):
                            TR.op("act", lambda e: e.copy(out=HT[:, h, qi * 128:(qi + 1) * 128], in_=PO[:, 0:128]),
                                  reads=[("PFM", 0)], writes=[("HT", qi)])

                        def filler(n):
                            if n <= 0:
                                return
                            def ff(e):
                                for _ in range(n):
                                    ins = e.matmul(PFM[:, 1, :], lhsT=IDB[:], rhs=KN[:, 0:512], start=True, stop=True, skip_group_check=True)
                                return ins
                            TR.op("pe", ff, reads=[("IDB",)], writes=[("PFM", 1)])

                        S_op(0); max_op(0); exp_op(0); recip_op(0)
                        for qi in range(NT):
                            if qi + 1 < NT:
                                S_op(qi + 1)
                            T_op(qi)
                            filler(FILL_A)
                            if qi + 1 < NT:
                                max_op(qi + 1)
                            evacPT_op(qi)
                            if qi + 1 < NT:
                                exp_op(qi + 1)
                            PV_op(qi)
                            filler(FILL_B)
                            Oscale_op(qi)
                            OT_op(qi)
                            evacOT_op(qi)
                            if qi + 1 < NT:
                                recip_op(qi + 1)
                                filler(FILL_C + (8 + qi) // 8)
                steps.append(Step(lambda wt, fn_kv=fn_kv, fn_q=fn_q: (fn_kv(wt), fn_q(wt)), wqkv_d[j, g], 2))

            for b in range(4):
                def fn_o(wt, b=b):
                    wv, wkeys = wt
                    wv = wv.rearrange("p (k n) -> p k n", k=16)
                    proj_block(wv, b, lambda k, i: HT[:, k, i * 128:(i + 1) * 128], lambda i: [("HT", i)], first=True, wkeys=wkeys)
                steps.append(Step(fn_o, wo_d[j, b], 4))

        for li, l in enumerate(layers):
            last = (li == len(layers) - 1)
            if l < 2:
                conv_phase(l, conv_layers.index(l))
            else:
                mla_phase(l - 2, l)
            ln_phase(l, 0)
            mlp_phase(li, l)
            ln_phase(l, 1, final_store=(last and stage != "L1"))
            if FUSED and l == 0:
                def halo_export(_):
                    HS = sb("HS", [128, 16, 2], BF16)
                    TR.op("act", lambda e: e.copy(out=HS[:], in_=HT[:, :, T - 2:T]), reads=[("HT", NT - 1)], writes=[("HS",)])
                    TR.dma("sp", halo_src[:, :], HS[:].rearrange("p k t -> p (k t)"), "halo", reads=[("HS",)], writes=[("HALOSRC",)])
                steps.append(Step(halo_export))
            if FUSED and l == 1:
                kv_latent_phase()
            if stage == "L1":
                kv_latent_phase()

                def store_h(_):
                    for i in range(NT):
                        TR.dma("sp", out_d[i * 128:(i + 1) * 128, :], H32[:, i, :], "out", reads=[("H32", i, b) for b in range(4)])
                steps.append(Step(store_h))

            _LAYER_ENDS.append(len(steps))
        wsteps = [si for si, s in enumerate(steps) if s.w is not None]
        slots_of = {}
        pos = 0
        for si in wsteps:
            n = steps[si].nslots
            if pos + n > NSLOT:
                pos = 0
            slots_of[si] = list(range(pos, pos + n))
            pos += n
        owner = [None] * NSLOT
        nxt = 0
        wtile = {}

        def issue(si):
            s = steps[si]
            sl = slots_of[si]
            width = s.w.shape[-1]
            view = WS[:, sl[0] * SLOT: sl[0] * SLOT + width]
            keys = [("WS", k) for k in sl]
            TR.dma("pool", view, s.w, "ws%d" % sl[0], writes=keys)
            wtile[si] = (view, keys)
            for k in sl:
                owner[k] = si

        if max_steps is not None:
            steps = steps[:max_steps]
            wsteps = [si for si in wsteps if si < max_steps]

            def dbg_store(_):
                for i in range(NT):
                    TR.dma("sp", out_d[i * 128:(i + 1) * 128, :], H32[:, i, :], "out", reads=[("H32", i, b) for b in range(4)],
                           extra=TR.barrier_keys())
            steps.append(Step(dbg_store))
        for si, s in enumerate(steps):
            while nxt < len(wsteps):
                cand = wsteps[nxt]
                if all(owner[k] is None or owner[k] < si for k in slots_of[cand]) and cand - si < 12:
                    issue(cand)
                    nxt += 1
                else:
                    break
            if s.w is not None:
                assert si in wtile, "weight not issued before use"
                s.fn(wtile[si])
            else:
                s.fn(None)
        if "out" in TR.dsem:
            ent = TR.dsem["out"]
            nc.sync.wait_ge(ent[0], ent[1])
        stuck, prog = TR.check_deadlock()
        if stuck:
            raise RuntimeError("sync deadlock: %r %r" % (stuck, prog))
        nc._tr_stats = prog if False else None
    return nc


_BF = ml_dtypes.bfloat16


def _rope_tables(pos0):
    inv = (1.0 / (np.float32(10000.0) ** (np.arange(0, 64, 2, dtype=np.float32) / np.float32(64)))).astype(np.float32)
    ang = (np.arange(pos0, pos0 + T, dtype=np.float32)[:, None] * inv[None, :]).astype(np.float32)
    cos, sin = np.cos(ang).astype(np.float32), np.sin(ang).astype(np.float32)
    cos2 = np.concatenate([cos, cos], axis=1).T
    sin2 = np.concatenate([-sin, sin], axis=1).T
    return np.ascontiguousarray(np.concatenate([cos2, sin2], axis=1).astype(np.float32))


def _chunk_kn(w, n):
    K, N = w.shape
    return np.ascontiguousarray(w.reshape(K // 128, 128, N // n, n).transpose(2, 1, 0, 3)).reshape(N // n, 128, (K // 128) * n)


_PROGS = {}


def _prog(stage):
    if stage not in _PROGS:
        _PROGS[stage] = build_program(stage)
    return _PROGS[stage]


def kernel(x, ln_g, ln_b, conv_w_in, conv_w, conv_w_out, kv_w_dkv, kv_norm_g, kv_w_ukv,
           mla_w_dq, mla_q_norm_g, mla_w_uq, mla_w_o, mlp_w1, mlp_w2):
    f = lambda a: np.asarray(a, dtype=np.float32)
    x, ln_g, ln_b = f(x), f(ln_g), f(ln_b)
    conv_w_in, conv_w, conv_w_out = f(conv_w_in), f(conv_w), f(conv_w_out)
    kv_w_dkv, kv_norm_g, kv_w_ukv = f(kv_w_dkv), f(kv_norm_g), f(kv_w_ukv)
    mla_w_dq, mla_q_norm_g, mla_w_uq, mla_w_o = f(mla_w_dq), f(mla_q_norm_g), f(mla_w_uq), f(mla_w_o)
    mlp_w1, mlp_w2 = f(mlp_w1), f(mlp_w2)
    n = 8
    cores = list(range(n))
    ident = np.eye(128, dtype=np.float32).astype(_BF)
    qq, kk = np.meshgrid(np.arange(128), np.arange(128), indexing="ij")
    mask = np.where(kk <= qq, 0.0, NEG).astype(np.float32).astype(_BF)
    small = np.zeros((128, 128), np.float32)
    for l in range(2):
        small[:, l * 48:(l + 1) * 48] = conv_w[l].reshape(3, 16, 128).transpose(2, 1, 0).reshape(128, 48)
    small[:, 96:100] = kv_norm_g.reshape(4, 128).T
    for j in range(2):
        small[:, 100 + 4 * j:104 + 4 * j] = mla_q_norm_g[j].reshape(4, 128).T
    w1r = np.stack([_chunk_kn(mlp_w1[l], 128) for l in range(4)])
    w2r = np.stack([np.ascontiguousarray(mlp_w2[l].reshape(4, 16, 128, 4, 512).transpose(0, 3, 2, 1, 4)).reshape(4, 4, 128, 16 * 512)
                    for l in range(4)])
    win = np.stack([np.ascontiguousarray(conv_w_in[l].reshape(16, 128, 3, 16, 128).transpose(3, 2, 1, 0, 4)).reshape(16, 3, 128, 16 * 128)
                    for l in range(2)])
    wout = np.stack([_chunk_kn(conv_w_out[l], 512) for l in range(2)])
    wdkv_lat = _chunk_kn(kv_w_dkv[:, :512], 512)[0]
    pe = kv_w_dkv[:, 512:576]
    wdkv_pe = _chunk_kn(np.concatenate([pe, pe[:, 32:], pe[:, :32]], axis=1), 128)[0]
    wdq = np.stack([_chunk_kn(mla_w_dq[j], 512)[0] for j in range(2)])
    wuq = []
    for j in range(2):
        w = mla_w_uq[j].reshape(512, 16, 192)
        wa = np.concatenate([w, w[:, :, 160:192], w[:, :, 128:160]], axis=2)
        wuq.append(_chunk_kn(wa.reshape(512, 4096), 512))
    wuq = np.stack(wuq)
    wukv = _chunk_kn(kv_w_ukv, 512)
    wqkv = np.ascontiguousarray(np.concatenate([np.broadcast_to(wukv[None], wuq.shape), wuq], axis=-1))
    wo = np.stack([_chunk_kn(mla_w_o[j], 512) for j in range(2)])
    ropes = [_rope_tables(0), _rope_tables(T)]

    def halo_t(rows):
        return np.ascontiguousarray(rows.reshape(2, 16, 128).transpose(2, 1, 0)).reshape(128, 32)

    maps = []
    for c in cores:
        b, hf = c // 2, c % 2
        sm = small.copy()
        sm[:, 110] = float(hf)
        halo = np.zeros((2, D), np.float32) if hf == 0 else x[b, T - 2:T]
        mrow = np.zeros((1, 2 * T), np.float32)
        if hf == 0:
            mrow[0, :T] = NEG
        maps.append({"x": np.ascontiguousarray(x[b, hf * T:(hf + 1) * T]), "ident": ident, "ln_g": ln_g, "ln_b": ln_b,
                     "smallp": sm, "w1": w1r, "w2": w2r, "w_in": win, "w_out": wout, "halo_t": halo_t(halo),
                     "wdkv_lat": wdkv_lat, "wdkv_pe": wdkv_pe, "wdq": wdq, "wqkv": wqkv, "wo": wo,
                     "rope": ropes[hf], "mask": mask, "maskrow": mrow.astype(_BF)})
    res = run_bass_kernel_spmd(_prog("ALL"), maps, core_ids=cores).results
    out = np.empty((4, 2048, D), np.float32)
    for c in cores:
        out[c // 2, (c % 2) * T:(c % 2 + 1) * T] = np.asarray(res[c]["out"], dtype=np.float32)
    return out
```

```python
import contextlib
import numpy as np
import ml_dtypes
import concourse.bass as bass
import concourse.mybir as mybir
from concourse.bass_utils import run_bass_kernel_spmd

F32 = mybir.dt.float32
BF16 = mybir.dt.bfloat16
ALU = mybir.AluOpType
AF = mybir.ActivationFunctionType
AX = mybir.AxisListType

D = 2048
T = 1024
NT = 8
KD = 16
NH = 16
DEPTH = 4
ALPHA = float((2 * DEPTH) ** 0.25)
LN_EPS = 1e-5
RMS_EPS = 1e-6
SCALE = float(192 ** -0.5)
NEG = -30000.0
NSLOT = 8
SLOT = 2048
SEM_LIMIT = 30000
FILL_A = 5
FILL_B = 2
FILL_C = 3
PSUM_KEYS = ("PACC", "PTR", "PFM")


class Tok:
    __slots__ = ("sem", "name", "val", "eng")

    def __init__(self, sem, name, val, eng):
        self.sem, self.name, self.val, self.eng = sem, name, val, eng


class Tracker:
    def __init__(self, nc, es):
        self.nc, self.es = nc, es
        self.E = {"pe": nc.tensor, "act": nc.scalar, "dve": nc.vector, "pool": nc.gpsimd, "sp": nc.sync}
        self.sem, self.semname, self.cnt, self.gen = {}, {}, {}, {}
        for e in self.E:
            self.gen[e] = 0
            self._newsem(e)
        self.waited = {e: {} for e in self.E}
        self.last_w = {}
        self.readers = {}
        self.dsem = {}
        self.n_inst = {e: 0 for e in self.E}
        self.trace = {e: [] for e in self.E}

    def check_deadlock(self):
        pc = {e: 0 for e in self.E}
        sems = {}
        progress = True
        while progress:
            progress = False
            for e in self.E:
                tr = self.trace[e]
                while pc[e] < len(tr):
                    kind, name, val = tr[pc[e]]
                    if kind == "w":
                        if sems.get(name, 0) >= val:
                            pc[e] += 1; progress = True
                        else:
                            break
                    else:
                        sems[name] = sems.get(name, 0) + val
                        pc[e] += 1; progress = True
        stuck = {e: self.trace[e][pc[e]] for e in self.E if pc[e] < len(self.trace[e])}
        return stuck, {e: (pc[e], len(self.trace[e])) for e in self.E}

    def _newsem(self, e):
        nm = "s_%s_%d" % (e, self.gen[e])
        self.gen[e] += 1
        self.sem[e] = self.es.enter_context(self.nc.semaphore(nm))
        self.semname[e] = nm
        self.cnt[e] = 0

    def _wait(self, e, tok):
        w = self.waited[e]
        if w.get(tok.name, 0) >= tok.val:
            return
        self.E[e].wait_ge(tok.sem, tok.val)
        self.trace[e].append(("w", tok.name, tok.val))
        w[tok.name] = tok.val

    def _deps(self, e, reads, writes, extra):
        for k in reads:
            t = self.last_w.get(k)
            if t is not None:
                self._wait(e, t)
        for k in writes:
            t = self.last_w.get(k)
            if t is not None and not (t.eng == e and t.eng != "dma"):
                self._wait(e, t)
            for t in self.readers.get(k, {}).values():
                if not (t.eng == e and t.eng != "dma"):
                    self._wait(e, t)
        for t in extra:
            if t is not None:
                self._wait(e, t)

    def _record(self, tok, reads, writes):
        for k in reads:
            self.readers.setdefault(k, {})[tok.eng if tok.eng != "dma" else tok.name] = tok
        for k in writes:
            self.last_w[k] = tok
            self.readers[k] = {}

    def op(self, e, fn, reads=(), writes=(), extra=()):
        xr = [k for k in reads if k[0] in PSUM_KEYS]
        if xr:
            writes = list(writes) + xr
        self._deps(e, reads, writes, extra)
        ins = fn(self.E[e])
        if self.cnt[e] >= SEM_LIMIT:
            self._newsem(e)
        self.cnt[e] += 1
        ins.then_inc(self.sem[e], 1)
        self.trace[e].append(("i", self.semname[e], 1))
        tok = Tok(self.sem[e], self.semname[e], self.cnt[e], e)
        self._record(tok, reads, writes)
        return tok

    def dma(self, q, out, in_, semname, reads=(), writes=(), extra=()):
        self._deps(q, reads, writes, extra)
        if semname not in self.dsem:
            self.dsem[semname] = [self.es.enter_context(self.nc.semaphore("d_" + semname)), 0]
        ent = self.dsem[semname]
        ins = self.E[q].dma_start(out=out, in_=in_)
        ent[1] += 16
        ins.then_inc(ent[0], 16)
        self.trace[q].append(("i", "d_" + semname, 16))
        tok = Tok(ent[0], "d_" + semname, ent[1], "dma")
        self._record(tok, reads, writes)
        return tok

    def barrier_keys(self):
        return [Tok(self.sem[e], self.semname[e], self.cnt[e], e) for e in self.E if self.cnt[e] > 0]


_LAYER_ENDS = []


class Step:
    def __init__(self, fn, w=None, nslots=0):
        self.fn, self.w, self.nslots = fn, w, nslots


def build_program(stage, max_steps=None):
    layers = {"L0": [0], "L1": [1], "L23": [2, 3], "ALL": [0, 1, 2, 3]}[stage]
    FUSED = stage == "ALL"
    PAIRS = [[0, 1], [2, 3], [4, 5], [6, 7]]
    nc = bass.Bass("TRN2", target_bir_lowering=False)

    def din(name, shape, dt=F32):
        return nc.dram_tensor(name, list(shape), dt, kind="ExternalInput").ap()

    def dout(name, shape, dt=F32):
        return nc.dram_tensor(name, list(shape), dt, kind="ExternalOutput").ap()

    x_d = din("x", [T, D])
    ident_d = din("ident", [128, 128], BF16)
    lng_d = din("ln_g", [DEPTH, 2, D])
    lnb_d = din("ln_b", [DEPTH, 2, D])
    small_d = din("smallp", [128, 128])
    w1_d = din("w1", [len(layers), 64, 128, KD * 128])
    w2_d = din("w2", [len(layers), 4, 4, 128, KD * 512])
    out_d = dout("out", [T, D])
    conv_layers = [l for l in layers if l < 2]
    mla_layers = [l for l in layers if l >= 2]
    if conv_layers:
        win_d = din("w_in", [len(conv_layers), 16, 3, 128, KD * 128])
        wout_d = din("w_out", [len(conv_layers), 4, 128, KD * 512])
        halo_d = din("halo_t", [128, KD * 2])
    if stage in ("L1", "ALL"):
        wdkv_lat_d = din("wdkv_lat", [128, KD * 512])
        wdkv_pe_d = din("wdkv_pe", [128, KD * 128])
    if stage == "L1":
        rope_d = din("rope", [64, 2 * T])
        ct_own_d = dout("ct_own", [128, 4 * T], BF16)
        kpe_own_d = dout("kpe_own", [64, T], BF16)
    if mla_layers:
        wdq_d = din("wdq", [2, 128, KD * 512])
        wqkv_d = din("wqkv", [2, 8, 128, 2 * 4 * 512])
        wo_d = din("wo", [2, 4, 128, KD * 512])
        rope_d = din("rope", [64, 2 * T])
        mask_d = din("mask", [128, 128], BF16)
    if stage == "L23":
        ct_all_d = din("ct_all", [128, 4 * 2 * T], BF16)
        kpe_all_d = din("kpe_all", [65, 2 * T], BF16)
    if FUSED:
        maskrow_d = din("maskrow", [1, 2 * T], BF16)
        halo_src = nc.dram_tensor("halo_src", [128, 32], BF16, kind="Internal").ap()
        halo_all = nc.dram_tensor("halo_all", [256, 32], BF16, addr_space="Local", kind="Internal").ap()
        lat_src = nc.dram_tensor("lat_src", [128, 5 * T], BF16, kind="Internal").ap()
        lat_all = nc.dram_tensor("lat_all", [256, 5 * T], BF16, addr_space="Local", kind="Internal").ap()

    es = contextlib.ExitStack()
    with es:
        def sb(name, shape, dt):
            return es.enter_context(nc.sbuf_tensor(name, list(shape), dt))

        def ps(name, shape, dt):
            return es.enter_context(nc.psum_tensor(name, list(shape), dt))

        TR = Tracker(nc, es)
        H32 = sb("H32", [128, NT, D], F32)
        HT = sb("HT", [128, KD, T], BF16)
        GB = sb("GB", [128, 2, D], F32)
        WS = sb("WS", [128, NSLOT * SLOT], BF16)
        ARN = 57344 // 2
        AR = sb("AR", [128, ARN], BF16)
        IDB = sb("IDB", [128, 128], BF16)
        SMALL = sb("SMALL", [128, 128], F32)
        ST = sb("ST", [128, 8, 24], F32)
        MV = sb("MV", [128, 8, 8], F32)
        SC32 = sb("SC32", [128, 2, 512], F32)
        PACC = ps("PACC", [128, 4, 512], F32)
        PTR = ps("PTR", [128, KD, 128], BF16)
        PFM = ps("PFM", [128, 2, 512], F32)

        def arena(off_bytes, nbytes, dt):
            v = AR[:, off_bytes // 2:(off_bytes + nbytes) // 2]
            return v if dt == BF16 else v.bitcast(F32)

        steps = []

        def init_fn(_):
            TR.dma("sp", IDB[:], ident_d[:, :], "c0", writes=[("IDB",)])
            TR.dma("sp", SMALL[:], small_d[:, :], "c0", writes=[("SMALL",)])
            for i in range(NT):
                TR.dma("sp", H32[:, i, :], x_d[i * 128:(i + 1) * 128, :], "x%d" % i,
                       writes=[("H32", i, b) for b in range(4)])
            HB = arena(49152, 8192, BF16).rearrange("p (a n) -> p a n", a=2)
            for i in range(NT):
                hb = HB[:, i % 2, :]
                TR.op("dve", lambda e, i=i, hb=hb: e.tensor_copy(out=hb, in_=H32[:, i, :]),
                      reads=[("H32", i, b) for b in range(4)], writes=[("HB", i % 2)])
                to_feat(i, hb, ("HB", i % 2))
        steps.append(Step(init_fn))

        def to_feat(i, hb, hbkey):
            def f(e):
                for k in range(KD):
                    ins = e.transpose(PTR[:, k, :], hb[:, k * 128:(k + 1) * 128], IDB[:])
                return ins
            TR.op("pe", f, reads=[hbkey, ("IDB",)], writes=[("PTR",)])
            TR.op("act", lambda e: e.copy(out=HT[:, :, i * 128:(i + 1) * 128], in_=PTR[:]),
                  reads=[("PTR",)], writes=[("HT", i)])

        def ln_phase(l, s, final_store=False):
            def load_fn(_):
                TR.dma("sp", GB[:, 0, :], lng_d[l, s].partition_broadcast(128), "gb", writes=[("GB",)])
                TR.dma("sp", GB[:, 1, :], lnb_d[l, s].partition_broadcast(128), "gb", writes=[("GB",)])
            steps.append(Step(load_fn))

            def fn(_):
                HB = arena(49152, 8192, BF16).rearrange("p (a n) -> p a n", a=2)
                hk = lambda i: [("H32", i, b) for b in range(4)]
                R = range(NT)
                for i in R:
                    def stats(e, i=i):
                        for c in range(4):
                            ins = e.bn_stats(out=ST[:, i, c * 6:(c + 1) * 6], in_=H32[:, i, c * 512:(c + 1) * 512])
                        return ins
                    TR.op("dve", stats, reads=hk(i), writes=[("ST", i)])
                for i in R:
                    TR.op("dve", lambda e, p=i: e.bn_aggr(out=MV[:, p, 0:2], in_=ST[:, p, :]),
                          reads=[("ST", i)], writes=[("MV", i, 0)])
                for i in R:
                    TR.op("dve", lambda e, p=i: e.tensor_scalar(out=MV[:, p, 2:3], in0=MV[:, p, 1:2], scalar1=LN_EPS,
                                                               scalar2=None, op0=ALU.add),
                          reads=[("MV", i, 0)], writes=[("MV", i, 1)])
                for i in R:
                    TR.op("act", lambda e, p=i: e.activation(out=MV[:, p, 3:4], in_=MV[:, p, 2:3], func=AF.Sqrt),
                          reads=[("MV", i, 1)], writes=[("MV", i, 2)])
                for i in R:
                    TR.op("dve", lambda e, p=i: e.reciprocal(out=MV[:, p, 4:5], in_=MV[:, p, 3:4]),
                          reads=[("MV", i, 2)], writes=[("MV", i, 3)])
                for i in R:
                    TR.op("dve", lambda e, p=i: e.scalar_tensor_tensor(out=MV[:, p, 5:6], in0=MV[:, p, 0:1], scalar=-1.0,
                                                                      in1=MV[:, p, 4:5], op0=ALU.mult, op1=ALU.mult),
                          reads=[("MV", i, 0), ("MV", i, 3)], writes=[("MV", i, 4)])
                for i in R:
                    TR.op("act", lambda e, i=i: e.activation(out=H32[:, i, :], in_=H32[:, i, :], func=AF.Identity,
                                                            bias=MV[:, i, 5:6], scale=MV[:, i, 4:5]),
                          reads=hk(i) + [("MV", i, 3), ("MV", i, 4)], writes=hk(i))
                for i in R:
                    TR.op("dve", lambda e, i=i: e.tensor_tensor(out=H32[:, i, :], in0=H32[:, i, :], in1=GB[:, 0, :], op=ALU.mult),
                          reads=hk(i) + [("GB",)], writes=hk(i))
                    TR.op("pool", lambda e, i=i: e.tensor_tensor(out=H32[:, i, :], in0=H32[:, i, :], in1=GB[:, 1, :], op=ALU.add),
                          reads=hk(i) + [("GB",)], writes=hk(i))
                    if final_store:
                        TR.dma("sp", out_d[i * 128:(i + 1) * 128, :], H32[:, i, :], "out", reads=hk(i))
                for i in R:
                    if not final_store:
                        p = i % 2
                        hb = HB[:, p, :]
                        TR.op("act", lambda e, i=i, hb=hb: e.copy(out=hb, in_=H32[:, i, :]),
                              reads=hk(i), writes=[("HB", p)])
                        to_feat(i, hb, ("HB", p))
            steps.append(Step(fn))

        def proj_block(wv, b, lhs_fn, lhs_keys_fn, first, wkeys, nk=16):
            for i in range(NT):
                bank = (b * NT + i) % 4
                def f(e, i=i, bank=bank):
                    for k in range(nk):
                        ins = e.matmul(PACC[:, bank, :], lhsT=lhs_fn(k, i), rhs=wv[:, k, :], start=(k == 0), stop=(k == nk - 1))
                    return ins
                TR.op("pe", f, reads=lhs_keys_fn(i) + wkeys, writes=[("PACC", bank)])
                dst = H32[:, i, b * 512:(b + 1) * 512]
                if first:
                    TR.op("dve", lambda e, dst=dst, bank=bank: e.scalar_tensor_tensor(
                        out=dst, in0=dst, scalar=ALPHA, in1=PACC[:, bank, :], op0=ALU.mult, op1=ALU.add),
                        reads=[("H32", i, b), ("PACC", bank)], writes=[("H32", i, b)])
                else:
                    TR.op("dve", lambda e, dst=dst, bank=bank: e.tensor_tensor(out=dst, in0=dst, in1=PACC[:, bank, :], op=ALU.add),
                          reads=[("H32", i, b), ("PACC", bank)], writes=[("H32", i, b)])

        def mlp_phase(li, l):
            AT = arena(0, 32768, BF16).rearrange("p (c t) -> p c t", c=16)
            RL = arena(32768, 4096, F32).rearrange("p (a n) -> p a n", a=2)
            for q in range(4):
                for c in range(16):
                    fch = q * 16 + c
                    def fn(wt, c=c):
                        wv, wkeys = wt
                        wv = wv.rearrange("p (k n) -> p k n", k=KD)
                        for th in range(2):
                            def f(e, th=th):
                                for k in range(KD):
                                    ins = e.matmul(PFM[:, th, :], lhsT=wv[:, k, :], rhs=HT[:, k, th * 512:(th + 1) * 512],
                                                   start=(k == 0), stop=(k == KD - 1))
                                return ins
                            TR.op("pe", f, reads=[("HT", 4 * th + j) for j in range(4)] + wkeys, writes=[("PFM", th)])
                            TR.op("act", lambda e, th=th: e.activation(out=RL[:, th, :], in_=PFM[:, th, :], func=AF.Relu),
                                  reads=[("PFM", th)], writes=[("RL", th)])
                            TR.op("dve", lambda e, th=th, c=c: e.tensor_tensor(out=AT[:, c, th * 512:(th + 1) * 512], in0=RL[:, th, :],
                                                                              in1=RL[:, th, :], op=ALU.mult),
                                  reads=[("RL", th)], writes=[("AT", c, th)])
                    steps.append(Step(fn, w1_d[li, fch], 1))
                for b in range(4):
                    def fn2(wt, q=q, b=b):
                        wv, wkeys = wt
                        wv = wv.rearrange("p (k n) -> p k n", k=16)
                        proj_block(wv, b, lambda k, i: AT[:, k, i * 128:(i + 1) * 128],
                                   lambda i: [("AT", k, i // 4) for k in range(16)], first=(q == 0), wkeys=wkeys)
                    steps.append(Step(fn2, w2_d[li, q, b], 4))

        def conv_phase(l, ci):
            ZT = arena(0, 32768, BF16).rearrange("p (c t) -> p c t", c=16)
            TC = arena(32768, 4128, F32)
            V = arena(36896, 4128, F32)
            Y = arena(41024, 4096, F32)
            HALO = sb("HALO%d" % l, [128, KD, 2], BF16)
            PH = PTR[:].rearrange("p k n -> p (k n)").bitcast(F32)

            def load_halo(_):
                if FUSED and l == 1:
                    HL = sb("HL", [128, 32], BF16)
                    TR.op("pool", lambda e: e.collective_compute("AllGather", ALU.bypass, replica_groups=PAIRS, ins=[halo_src], outs=[halo_all]),
                          reads=[("HALOSRC",)], writes=[("HALOALL",)])
                    TR.dma("sp", HL[:], halo_all[0:128, :], "halo", reads=[("HALOALL",)], writes=[("HL",)])
                    TR.op("dve", lambda e: e.tensor_scalar(out=HALO[:].rearrange("p k t -> p (k t)"), in0=HL[:], scalar1=SMALL[:, 110:111],
                                                           scalar2=None, op0=ALU.mult),
                          reads=[("HL",), ("SMALL",)], writes=[("HALO",)])
                else:
                    TR.dma("pool", HALO[:].rearrange("p k t -> p (k t)"), halo_d[:, :], "halo", writes=[("HALO",)])
            steps.append(Step(load_halo))
            cw = lambda c, j: SMALL[:, l * 48 + c * 3 + j: l * 48 + c * 3 + j + 1]
            for c in range(16):
                def fn_part(wt, c=c, part=0):
                    wv, wkeys = wt
                    wv = wv.rearrange("p (k n) -> p k n", k=KD)
                    banks = {1: (PACC, 0), 2: (PACC, 2), 0: (PFM, 0)}[part]
                    pt, b0 = banks
                    pkey = "PACC" if pt is PACC else "PFM"
                    if part in (1, 2):
                        col = 0 if part == 1 else 2
                        def fh(e):
                            for k in range(KD):
                                ins = e.matmul(PH[:, col:col + 2], lhsT=wv[:, k, :], rhs=HALO[:, k, :], start=(k == 0), stop=(k == KD - 1))
                            return ins
                        TR.op("pe", fh, reads=[("HALO",)] + wkeys, writes=[("PTR",)])
                    for th in range(2):
                        def f(e, th=th):
                            for k in range(KD):
                                ins = e.matmul(pt[:, b0 + th, :], lhsT=wv[:, k, :], rhs=HT[:, k, th * 512:(th + 1) * 512],
                                               start=(k == 0), stop=(k == KD - 1))
                            return ins
                        TR.op("pe", f, reads=[("HT", 4 * th + j) for j in range(4)] + wkeys, writes=[(pkey, b0 + th)])
                    if part == 1:
                        for th in range(2):
                            TR.op("act", lambda e, th=th: e.copy(out=TC[:, th * 512:(th + 1) * 512], in_=PACC[:, th, :]),
                                  reads=[("PACC", th)], writes=[("TC", th)])
                        TR.op("act", lambda e: e.copy(out=TC[:, 1024:1026], in_=PH[:, 0:2]), reads=[("PTR",)], writes=[("TC", 2)])
                    elif part == 2:
                        for th in range(2):
                            TR.op("dve", lambda e, th=th: e.tensor_tensor(out=V[:, 2 + th * 512: 2 + (th + 1) * 512], in0=TC[:, th * 512:(th + 1) * 512],
                                                                         in1=PACC[:, 2 + th, :], op=ALU.mult),
                                  reads=[("TC", th), ("PACC", 2 + th)], writes=[("V", th)])
                        TR.op("dve", lambda e: e.tensor_tensor(out=V[:, 0:2], in0=TC[:, 1024:1026], in1=PH[:, 2:4], op=ALU.mult),
                              reads=[("TC", 2), ("PTR",)], writes=[("V", 2)])
                        vk = [("V", 0), ("V", 1), ("V", 2)]
                        TR.op("dve", lambda e, c=c: e.tensor_scalar(out=Y[:, 0:1024], in0=V[:, 2:1026], scalar1=cw(c, 2), scalar2=None, op0=ALU.mult),
                              reads=vk + [("SMALL",)], writes=[("Y",)])
                        TR.op("dve", lambda e, c=c: e.scalar_tensor_tensor(out=Y[:, 0:1024], in0=V[:, 1:1025], scalar=cw(c, 1), in1=Y[:, 0:1024],
                                                                          op0=ALU.mult, op1=ALU.add),
                              reads=vk + [("Y",), ("SMALL",)], writes=[("Y",)])
                        TR.op("dve", lambda e, c=c: e.scalar_tensor_tensor(out=Y[:, 0:1024], in0=V[:, 0:1024], scalar=cw(c, 0), in1=Y[:, 0:1024],
                                                                          op0=ALU.mult, op1=ALU.add),
                              reads=vk + [("Y",), ("SMALL",)], writes=[("Y",)])
                    else:
                        for th in range(2):
                            TR.op("dve", lambda e, th=th, c=c: e.tensor_tensor(out=ZT[:, c, th * 512:(th + 1) * 512], in0=Y[:, th * 512:(th + 1) * 512],
                                                                              in1=PFM[:, th, :], op=ALU.mult),
                                  reads=[("Y",), ("PFM", th)], writes=[("ZT", c, th)])
                for part in (1, 2, 0):
                    steps.append(Step(lambda wt, c=c, part=part, fn_part=fn_part: fn_part(wt, c, part), win_d[ci, c, part], 1))
            for b in range(4):
                def fn2(wt, b=b):
                    wv, wkeys = wt
                    wv = wv.rearrange("p (k n) -> p k n", k=16)
                    proj_block(wv, b, lambda k, i: ZT[:, k, i * 128:(i + 1) * 128],
                               lambda i: [("ZT", k, i // 4) for k in range(16)], first=True, wkeys=wkeys)
                steps.append(Step(fn2, wout_d[ci, b], 4))

        def rms_to_feat(i, bank, gcol0, dstT, dkey):
            p = i % 2
            SQ = SC32[:, p, :]
            NB = arena(53248 + p * 1024, 1024, BF16)
            TR.op("dve", lambda e: e.memset(MV[:, p, 6:7], 0.0), writes=[("MV", p, 6)])
            TR.op("act", lambda e: e.activation(out=SQ, in_=PACC[:, bank, :], func=AF.Square, accum_out=MV[:, p, 6:7]),
                  reads=[("PACC", bank), ("MV", p, 6)], writes=[("SQ", p), ("MV", p, 6)])
            TR.op("dve", lambda e: e.tensor_scalar(out=MV[:, p, 7:8], in0=MV[:, p, 6:7], scalar1=1.0 / 512, scalar2=RMS_EPS,
                                                   op0=ALU.mult, op1=ALU.add),
                  reads=[("MV", p, 6)], writes=[("MV", p, 7)])
            TR.op("act", lambda e: e.activation(out=MV[:, p, 6:7], in_=MV[:, p, 7:8], func=AF.Sqrt),
                  reads=[("MV", p, 7)], writes=[("MV", p, 6)])
            TR.op("dve", lambda e: e.reciprocal(out=MV[:, p, 7:8], in_=MV[:, p, 6:7]),
                  reads=[("MV", p, 6)], writes=[("MV", p, 7)])
            TR.op("dve", lambda e: e.tensor_scalar(out=NB, in0=PACC[:, bank, :], scalar1=MV[:, p, 7:8], scalar2=None, op0=ALU.mult),
                  reads=[("PACC", bank), ("MV", p, 7)], writes=[("NB", p)])
            def f(e):
                for kc in range(4):
                    ins = e.transpose(PTR[:, kc, :], NB[:, kc * 128:(kc + 1) * 128], IDB[:])
                return ins
            TR.op("pe", f, reads=[("NB", p), ("IDB",)], writes=[("PTR",)])
            for kc in range(4):
                TR.op("dve", lambda e, kc=kc: e.tensor_scalar(out=dstT[:, kc, i * 128:(i + 1) * 128], in0=PTR[:, kc, :],
                                                             scalar1=SMALL[:, gcol0 + kc: gcol0 + kc + 1], scalar2=None, op0=ALU.mult),
                      reads=[("PTR",), ("SMALL",)], writes=[(dkey, i)])

        def load_rope():
            def fn(_):
                TR.dma("sp", GB[0:64, 0, :], rope_d[:, :], "gb", writes=[("GB",)])
            steps.append(Step(fn))

        ROPE = GB[0:64, 0, :]
        RT = GB[0:64, 1, :]

        def rope_combine(pa, pb, keys_in, th, dst, dkeys):
            TR.op("dve", lambda e: e.tensor_tensor(out=RT[:, 0:512], in0=pa, in1=ROPE[:, th * 512:(th + 1) * 512], op=ALU.mult),
                  reads=keys_in + [("GB",)], writes=[("RT", 0)])
            TR.op("dve", lambda e: e.tensor_tensor(out=RT[:, 512:1024], in0=pb, in1=ROPE[:, T + th * 512:T + (th + 1) * 512], op=ALU.mult),
                  reads=keys_in + [("GB",)], writes=[("RT", 1)])
            TR.op("dve", lambda e: e.tensor_tensor(out=dst, in0=RT[:, 0:512], in1=RT[:, 512:1024], op=ALU.add),
                  reads=[("RT", 0), ("RT", 1), ("GB",)], writes=dkeys)

        def kv_latent_phase():
            CTO = arena(0, 8192, BF16).rearrange("p (k t) -> p k t", k=4)
            KPO = arena(8192, 2048, BF16)
            load_rope()

            def fn(wt):
                wv, wkeys = wt
                wv = wv.rearrange("p (k n) -> p k n", k=KD)
                for i in range(NT):
                    bank = i % 4
                    def f(e, i=i, bank=bank):
                        for k in range(KD):
                            ins = e.matmul(PACC[:, bank, :], lhsT=HT[:, k, i * 128:(i + 1) * 128], rhs=wv[:, k, :], start=(k == 0), stop=(k == KD - 1))
                        return ins
                    TR.op("pe", f, reads=[("HT", i)] + wkeys, writes=[("PACC", bank)])
                    rms_to_feat(i, bank, 96, CTO, "CTO")
            steps.append(Step(fn, wdkv_lat_d, 4))

            def fn2(wt):
                wv, wkeys = wt
                wv = wv.rearrange("p (k n) -> p k n", k=KD)
                for th in range(2):
                    for sw in range(2):
                        def f(e, th=th, sw=sw):
                            for k in range(KD):
                                ins = e.matmul(PFM[0:64, sw, :], lhsT=wv[:, k, sw * 64:(sw + 1) * 64], rhs=HT[:, k, th * 512:(th + 1) * 512],
                                               start=(k == 0), stop=(k == KD - 1))
                            return ins
                        TR.op("pe", f, reads=[("HT", 4 * th + j) for j in range(4)] + wkeys, writes=[("PFM", sw)])
                    rope_combine(PFM[0:64, 0, :], PFM[0:64, 1, :], [("PFM", 0), ("PFM", 1)], th,
                                 KPO[0:64, th * 512:(th + 1) * 512], [("KPO", th)])
                if FUSED:
                    TR.dma("sp", lat_src[:, 0:4 * T], CTO[:].rearrange("p k t -> p (k t)"), "lat", reads=[("CTO", i) for i in range(NT)],
                           writes=[("LATSRC", 0)])
                    TR.dma("sp", lat_src[0:64, 4 * T:5 * T], KPO[0:64, :], "lat", reads=[("KPO", 0), ("KPO", 1)], writes=[("LATSRC", 1)])
                    TR.op("pool", lambda e: e.collective_compute("AllGather", ALU.bypass, replica_groups=PAIRS, ins=[lat_src], outs=[lat_all]),
                          reads=[("LATSRC", 0), ("LATSRC", 1)], writes=[("LATALL",)])
                else:
                    TR.dma("sp", ct_own_d[:, :], CTO[:].rearrange("p k t -> p (k t)"), "out", reads=[("CTO", i) for i in range(NT)])
                    TR.dma("sp", kpe_own_d[:, :], KPO[0:64, :], "out", reads=[("KPO", 0), ("KPO", 1)])
            steps.append(Step(fn2, wdkv_pe_d, 1))

        def mla_phase(j, l):
            CT = arena(0, 16384, BF16).rearrange("p (k t) -> p k t", k=4)
            KPE = arena(16384, 4096, BF16)
            QLT = arena(20480, 8192, BF16).rearrange("p (k t) -> p k t", k=4)
            QN = arena(28672, 2048, BF16)
            QP = arena(30720, 2048, BF16)
            KN = arena(32768, 4096, BF16)
            V2 = arena(36864, 8192, BF16).rearrange("p (t n) -> p t n", t=16)
            PBS = [arena(45056, 4096, BF16), GB[:, 1, 1024:2048].bitcast(BF16)]
            PTS = [arena(49152, 4096, BF16).rearrange("p (t n) -> p t n", t=16),
                   SC32[:].rearrange("p a n -> p (a n)").bitcast(BF16).rearrange("p (t n) -> p t n", t=16)]
            OBS = [arena(55296, 256, BF16), arena(55808, 256, BF16)]
            MASK = arena(55552, 256, BF16)
            PO = PFM[:, 0, 256:512].bitcast(BF16)
            load_rope()

            def load_kv(_):
                if FUSED:
                    TR.dma("sp", CT[:, :, 0:T], lat_all[0:128, 0:4 * T].rearrange("p (k t) -> p k t", k=4), "kv",
                           reads=[("LATALL",)], writes=[("CT",)], extra=TR.barrier_keys())
                    TR.dma("sp", CT[:, :, T:2 * T], lat_src[:, 0:4 * T].rearrange("p (k t) -> p k t", k=4), "kv",
                           reads=[("LATSRC", 0), ("LATALL",)], writes=[("CT",)])
                    TR.dma("sp", KPE[0:64, 0:T], lat_all[0:64, 4 * T:5 * T], "kv", reads=[("LATALL",)], writes=[("KPE",)])
                    TR.dma("sp", KPE[0:64, T:2 * T], lat_src[0:64, 4 * T:5 * T], "kv", reads=[("LATSRC", 1), ("LATALL",)], writes=[("KPE",)])
                    TR.dma("sp", KPE[64:65, :], maskrow_d[:, :], "kv", writes=[("KPE",)])
                else:
                    TR.dma("sp", CT[:].rearrange("p k t -> p (k t)"), ct_all_d[:, :], "kv", writes=[("CT",)], extra=TR.barrier_keys())
                    TR.dma("sp", KPE[0:65, :], kpe_all_d[:, :], "kv", writes=[("KPE",)])
                TR.dma("sp", MASK, mask_d[:, :], "kv", writes=[("MASK",)])
                TR.op("dve", lambda e: e.memset(QP[64:65, :], 1.0), writes=[("QP1",)])
            steps.append(Step(load_kv))

            def fn_dq(wt):
                wv, wkeys = wt
                wv = wv.rearrange("p (k n) -> p k n", k=KD)
                for i in range(NT):
                    bank = i % 4
                    def f(e, i=i, bank=bank):
                        for k in range(KD):
                            ins = e.matmul(PACC[:, bank, :], lhsT=HT[:, k, i * 128:(i + 1) * 128], rhs=wv[:, k, :], start=(k == 0), stop=(k == KD - 1))
                        return ins
                    TR.op("pe", f, reads=[("HT", i)] + wkeys, writes=[("PACC", bank)])
                    rms_to_feat(i, bank, 100 + 4 * j, QLT, "QLT")
            steps.append(Step(fn_dq, wdq_d[j], 4))

            for g in range(8):
                state = {}

                def fn_kv(wt, g=g, state=state):
                    wall, wkeys = wt
                    state["wukv"] = (wall[:, 0:2048].rearrange("p (k n) -> p k n", k=4), wkeys)
                    wv = state["wukv"][0]
                    for t in range(16):
                        def f(e, t=t):
                            for kc in range(4):
                                ins = e.matmul(PFM[:, t % 2, 0:256].rearrange("p (h c) -> p h c", h=2), lhsT=CT[:, kc, t * 128:(t + 1) * 128],
                                               rhs=wv[:, kc, :].rearrange("p (h c) -> p h c", h=2)[:, :, 128:256], start=(kc == 0), stop=(kc == 3))
                            return ins
                        TR.op("pe", f, reads=[("CT",)] + wkeys, writes=[("PFM", t % 2)])
                        if t % 2 == 0:
                            TR.op("act", lambda e, t=t: e.copy(out=V2[:, t, :], in_=PFM[:, 0, 0:256]), reads=[("PFM", 0)], writes=[("V2", t)])
                        else:
                            TR.op("dve", lambda e, t=t: e.tensor_copy(out=V2[:, t, :], in_=PFM[:, 1, 0:256]), reads=[("PFM", 1)], writes=[("V2", t)])

                def fn_q(wt, g=g, state=state):
                    wall, wqkeys = wt
                    wq = wall[:, 2048:4096].rearrange("p (k n) -> p k n", k=4)
                    wkv, wkvkeys = state["wukv"]
                    for hh in range(2):
                        h = 2 * g + hh
                        for kb in range(4):
                            def f(e, kb=kb):
                                for kc in range(4):
                                    ins = e.matmul(PACC[:, kb, :], lhsT=wkv[:, kc, hh * 256: hh * 256 + 128], rhs=CT[:, kc, kb * 512:(kb + 1) * 512],
                                                   start=(kc == 0), stop=(kc == 3))
                                return ins
                            TR.op("pe", f, reads=[("CT",)] + wkvkeys, writes=[("PACC", kb)])
                            TR.op("act", lambda e, kb=kb: e.copy(out=KN[:, kb * 512:(kb + 1) * 512], in_=PACC[:, kb, :]),
                                  reads=[("PACC", kb)], writes=[("KN", kb)])
                        for th in range(2):
                            def f(e, th=th):
                                for kc in range(4):
                                    ins = e.matmul(PACC[:, th, :], lhsT=wq[:, kc, hh * 256: hh * 256 + 128], rhs=QLT[:, kc, th * 512:(th + 1) * 512],
                                                   start=(kc == 0), stop=(kc == 3))
                                return ins
                            TR.op("pe", f, reads=[("QLT", 4 * th + jj) for jj in range(4)] + wqkeys, writes=[("PACC", th)])
                            TR.op("act", lambda e, th=th: e.copy(out=QN[:, th * 512:(th + 1) * 512], in_=PACC[:, th, :]),
                                  reads=[("PACC", th)], writes=[("QN", th)])
                            for sw in range(2):
                                def f2(e, th=th, sw=sw):
                                    for kc in range(4):
                                        ins = e.matmul(PFM[0:64, sw, :], lhsT=wq[:, kc, hh * 256 + 128 + sw * 64: hh * 256 + 192 + sw * 64],
                                                       rhs=QLT[:, kc, th * 512:(th + 1) * 512], start=(kc == 0), stop=(kc == 3))
                                    return ins
                                TR.op("pe", f2, reads=[("QLT", 4 * th + jj) for jj in range(4)] + wqkeys, writes=[("PFM", sw)])
                            rope_combine(PFM[0:64, 0, :], PFM[0:64, 1, :], [("PFM", 0), ("PFM", 1)], th,
                                         QP[0:64, th * 512:(th + 1) * 512], [("QP", th)])
                        SFL = PACC[:].rearrange("p b n -> p (b n)")

                        def nkof(qi):
                            nk = 8 + qi + 1
                            return nk, (nk * 128 + 511) // 512

                        def S_op(qi):
                            nk, nkb = nkof(qi)
                            def fs(e):
                                for kb in range(nkb):
                                    w = min(512, nk * 128 - kb * 512)
                                    e.matmul(PACC[:, kb, 0:w], lhsT=QN[:, qi * 128:(qi + 1) * 128], rhs=KN[:, kb * 512: kb * 512 + w],
                                             start=True, stop=False, skip_group_check=True)
                                for kb in range(nkb):
                                    w = min(512, nk * 128 - kb * 512)
                                    diag = (8 + qi) // 4 == kb
                                    ins = e.matmul(PACC[:, kb, 0:w], lhsT=QP[0:65, qi * 128:(qi + 1) * 128], rhs=KPE[0:65, kb * 512: kb * 512 + w],
                                                   start=False, stop=(not diag), skip_group_check=True)
                                kb = (8 + qi) // 4
                                o = ((8 + qi) % 4) * 128
                                ins = e.matmul(PACC[:, kb, o:o + 128], lhsT=IDB[:], rhs=MASK, start=False, stop=True, skip_group_check=True)
                                return ins
                            TR.op("pe", fs, reads=[("QN", qi // 4), ("QP", qi // 4), ("QP1",), ("KPE",), ("MASK",), ("IDB",)] + [("KN", kb) for kb in range(nkb)],
                                  writes=[("PACC", kb) for kb in range(nkb)])

                        def max_op(qi):
                            nk, nkb = nkof(qi)
                            p = qi % 2
                            pk = [("PACC", kb) for kb in range(nkb)]
                            TR.op("dve", lambda e: e.reduce_max(out=MV[:, p, 0:1], in_=SFL[:, 0:nk * 128], axis=AX.X),
                                  reads=pk, writes=[("AM", p, 0)])
                            TR.op("dve", lambda e: e.tensor_scalar(out=MV[:, p, 1:2], in0=MV[:, p, 0:1], scalar1=-SCALE, scalar2=None, op0=ALU.mult),
                                  reads=[("AM", p, 0)], writes=[("AM", p, 1)])
                            TR.op("dve", lambda e: e.memset(MV[:, p, 2:3], 0.0), writes=[("AM", p, 2)])

                        def exp_op(qi):
                            nk, nkb = nkof(qi)
                            p = qi % 2
                            pk = [("PACC", kb) for kb in range(nkb)]
                            TR.op("act", lambda e: e.activation(out=PBS[p][:, 0:nk * 128], in_=SFL[:, 0:nk * 128], func=AF.Exp,
                                                                bias=MV[:, p, 1:2], scale=SCALE, accum_out=MV[:, p, 2:3]),
                                  reads=pk + [("AM", p, 1), ("AM", p, 2)], writes=[("PB", p), ("AM", p, 2)])

                        def recip_op(qi):
                            p = qi % 2
                            TR.op("dve", lambda e: e.reciprocal(out=MV[:, p, 3:4], in_=MV[:, p, 2:3]), reads=[("AM", p, 2)], writes=[("AM", p, 3)])

                        def T_op(qi):
                            nk, nkb = nkof(qi)
                            p = qi % 2
                            def ft(e):
                                for t in range(nk):
                                    ins = e.transpose(PTR[:, t, :], PBS[p][:, t * 128:(t + 1) * 128], IDB[:])
                                return ins
                            TR.op("pe", ft, reads=[("PB", p), ("IDB",), ("GB",)], writes=[("PTR",)])

                        def evacPT_op(qi):
                            nk, nkb = nkof(qi)
                            p = qi % 2
                            TR.op("act", lambda e: e.copy(out=PTS[p][:, 0:nk, :], in_=PTR[:, 0:nk, :]), reads=[("PTR",)], writes=[("PT", p)])

                        def PV_op(qi):
                            nk, nkb = nkof(qi)
                            p = qi % 2
                            def fo(e):
                                for t in range(nk):
                                    ins = e.matmul(PFM[:, 0, 0:128], lhsT=PTS[p][:, t, :], rhs=V2[:, t, hh * 128:(hh + 1) * 128], start=(t == 0), stop=(t == nk - 1))
                                return ins
                            TR.op("pe", fo, reads=[("PT", p)] + [("V2", t) for t in range(nk)], writes=[("PFM", 0)])

                        def Oscale_op(qi):
                            p = qi % 2
                            TR.op("dve", lambda e: e.tensor_scalar(out=OBS[p], in0=PFM[:, 0, 0:128], scalar1=MV[:, p, 3:4], scalar2=None, op0=ALU.mult),
                                  reads=[("PFM", 0), ("AM", p, 3)], writes=[("OB", p)])

                        def OT_op(qi):
                            p = qi % 2
                            TR.op("pe", lambda e: e.transpose(PO[:, 0:128], OBS[p], IDB[:]), reads=[("OB", p), ("IDB",)], writes=[("PFM", 0)])

                        def evacOT_op(qi):
                            TR.op("act", lambda e: e.copy(out=HT[:, h, qi * 128:(qi + 1) * 128], in_=PO[:, 0:128]),
                                  reads=[("PFM", 0)], writes=[("HT", qi)])

                        def filler(n):
                            if n <= 0:
                                return
                            def ff(e):
                                for _ in range(n):
                                    ins = e.matmul(PFM[:, 1, :], lhsT=IDB[:], rhs=KN[:, 0:512], start=True, stop=True, skip_group_check=True)
                                return ins
                            TR.op("pe", ff, reads=[("IDB",)], writes=[("PFM", 1)])

                        S_op(0); max_op(0); exp_op(0); recip_op(0)
                        for qi in range(NT):
                            if qi + 1 < NT:
                                S_op(qi + 1)
                            T_op(qi)
                            filler(FILL_A)
                            if qi + 1 < NT:
                                max_op(qi + 1)
                            evacPT_op(qi)
                            if qi + 1 < NT:
                                exp_op(qi + 1)
                            PV_op(qi)
                            filler(FILL_B)
                            Oscale_op(qi)
                            OT_op(qi)
                            evacOT_op(qi)
                            if qi + 1 < NT:
                                recip_op(qi + 1)
                                filler(FILL_C + (8 + qi) // 8)
                steps.append(Step(lambda wt, fn_kv=fn_kv, fn_q=fn_q: (fn_kv(wt), fn_q(wt)), wqkv_d[j, g], 2))

            for b in range(4):
                def fn_o(wt, b=b):
                    wv, wkeys = wt
                    wv = wv.rearrange("p (k n) -> p k n", k=16)
                    proj_block(wv, b, lambda k, i: HT[:, k, i * 128:(i + 1) * 128], lambda i: [("HT", i)], first=True, wkeys=wkeys)
                steps.append(Step(fn_o, wo_d[j, b], 4))

        for li, l in enumerate(layers):
            last = (li == len(layers) - 1)
            if l < 2:
                conv_phase(l, conv_layers.index(l))
            else:
                mla_phase(l - 2, l)
            ln_phase(l, 0)
            mlp_phase(li, l)
            ln_phase(l, 1, final_store=(last and stage != "L1"))
            if FUSED and l == 0:
                def halo_export(_):
                    HS = sb("HS", [128, 16, 2], BF16)
                    TR.op("act", lambda e: e.copy(out=HS[:], in_=HT[:, :, T - 2:T]), reads=[("HT", NT - 1)], writes=[("HS",)])
                    TR.dma("sp", halo_src[:, :], HS[:].rearrange("p k t -> p (k t)"), "halo", reads=[("HS",)], writes=[("HALOSRC",)])
                steps.append(Step(halo_export))
            if FUSED and l == 1:
                kv_latent_phase()
            if stage == "L1":
                kv_latent_phase()

                def store_h(_):
                    for i in range(NT):
                        TR.dma("sp", out_d[i * 128:(i + 1) * 128, :], H32[:, i, :], "out", reads=[("H32", i, b) for b in range(4)])
                steps.append(Step(store_h))

            _LAYER_ENDS.append(len(steps))
        wsteps = [si for si, s in enumerate(steps) if s.w is not None]
        slots_of = {}
        pos = 0
        for si in wsteps:
            n = steps[si].nslots
            if pos + n > NSLOT:
                pos = 0
            slots_of[si] = list(range(pos, pos + n))
            pos += n
        owner = [None] * NSLOT
        nxt = 0
        wtile = {}

        def issue(si):
            s = steps[si]
            sl = slots_of[si]
            width = s.w.shape[-1]
            view = WS[:, sl[0] * SLOT: sl[0] * SLOT + width]
            keys = [("WS", k) for k in sl]
            TR.dma("pool", view, s.w, "ws%d" % sl[0], writes=keys)
            wtile[si] = (view, keys)
            for k in sl:
                owner[k] = si

        if max_steps is not None:
            steps = steps[:max_steps]
            wsteps = [si for si in wsteps if si < max_steps]

            def dbg_store(_):
                for i in range(NT):
                    TR.dma("sp", out_d[i * 128:(i + 1) * 128, :], H32[:, i, :], "out", reads=[("H32", i, b) for b in range(4)],
                           extra=TR.barrier_keys())
            steps.append(Step(dbg_store))
        for si, s in enumerate(steps):
            while nxt < len(wsteps):
                cand = wsteps[nxt]
                if all(owner[k] is None or owner[k] < si for k in slots_of[cand]) and cand - si < 12:
                    issue(cand)
                    nxt += 1
                else:
                    break
            if s.w is not None:
                assert si in wtile, "weight not issued before use"
                s.fn(wtile[si])
            else:
                s.fn(None)
        if "out" in TR.dsem:
            ent = TR.dsem["out"]
            nc.sync.wait_ge(ent[0], ent[1])
        stuck, prog = TR.check_deadlock()
        if stuck:
            raise RuntimeError("sync deadlock: %r %r" % (stuck, prog))
        nc._tr_stats = prog if False else None
    return nc


_BF = ml_dtypes.bfloat16


def _rope_tables(pos0):
    inv = (1.0 / (np.float32(10000.0) ** (np.arange(0, 64, 2, dtype=np.float32) / np.float32(64)))).astype(np.float32)
    ang = (np.arange(pos0, pos0 + T, dtype=np.float32)[:, None] * inv[None, :]).astype(np.float32)
    cos, sin = np.cos(ang).astype(np.float32), np.sin(ang).astype(np.float32)
    cos2 = np.concatenate([cos, cos], axis=1).T
    sin2 = np.concatenate([-sin, sin], axis=1).T
    return np.ascontiguousarray(np.concatenate([cos2, sin2], axis=1).astype(np.float32))


def _chunk_kn(w, n):
    K, N = w.shape
    return np.ascontiguousarray(w.reshape(K // 128, 128, N // n, n).transpose(2, 1, 0, 3)).reshape(N // n, 128, (K // 128) * n)


_PROGS = {}


def _prog(stage):
    if stage not in _PROGS:
        _PROGS[stage] = build_program(stage)
    return _PROGS[stage]


def kernel(x, ln_g, ln_b, conv_w_in, conv_w, conv_w_out, kv_w_dkv, kv_norm_g, kv_w_ukv,
           mla_w_dq, mla_q_norm_g, mla_w_uq, mla_w_o, mlp_w1, mlp_w2):
    f = lambda a: np.asarray(a, dtype=np.float32)
    x, ln_g, ln_b = f(x), f(ln_g), f(ln_b)
    conv_w_in, conv_w, conv_w_out = f(conv_w_in), f(conv_w), f(conv_w_out)
    kv_w_dkv, kv_norm_g, kv_w_ukv = f(kv_w_dkv), f(kv_norm_g), f(kv_w_ukv)
    mla_w_dq, mla_q_norm_g, mla_w_uq, mla_w_o = f(mla_w_dq), f(mla_q_norm_g), f(mla_w_uq), f(mla_w_o)
    mlp_w1, mlp_w2 = f(mlp_w1), f(mlp_w2)
    n = 8
    cores = list(range(n))
    ident = np.eye(128, dtype=np.float32).astype(_BF)
    qq, kk = np.meshgrid(np.arange(128), np.arange(128), indexing="ij")
    mask = np.where(kk <= qq, 0.0, NEG).astype(np.float32).astype(_BF)
    small = np.zeros((128, 128), np.float32)
    for l in range(2):
        small[:, l * 48:(l + 1) * 48] = conv_w[l].reshape(3, 16, 128).transpose(2, 1, 0).reshape(128, 48)
    small[:, 96:100] = kv_norm_g.reshape(4, 128).T
    for j in range(2):
        small[:, 100 + 4 * j:104 + 4 * j] = mla_q_norm_g[j].reshape(4, 128).T
    w1r = np.stack([_chunk_kn(mlp_w1[l], 128) for l in range(4)])
    w2r = np.stack([np.ascontiguousarray(mlp_w2[l].reshape(4, 16, 128, 4, 512).transpose(0, 3, 2, 1, 4)).reshape(4, 4, 128, 16 * 512)
                    for l in range(4)])
    win = np.stack([np.ascontiguousarray(conv_w_in[l].reshape(16, 128, 3, 16, 128).transpose(3, 2, 1, 0, 4)).reshape(16, 3, 128, 16 * 128)
                    for l in range(2)])
    wout = np.stack([_chunk_kn(conv_w_out[l], 512) for l in range(2)])
    wdkv_lat = _chunk_kn(kv_w_dkv[:, :512], 512)[0]
    pe = kv_w_dkv[:, 512:576]
    wdkv_pe = _chunk_kn(np.concatenate([pe, pe[:, 32:], pe[:, :32]], axis=1), 128)[0]
    wdq = np.stack([_chunk_kn(mla_w_dq[j], 512)[0] for j in range(2)])
    wuq = []
    for j in range(2):
        w = mla_w_uq[j].reshape(512, 16, 192)
        wa = np.concatenate([w, w[:, :, 160:192], w[:, :, 128:160]], axis=2)
        wuq.append(_chunk_kn(wa.reshape(512, 4096), 512))
    wuq = np.stack(wuq)
    wukv = _chunk_kn(kv_w_ukv, 512)
    wqkv = np.ascontiguousarray(np.concatenate([np.broadcast_to(wukv[None], wuq.shape), wuq], axis=-1))
    wo = np.stack([_chunk_kn(mla_w_o[j], 512) for j in range(2)])
    ropes = [_rope_tables(0), _rope_tables(T)]

    def halo_t(rows):
        return np.ascontiguousarray(rows.reshape(2, 16, 128).transpose(2, 1, 0)).reshape(128, 32)

    maps = []
    for c in cores:
        b, hf = c // 2, c % 2
        sm = small.copy()
        sm[:, 110] = float(hf)
        halo = np.zeros((2, D), np.float32) if hf == 0 else x[b, T - 2:T]
        mrow = np.zeros((1, 2 * T), np.float32)
        if hf == 0:
            mrow[0, :T] = NEG
        maps.append({"x": np.ascontiguousarray(x[b, hf * T:(hf + 1) * T]), "ident": ident, "ln_g": ln_g, "ln_b": ln_b,
                     "smallp": sm, "w1": w1r, "w2": w2r, "w_in": win, "w_out": wout, "halo_t": halo_t(halo),
                     "wdkv_lat": wdkv_lat, "wdkv_pe": wdkv_pe, "wdq": wdq, "wqkv": wqkv, "wo": wo,
                     "rope": ropes[hf], "mask": mask, "maskrow": mrow.astype(_BF)})
    res = run_bass_kernel_spmd(_prog("ALL"), maps, core_ids=cores).results
    out = np.empty((4, 2048, D), np.float32)
    for c in cores:
        out[c // 2, (c % 2) * T:(c % 2 + 1) * T] = np.asarray(res[c]["out"], dtype=np.float32)
    return out
```
